# Optimizing a Trainium2 kernel written in Bass

```python
import math
import jax
import jax.numpy as jnp
from jax import lax
import numpy as np

D_MODEL = 1024
BATCH = 32
SEQ = 256
DEPTH = 4
DEC_BATCH = 2
DEC_SEQ = 2048
PAST_LEN = 512

GRID_W = 64
N_MIXERS = 3
D_INNER = 2 * D_MODEL
N_S5_LAYERS = (DEPTH + 2) // 3
N_POOL_LAYERS = (DEPTH + 1) // 3
N_MLA_LAYERS = DEPTH // 3
S5_GROUP = 16
S5_GROUPS = D_INNER // S5_GROUP
S5_STATE = 64
POOL_WINDOWS = (2, 4, 8, 16)
POOL_GROUPS = len(POOL_WINDOWS)
POOL_GROUP_W = D_INNER // POOL_GROUPS
MLA_HEADS = 16
MLA_NOPE = 128
MLA_ROPE = 64
MLA_V = 128
MLA_Q_RANK = 256
MLA_KV_RANK = 128
ROPE_THETA = 10000.0
ATTN_BLOCK = 128
NORM_EPS = 1e-6

kernel_name = 'hybrid_s5_pool_mla_diffusion_step'


def _f32(t):
    return t.astype(jnp.float32)


def _rmsnorm(x, g):
    xf = _f32(x)
    y = xf * lax.rsqrt(jnp.mean(xf * xf, axis=-1, keepdims=True) + NORM_EPS)
    return (y * _f32(g)).astype(x.dtype)


def _ada(cond, w, b):
    m = jax.nn.silu(cond) @ w + b
    return jnp.split(m, 3, axis=-1)


def _lin_rec(left, right):
    a_l, b_l = left
    a_r, b_r = right
    return a_r * a_l, a_r * b_l + b_r


def _s5_mixer(h, h0_re, h0_im, w_in, lam_re, lam_im, log_step, b_re, b_im, c_re, c_im,
              d_skip, glu_w, glu_b, w_out):
    bsz, n_tok, _ = h.shape
    u, z = jnp.split(h @ w_in, 2, axis=-1)
    uf = _f32(u)
    ug = uf.reshape(bsz, n_tok, S5_GROUPS, S5_GROUP).astype(jnp.complex64)
    y = _f32(d_skip) * uf
    fin_re, fin_im = [], []
    for dirn in range(2):
        lam = lax.complex(_f32(lam_re[dirn]), _f32(lam_im[dirn]))
        step = jnp.exp(_f32(log_step[dirn]))[:, None]
        lam_bar = jnp.exp(lam * step)
        b_bar = ((lam_bar - 1.0) / lam)[..., None] * lax.complex(_f32(b_re[dirn]), _f32(b_im[dirn]))
        c_mat = lax.complex(_f32(c_re[dirn]), _f32(c_im[dirn]))
        bu = jnp.einsum('blgc,gpc->blgp', ug, b_bar)
        h0 = lax.complex(_f32(h0_re[:, dirn]), _f32(h0_im[:, dirn]))
        edge = 0 if dirn == 0 else n_tok - 1
        bu = bu.at[:, edge].add(lam_bar * h0)
        a = jnp.broadcast_to(lam_bar, (1, n_tok) + lam_bar.shape)
        _, states = lax.associative_scan(_lin_rec, (a, bu), axis=1, reverse=(dirn == 1))
        y = y + jnp.einsum('blgp,gcp->blgc', states, c_mat).real.reshape(bsz, n_tok, D_INNER)
        final = states[:, n_tok - 1 - edge]
        fin_re.append(final.real)
        fin_im.append(final.imag)
    y = jax.nn.gelu(y)
    y = y * jax.nn.sigmoid(y @ _f32(glu_w) + _f32(glu_b))
    out = (y.astype(h.dtype) * jax.nn.silu(z)) @ w_out
    return out, jnp.stack(fin_re, axis=1), jnp.stack(fin_im, axis=1)


def _pool_mixer(h, w_in, pool_w, pool_scale, w_out):
    bsz, n_tok, _ = h.shape
    u, z = jnp.split(h @ w_in, 2, axis=-1)
    ug = _f32(u).reshape(bsz, n_tok, POOL_GROUPS, POOL_GROUP_W)
    cs = jnp.concatenate([jnp.zeros_like(ug[:, :1]), jnp.cumsum(ug, axis=1)], axis=1)
    t = np.arange(n_tok)
    pooled = []
    for g, win in enumerate(POOL_WINDOWS):
        lo = win // 2
        start = np.clip(t - lo, 0, n_tok)
        end = np.clip(t - lo + win, 0, n_tok)
        cnt = (end - start).astype(np.float32)[:, None]
        csg = cs[:, :, g]
        pooled.append((csg[:, end] - csg[:, start]) / cnt - ug[:, :, g])
    p = jnp.stack(pooled, axis=2)
    m = jnp.einsum('blgc,gcd->blgd', p, _f32(pool_w)).reshape(bsz, n_tok, D_INNER)
    m = m * _f32(pool_scale)
    return (m.astype(h.dtype) * jax.nn.silu(z)) @ w_out


def _rope_2d(x, n_tok):
    rows = n_tok // GRID_W
    tok = jnp.arange(rows * GRID_W)
    row = (tok // GRID_W).astype(jnp.float32)
    col = (tok % GRID_W).astype(jnp.float32)
    axis_dim = MLA_ROPE // 2
    half = axis_dim // 2
    inv = ROPE_THETA ** (-jnp.arange(half, dtype=jnp.float32) / half)
    bshape = (n_tok,) + (1,) * (x.ndim - 3) + (half,)
    xf = _f32(x)
    out = []
    for i, pos in enumerate((row, col)):
        ang = (pos[:, None] * inv).reshape(bshape)
        cos, sin = jnp.cos(ang), jnp.sin(ang)
        seg = xf[..., i * axis_dim:(i + 1) * axis_dim]
        x1, x2 = seg[..., :half], seg[..., half:]
        out += [x1 * cos - x2 * sin, x1 * sin + x2 * cos]
    return jnp.concatenate(out, axis=-1).astype(x.dtype)


def _mla_project(h, w_in, q_norm, wq_b, kv_norm):
    bsz, n_tok, _ = h.shape
    splits = [MLA_Q_RANK, MLA_Q_RANK + MLA_KV_RANK, MLA_Q_RANK + MLA_KV_RANK + MLA_ROPE]
    q_a, ckv, kpe, z = jnp.split(h @ w_in, splits, axis=-1)
    q = (_rmsnorm(q_a, q_norm) @ wq_b).reshape(bsz, n_tok, MLA_HEADS, MLA_NOPE + MLA_ROPE)
    return q[..., :MLA_NOPE], q[..., MLA_NOPE:], _rmsnorm(ckv, kv_norm), kpe, z


def _mla_expand(ckv_n, wkv_b):
    bsz, n_tok, _ = ckv_n.shape
    kv = (ckv_n @ wkv_b).reshape(bsz, n_tok, MLA_HEADS, MLA_NOPE + MLA_V)
    return kv[..., :MLA_NOPE], kv[..., MLA_NOPE:]


def _mla_attend(q_nope, q_pe, k_nope, k_pe, v):
    bsz, n_q, n_h, _ = q_nope.shape
    qb = math.gcd(n_q, ATTN_BLOCK)
    nb = n_q // qb
    scale = (MLA_NOPE + MLA_ROPE) ** -0.5

    def to_blocks(t):
        return t.reshape((bsz, nb, qb) + t.shape[2:]).swapaxes(0, 1)

    def block(qs):
        qn, qp = qs
        s = jnp.einsum('bqhd,bkhd->bhqk', qn, k_nope) + jnp.einsum('bqhr,bkr->bhqk', qp, k_pe)
        p = jax.nn.softmax(_f32(s) * scale, axis=-1)
        return jnp.einsum('bhqk,bkhd->bqhd', p.astype(v.dtype), v)

    o = lax.map(block, (to_blocks(q_nope), to_blocks(q_pe)))
    return o.swapaxes(0, 1).reshape(bsz, n_q, n_h * MLA_V)


def _mla_context(h, w_in, q_norm, wq_b, kv_norm, wkv_b, w_out):
    q_nope, q_pe, ckv_n, kpe, z = _mla_project(h, w_in, q_norm, wq_b, kv_norm)
    k_nope, v = _mla_expand(ckv_n, wkv_b)
    o = _mla_attend(q_nope, q_pe, k_nope, kpe, v)
    return (o * jax.nn.silu(z)) @ w_out, ckv_n, kpe


def _mla_latent(h, ctx_ckv, ctx_kpe, w_in, q_norm, wq_b, kv_norm, wkv_b, w_out):
    n_tok = h.shape[1]
    q_nope, q_pe, ckv_n, kpe, z = _mla_project(h, w_in, q_norm, wq_b, kv_norm)
    q_pe = _rope_2d(q_pe, n_tok)
    kpe = _rope_2d(kpe, n_tok)
    k_nope, v = _mla_expand(jnp.concatenate([ctx_ckv.astype(ckv_n.dtype), ckv_n], axis=1), wkv_b)
    k_pe = jnp.concatenate([ctx_kpe.astype(kpe.dtype), kpe], axis=1)
    o = _mla_attend(q_nope, q_pe, k_nope, k_pe, v)
    return (o * jax.nn.silu(z)) @ w_out


def setup_inputs(seed: int = 0) -> dict:
    key = jax.random.key(seed)
    ks = iter(jax.random.split(key, 48))

    def nrm(shape, scale=1.0):
        return scale * jax.random.normal(next(ks), shape, jnp.float32)

    def gain(shape):
        return 1.0 + nrm(shape, 0.02)

    G, P, W, D = S5_GROUPS, S5_STATE, D_INNER, D_MODEL
    s5_lam_im = jnp.pi * jnp.arange(P, dtype=jnp.float32) + nrm((N_S5_LAYERS, 2, G, P), 0.01)
    s5_log_step = jax.random.uniform(next(ks), (N_S5_LAYERS, 2, G), jnp.float32,
                                     math.log(1e-3), math.log(1e-1))
    mla_in = MLA_Q_RANK + MLA_KV_RANK + MLA_ROPE + W
    return {
        'x_prompt': nrm((BATCH, SEQ, D)),
        'x_sample': nrm((DEC_BATCH, DEC_SEQ, D)),
        'state_s5_re': nrm((DEC_BATCH, N_S5_LAYERS, 2, G, P), 0.1),
        'state_s5_im': nrm((DEC_BATCH, N_S5_LAYERS, 2, G, P), 0.1),
        'cache_ckv': nrm((DEC_BATCH, N_MLA_LAYERS, PAST_LEN, MLA_KV_RANK)),
        'cache_kpe': nrm((DEC_BATCH, N_MLA_LAYERS, PAST_LEN, MLA_ROPE)),
        'c': nrm((DEC_BATCH, D)),
        'c_ctx': nrm((D,)),
        'norm_g': gain((DEPTH, D)),
        'ada_w': nrm((DEPTH, D, 3 * D), 0.5 * D ** -0.5),
        'ada_b': nrm((DEPTH, 3 * D), 0.02),
        'final_norm_g': gain((D,)),
        's5_w_in': nrm((N_S5_LAYERS, D, 2 * W), D ** -0.5),
        's5_lam_re': -0.5 + nrm((N_S5_LAYERS, 2, G, P), 0.01),
        's5_lam_im': s5_lam_im,
        's5_log_step': s5_log_step,
        's5_b_re': nrm((N_S5_LAYERS, 2, G, P, S5_GROUP), (2 * S5_GROUP) ** -0.5),
        's5_b_im': nrm((N_S5_LAYERS, 2, G, P, S5_GROUP), (2 * S5_GROUP) ** -0.5),
        's5_c_re': nrm((N_S5_LAYERS, 2, G, S5_GROUP, P), (2 * P) ** -0.5),
        's5_c_im': nrm((N_S5_LAYERS, 2, G, S5_GROUP, P), (2 * P) ** -0.5),
        's5_d': nrm((N_S5_LAYERS, W)),
        's5_glu_w': nrm((N_S5_LAYERS, W, W), W ** -0.5),
        's5_glu_b': nrm((N_S5_LAYERS, W), 0.02),
        's5_w_out': nrm((N_S5_LAYERS, W, D), W ** -0.5),
        'pool_w_in': nrm((N_POOL_LAYERS, D, 2 * W), D ** -0.5),
        'pool_w': nrm((N_POOL_LAYERS, POOL_GROUPS, POOL_GROUP_W, POOL_GROUP_W), POOL_GROUP_W ** -0.5),
        'pool_scale': 1.0 + nrm((N_POOL_LAYERS, W), 0.1),
        'pool_w_out': nrm((N_POOL_LAYERS, W, D), W ** -0.5),
        'mla_w_in': nrm((N_MLA_LAYERS, D, mla_in), D ** -0.5),
        'mla_q_norm': gain((N_MLA_LAYERS, MLA_Q_RANK)),
        'mla_wq_b': nrm((N_MLA_LAYERS, MLA_Q_RANK, MLA_HEADS * (MLA_NOPE + MLA_ROPE)), MLA_Q_RANK ** -0.5),
        'mla_kv_norm': gain((N_MLA_LAYERS, MLA_KV_RANK)),
        'mla_wkv_b': nrm((N_MLA_LAYERS, MLA_KV_RANK, MLA_HEADS * (MLA_NOPE + MLA_V)), MLA_KV_RANK ** -0.5),
        'mla_w_out': nrm((N_MLA_LAYERS, MLA_HEADS * MLA_V, D), (MLA_HEADS * MLA_V) ** -0.5),
    }


def reference(x_prompt, x_sample, state_s5_re, state_s5_im, cache_ckv, cache_kpe, c, c_ctx,
              norm_g, ada_w, ada_b, final_norm_g,
              s5_w_in, s5_lam_re, s5_lam_im, s5_log_step, s5_b_re, s5_b_im, s5_c_re, s5_c_im,
              s5_d, s5_glu_w, s5_glu_b, s5_w_out,
              pool_w_in, pool_w, pool_scale, pool_w_out,
              mla_w_in, mla_q_norm, mla_wq_b, mla_kv_norm, mla_wkv_b, mla_w_out):
    xp, xs = x_prompt, x_sample
    zero_state = jnp.zeros((xp.shape[0], 2, S5_GROUPS, S5_STATE), jnp.float32)
    new_re, new_im, new_ckv, new_kpe = [], [], [], []
    for layer in range(DEPTH):
        kind = layer % N_MIXERS
        j = layer // N_MIXERS
        sh_p, sc_p, g_p = _ada(c_ctx, ada_w[layer], ada_b[layer])
        sh_s, sc_s, g_s = [t[:, None] for t in _ada(c, ada_w[layer], ada_b[layer])]
        hp = _rmsnorm(xp, norm_g[layer]) * (1 + sc_p) + sh_p
        hs = _rmsnorm(xs, norm_g[layer]) * (1 + sc_s) + sh_s
        if kind == 0:
            prm = (s5_w_in[j], s5_lam_re[j], s5_lam_im[j], s5_log_step[j], s5_b_re[j], s5_b_im[j],
                   s5_c_re[j], s5_c_im[j], s5_d[j], s5_glu_w[j], s5_glu_b[j], s5_w_out[j])
            yp, fin_re, fin_im = _s5_mixer(hp, zero_state, zero_state, *prm)
            ys, _, _ = _s5_mixer(hs, state_s5_re[:, j], state_s5_im[:, j], *prm)
            new_re.append(fin_re)
            new_im.append(fin_im)
        elif kind == 1:
            prm = (pool_w_in[j], pool_w[j], pool_scale[j], pool_w_out[j])
            yp = _pool_mixer(hp, *prm)
            ys = _pool_mixer(hs, *prm)
        else:
            prm = (mla_w_in[j], mla_q_norm[j], mla_wq_b[j], mla_kv_norm[j], mla_wkv_b[j], mla_w_out[j])
            yp, ckv_n, kpe = _mla_context(hp, *prm)
            ys = _mla_latent(hs, cache_ckv[:, j], cache_kpe[:, j], *prm)
            new_ckv.append(ckv_n)
            new_kpe.append(kpe)
        xp = xp + g_p * yp
        xs = xs + g_s * ys
    y_prompt = _rmsnorm(xp, final_norm_g)
    y_sample = _rmsnorm(xs, final_norm_g)
    new_s5_re = jnp.stack(new_re, axis=1)
    new_s5_im = jnp.stack(new_im, axis=1)
    new_ckv_s = jnp.stack(new_ckv, axis=1)
    new_kpe_s = jnp.stack(new_kpe, axis=1)
    return (y_prompt, y_sample, new_s5_re, new_s5_im, new_ckv_s, new_kpe_s)
```

```python
import os
import numpy as np
from contextlib import ExitStack
import concourse.bass as bass
import concourse.mybir as mybir
from concourse.bass_utils import run_bass_kernel_spmd

F32 = mybir.dt.float32
BF16 = mybir.dt.bfloat16
I32 = mybir.dt.int32
AF = mybir.ActivationFunctionType
ALU = mybir.AluOpType
AX = mybir.AxisListType

ENGS = ['pe', 'act', 'dve', 'pool', 'sp']


class Res:
    __slots__ = ('name', 'lw', 'rs')

    def __init__(self, name):
        self.name = name
        self.lw = None
        self.rs = {}


class Sched:
    def __init__(self, nc, stack, n_dma=24):
        self.nc = nc
        self.sem = {e: stack.enter_context(nc.semaphore('s_' + e)) for e in ENGS}
        self.cnt = {e: 0 for e in ENGS}
        self.ops = {e: [] for e in ENGS}
        self.waited = {e: {} for e in ENGS}
        self.dsem = [stack.enter_context(nc.semaphore('d%d' % i)) for i in range(n_dma)]
        self.dval = [0] * n_dma
        half = n_dma // 2
        self.dpool = {'pool': list(range(0, half)), 'sp': list(range(half, n_dma)), 'act': list(range(half, n_dma))}
        self.dnext = {'pool': 0, 'sp': 0, 'act': 0}
        self.n_ops = 0

    def _wait(self, e, stamp):
        key, val = stamp
        if self.waited[e].get(key, 0) >= val:
            return
        self.waited[e][key] = val
        sem = self.sem[key[1]] if key[0] == 'e' else self.dsem[key[1]]
        self.ops[e].append(lambda eng, sem=sem, val=val: eng.wait_ge(sem, val))

    def _deps(self, e, reads, writes):
        deps = []
        for r in reads:
            if r.lw is not None:
                deps.append(r.lw)
        for w in writes:
            if w.lw is not None:
                deps.append(w.lw)
            deps.extend(w.rs.values())
        for d in deps:
            if e == 'pe' and d[0] == ('e', 'pe'):
                continue
            self._wait(e, d)

    def _update(self, stamp, reads, writes):
        for r in reads:
            r.rs[stamp[0]] = stamp
        for w in writes:
            w.lw = stamp
            w.rs = {}

    def op(self, e, fn, reads=(), writes=()):
        self._deps(e, reads, writes)
        self.cnt[e] += 1
        sem = self.sem[e]
        self.ops[e].append(lambda eng, fn=fn, sem=sem: fn(eng).then_inc(sem, 1))
        self._update((('e', e), self.cnt[e]), reads, writes)
        self.n_ops += 1

    def dma(self, e, fn, reads=(), writes=()):
        self._deps(e, reads, writes)
        lst = self.dpool[e]
        i = lst[self.dnext[e] % len(lst)]
        self.dnext[e] += 1
        if self.dval[i] > 0:
            self._wait(e, (('d', i), self.dval[i]))
        self.dval[i] += 16
        sem = self.dsem[i]
        self.ops[e].append(lambda eng, fn=fn, sem=sem: fn(eng).then_inc(sem, 16))
        self._update((('d', i), self.dval[i]), reads, writes)
        self.n_ops += 1

    def finish(self):
        for i, v in enumerate(self.dval):
            if v > 0:
                self._wait('sp', (('d', i), v))
        for e in ENGS:
            if e != 'sp' and self.cnt[e] > 0:
                self._wait('sp', (('e', e), self.cnt[e]))

    def replay(self):
        nc = self.nc
        with nc.Block() as block:
            @block.tensor
            def _(eng):
                for f in self.ops['pe']:
                    f(eng)

            @block.scalar
            def _(eng):
                for f in self.ops['act']:
                    f(eng)

            @block.vector
            def _(eng):
                for f in self.ops['dve']:
                    f(eng)

            @block.gpsimd
            def _(eng):
                for f in self.ops['pool']:
                    f(eng)

            @block.sync
            def _(eng):
                for f in self.ops['sp']:
                    f(eng)


D = 1024
W = 2048
NTOK = 3072
EPS = 1e-6
ATT_SCALE = 192.0 ** -0.5


class T:
    def __init__(self, t, name):
        self.t = t
        self.r = Res(name)

    def __getitem__(self, k):
        return self.t[k]


class Prog:
    def __init__(self, nc, st):
        self.nc = nc
        self.st = st
        self.S = Sched(nc, st)
        self.ps = []
        for i in range(8):
            t = st.enter_context(nc.psum_tensor("ps%d" % i, [128, 512], F32))
            self.ps.append(T(t, "ps%d" % i))
        self.psn = 0
        self.cnt = 0

    def sb(self, scope, name, shape, dt, side=None):
        self.cnt += 1
        nm = "%s_%d" % (name, self.cnt)
        kw = {} if side is None else {"side": side}
        return T(scope.enter_context(self.nc.sbuf_tensor(nm, list(shape), dt, **kw)), nm)

    def next_ps(self):
        p = self.ps[self.psn]
        self.psn = (self.psn + 1) % 8
        return p

    def barrier(self):
        S = self.S
        for e in ENGS:
            for e2 in ENGS:
                if S.cnt[e2] > 0:
                    S._wait(e, (('e', e2), S.cnt[e2]))
            for i, v in enumerate(S.dval):
                if v > 0:
                    S._wait(e, (('d', i), v))

    def mm(self, ps, out_ap, lhsT, rhs, start, stop, reads):
        self.S.op('pe', lambda e: e.matmul(out_ap, lhsT=lhsT, rhs=rhs, start=start, stop=stop),
                  reads=[x.r for x in reads], writes=[ps.r])

    def tr(self, ps, out_ap, in_ap, reads, ident=None):
        idb = self.identb if ident is None else ident
        self.S.op('pe', lambda e: e.transpose(out=out_ap, in_=in_ap, identity=idb[:]),
                  reads=[x.r for x in reads] + [idb.r], writes=[ps.r])

    def act(self, out_ap, in_ap, func, reads, writes, scale=None, bias=None, accum=None):
        kw = {}
        if scale is not None:
            kw['scale'] = scale
        if bias is not None:
            kw['bias'] = bias
        if accum is not None:
            kw['accum_out'] = accum
        self.S.op('act', lambda e: e.activation(out=out_ap, in_=in_ap, func=func, **kw),
                  reads=[x.r for x in reads], writes=[x.r for x in writes])

    def tt(self, eng, out_ap, a, b, op, reads, writes):
        self.S.op(eng, lambda e: e.tensor_tensor(out=out_ap, in0=a, in1=b, op=op),
                  reads=[x.r for x in reads], writes=[x.r for x in writes])

    def ts(self, eng, out_ap, a, s1, s2, op0, op1, reads, writes):
        if op1 is None:
            self.S.op(eng, lambda e: e.tensor_scalar(out=out_ap, in0=a, scalar1=s1, scalar2=None, op0=op0),
                      reads=[x.r for x in reads], writes=[x.r for x in writes])
        else:
            self.S.op(eng, lambda e: e.tensor_scalar(out=out_ap, in0=a, scalar1=s1, scalar2=s2, op0=op0, op1=op1),
                      reads=[x.r for x in reads], writes=[x.r for x in writes])

    def stt(self, out_ap, a, sc, b, op0, op1, reads, writes):
        self.S.op('dve', lambda e: e.scalar_tensor_tensor(out=out_ap, in0=a, scalar=sc, in1=b, op0=op0, op1=op1),
                  reads=[x.r for x in reads], writes=[x.r for x in writes])

    def cp(self, eng, out_ap, in_ap, reads, writes):
        if eng == 'act':
            self.S.op('act', lambda e: e.activation(out=out_ap, in_=in_ap, func=AF.Copy),
                      reads=[x.r for x in reads], writes=[x.r for x in writes])
        else:
            self.S.op(eng, lambda e: e.tensor_copy(out=out_ap, in_=in_ap),
                      reads=[x.r for x in reads], writes=[x.r for x in writes])

    def recip(self, out_ap, in_ap, reads, writes):
        self.S.op('dve', lambda e: e.reciprocal(out=out_ap, in_=in_ap),
                  reads=[x.r for x in reads], writes=[x.r for x in writes])

    def memset(self, eng, ap, val, writes):
        self.S.op(eng, lambda e: e.memset(ap, val), writes=[x.r for x in writes])

    def dma(self, eng, out_ap, in_ap, reads, writes):
        self.S.dma(eng, lambda e: e.dma_start(out=out_ap, in_=in_ap),
                   reads=[x.r for x in reads], writes=[x.r for x in writes])


class RW:
    def __init__(self, name):
        self.r = Res(name)


class TV:
    def __init__(self, base, name):
        self.t = base.t
        self.r = Res(name)

    def __getitem__(self, k):
        return self.t[k]


class DT:
    def __init__(self, ap, name):
        self.ap = ap
        self.r = Res(name)


def tile_kind(ti):
    return 1 if ti < 2 else 0


class Builder:
    def __init__(self, layers):
        self.layers = list(layers)
        nc = bass.Bass("TRN2", target_bir_lowering=False)
        self.nc = nc
        self.din_names = {}
        self.dout_names = {}

    def din(self, name, shape, dt=F32):
        ap = self.nc.dram_tensor(name, list(shape), dt, kind="ExternalInput").ap()
        self.din_names[name] = tuple(shape)
        return DT(ap, name)

    def dout(self, name, shape):
        ap = self.nc.dram_tensor(name, list(shape), F32, kind="ExternalOutput").ap()
        self.dout_names[name] = tuple(shape)
        return DT(ap, name)

    def dbg(self, name, t, shape):
        if not os.environ.get("KDBG"):
            return
        d = self.dout("dbg_" + name, shape)
        self.P.dma('sp', d.ap, t, [], [d])

    def dscr(self, name, shape, dt):
        return DT(self.nc.dram_tensor(name, list(shape), dt).ap(), name)

    def build(self):
        nc = self.nc
        with ExitStack() as st:
            P = Prog(nc, st)
            self.P = P
            self.declare_io()
            self.setup_consts(st)
            nl = len(self.layers)
            for li, layer in enumerate(self.layers):
                with ExitStack() as lsc:
                    self.run_layer(li, layer, lsc, first=(li == 0), last=(li == nl - 1))
                P.barrier()
            P.S.finish()
            P.S.replay()
        return nc

    def declare_io(self):
        self.xs = self.din("xs", [2048, D])
        self.xp = self.din("xp", [1024, D])
        self.condT = self.din("condT", [128, 16])
        self.ident = self.din("ident", [128, 128])
        self.norm_g = self.din("norm_g", [4, D])
        self.ada_w = self.din("ada_w", [4, D, 3 * D])
        self.ada_b = self.din("ada_b", [4, 3 * D])
        self.fng = self.din("final_norm_g", [1, D])
        self.ys = self.dout("ys", [2048, D])
        self.yp = self.dout("yp", [1024, D])
        self.xres = self.dscr("xres", [3, 128, 8, D], F32)
        self.ptd = self.dscr("ptd", [16, 128, NTOK], BF16)
        self.xres_r = [[RW("xres%d_%d" % (a, b)) for b in range(8)] for a in range(3)]
        self.xin_r = [[RW("xin%d_%d" % (a, b)) for b in range(8)] for a in range(3)]
        self.yout_r = [[RW("yout%d_%d" % (a, b)) for b in range(8)] for a in range(3)]
        self.ptd_r = [RW("ptd%d" % a) for a in range(16)]
        kinds = set(l % 3 for l in self.layers)
        if 1 in kinds:
            self.pool_w_in = self.din("pool_w_in", [D, 2 * W])
            self.pool_w = self.din("pool_w", [4, 512, 512])
            self.pool_scaleT = self.din("pool_scaleT", [128, 16])
            self.pool_w_out = self.din("pool_w_out", [W, D])
            self.pool_invc = self.din("pool_invc", [4, NTOK])
        if 2 in kinds:
            self.declare_mla()
        if 0 in kinds:
            self.declare_s5()

    def x_view(self, src_first, ti):
        if src_first:
            if ti < 2:
                return self.xs.ap.rearrange("(a p t) d -> a p t d", p=128, t=8)[ti], self.xin_r[ti]
            return self.xp.ap.rearrange("(p t) d -> p t d", t=8), self.xin_r[ti]
        return self.xres.ap[ti], self.xres_r[ti]

    def y_view(self, ti):
        if ti < 2:
            return self.ys.ap.rearrange("(a p t) d -> a p t d", p=128, t=8)[ti], self.yout_r[ti]
        return self.yp.ap.rearrange("(p t) d -> p t d", t=8), self.yout_r[ti]

    def setup_consts(self, st):
        P = self.P
        P.identb = P.sb(st, "identb", [128, 128], BF16)
        P.dma('pool', P.identb[:], self.ident.ap[:, :], [self.ident], [P.identb])
        self.identf = P.sb(st, "identf", [128, 128], F32)
        P.dma('sp', self.identf[:], self.ident.ap[:, :], [self.ident], [self.identf])
        cT = P.sb(st, "cT", [128, 16], F32)
        P.dma('sp', cT[:], self.condT.ap[:, :], [self.condT], [cT])
        cS = P.sb(st, "cS", [128, 16], F32)
        P.act(cS[:], cT[:], AF.Silu, [cT], [cS])
        self.cS = cS
        self.wbufs = []
        self.wbn = 0
        self.gbcd = self.dscr("gbcd", [128, 2, D], F32)
        self.junk = P.sb(st, "junk", [128, D], BF16)

    def set_wbufs(self, scope, n, size=4096):
        self.wbufs = [self.P.sb(scope, "wbuf%d" % i, [128, size], BF16) for i in range(n)]
        self.wbn = 0

    def wbuf(self):
        w = self.wbufs[self.wbn]
        self.wbn = (self.wbn + 1) % len(self.wbufs)
        return w

    def load_w(self, src, rows0, nk, col0, ncol):
        P = self.P
        w = self.wbuf()
        view = w[:, 0:nk * ncol].rearrange("p (k n) -> p k n", k=nk)
        sap = src.ap[rows0:rows0 + nk * 128, col0:col0 + ncol].rearrange("(k p) n -> p k n", p=128)
        P.dma('pool', view, sap, [src], [w])
        return w, view

    def ada_phase(self, layer, sc):
        P = self.P
        cS = self.cS
        self.condB = P.sb(sc, "condB", [128, 2, 8, 128], BF16)
        for j in range(2):
            for k in range(8):
                P.cp('dve', self.condB[:, j, k, :], cS[:, k * 2 + j:k * 2 + j + 1].to_broadcast([128, 128]),
                     [cS], [self.condB])
        mod = P.sb(sc, "mod", [128, 2, 3 * D], F32)
        bb = P.sb(sc, "adab", [128, 3 * D], F32)
        P.dma('sp', bb[:], self.ada_b.ap[layer].partition_broadcast(128), [self.ada_b], [bb])
        ng = P.sb(sc, "ngbc", [128, D], F32)
        P.dma('sp', ng[:], self.norm_g.ap[layer].partition_broadcast(128), [self.norm_g], [ng])
        aw = DT(self.ada_w.ap[layer], "x")
        aw.r = self.ada_w.r
        for cb in range(6):
            w, wv = self.load_w(aw, 0, 8, cb * 512, 512)
            for j in range(2):
                ps = P.next_ps()
                for k in range(8):
                    P.mm(ps, ps[:, :], self.condB[:, j, k, :], wv[:, k, :], k == 0, k == 7, [self.condB, w])
                P.tt('dve', mod[:, j, cb * 512:(cb + 1) * 512], ps[:, :], bb[:, cb * 512:(cb + 1) * 512], ALU.add,
                     [ps, bb], [mod])
        for j in range(2):
            P.stt(mod[:, j, D:2 * D], mod[:, j, D:2 * D], 1.0, ng[:], ALU.add, ALU.mult, [mod, ng], [mod])
            P.dma('sp', self.gbcd.ap[:, j, :], mod[:, j, 2 * D:3 * D], [mod], [self.gbcd])
        return mod

    def rstd_from_ss(self, ss, rstd, n, dim):
        P = self.P
        P.ts('dve', rstd[:, 0:n], ss[:, 0:n], 1.0 / dim, EPS, ALU.mult, ALU.add, [ss], [rstd])
        P.act(rstd[:, 0:n], rstd[:, 0:n], AF.Sqrt, [rstd], [rstd])
        P.recip(rstd[:, 0:n], rstd[:, 0:n], [rstd], [rstd])

    def norm_phase(self, layer, first, hT, sc):
        P = self.P
        with ExitStack() as s2:
            self.set_wbufs(s2, 3)
            mod = self.ada_phase(layer, s2)
            xtb = [P.sb(s2, "xt%d" % i, [128, 8, D], F32) for i in range(2)]
            hcm = P.sb(s2, "hcm", [128, 8, D], BF16)
            tmp = [P.sb(s2, "ntmp%d" % i, [128, D], F32) for i in range(2)]
            ss = P.sb(s2, "nss", [128, 8], F32)
            rstd = P.sb(s2, "nrstd", [128, 8], F32)
            for ti in range(3):
                j = tile_kind(ti)
                xt = xtb[ti % 2]
                xv, xsrc = self.x_view(first, ti)
                P.dma('sp', xt[:], xv, list(xsrc), [xt])
                for t in range(8):
                    P.act(self.junk[:], xt[:, t, :], AF.Square, [xt], [self.junk, ss], accum=ss[:, t:t + 1])
                self.rstd_from_ss(ss, rstd, 8, D)
                for t in range(8):
                    tm = tmp[t % 2]
                    P.stt(tm[:], xt[:, t, :], rstd[:, t:t + 1], mod[:, j, D:2 * D], ALU.mult, ALU.mult,
                          [xt, rstd, mod], [tm])
                    P.tt('pool' if t % 3 != 2 else 'dve', hcm[:, t, :], tm[:], mod[:, j, 0:D], ALU.add, [tm, mod], [hcm])
                for t in range(8):
                    ps = P.next_ps()
                    pv = ps[:, :].bitcast(BF16).rearrange("p (k n) -> p k n", k=8)
                    for k in range(8):
                        P.tr(ps, pv[:, k, :], hcm[:, t, k * 128:(k + 1) * 128], [hcm])
                    eng = 'act' if t % 2 == 0 else 'dve'
                    base = ti * 1024 + t
                    P.cp(eng, hT[:, :, base:(ti + 1) * 1024:8], pv, [ps], [hT])
        P.barrier()

    def out_phase(self, wout_src, pt_loader, first, last, sc):
        P = self.P
        with ExitStack() as s2:
            wo = P.sb(s2, "wo", [128, 16, D], BF16)
            for q in range(4):
                sap = wout_src.ap[q * 512:(q + 1) * 512, :].rearrange("(k p) n -> p k n", p=128)
                P.dma('pool', wo[:, q * 4:(q + 1) * 4, :], sap, [wout_src], [wo])
            self.gbc = P.sb(s2, "gbc", [128, 2, D], F32)
            P.dma('sp', self.gbc[:], self.gbcd.ap[:, :, :], [self.gbcd], [self.gbc])
            if last:
                self.fng_bc = P.sb(s2, "fng_bc", [128, D], F32)
                P.dma('sp', self.fng_bc[:], self.fng.ap[0].partition_broadcast(128), [self.fng], [self.fng_bc])
            xts = [P.sb(s2, "oxt%d" % i, [128, D], F32) for i in range(3)]
            tmps = [P.sb(s2, "otmp%d" % i, [128, 512], F32) for i in range(2)]
            ss = P.sb(s2, "oss", [128, 1], F32)
            rstd = P.sb(s2, "orstd", [128, 1], F32)
            for ti in range(3):
                j = tile_kind(ti)
                pt = pt_loader(ti, s2)
                xv, xsrc = self.x_view(first, ti)
                if last:
                    ov, odst = self.y_view(ti)
                else:
                    ov, odst = self.xres.ap[ti], self.xres_r[ti]
                def ld(t_):
                    P.dma('sp', xts[t_ % 3][:], xv[:, t_, :], [xsrc[t_]], [xts[t_ % 3]])
                ld(0)
                for t in range(8):
                    xt = xts[t % 3]
                    if t + 1 < 8:
                        ld(t + 1)
                    for h in range(2):
                        ps = P.next_ps()
                        for k in range(16):
                            P.mm(ps, ps[:, :], pt[:, k, t:1024:8], wo[:, k, h * 512:(h + 1) * 512],
                                 k == 0, k == 15, [pt, wo])
                        tm = tmps[h]
                        P.tt('dve', tm[:], ps[:, :], self.gbc[:, j, h * 512:(h + 1) * 512], ALU.mult,
                             [ps, self.gbc], [tm])
                        P.tt('pool' if h == 0 else 'dve', xt[:, h * 512:(h + 1) * 512], xt[:, h * 512:(h + 1) * 512], tm[:],
                             ALU.add, [xt, tm], [xt])
                    if last:
                        P.act(self.junk[:], xt[:], AF.Square, [xt], [self.junk, ss], accum=ss[:, 0:1])
                        self.rstd_from_ss(ss, rstd, 1, D)
                        P.stt(xt[:], xt[:], rstd[:, 0:1], self.fng_bc[:], ALU.mult, ALU.mult,
                              [xt, rstd, self.fng_bc], [xt])
                    P.dma('sp', ov[:, t, :], xt[:], [xt], [odst[t]])
        P.barrier()

    def pt_from_dram(self, ti, sc):
        P = self.P

        def ld(t_):
            pt_ = self._pt_tiles[t_ % 2]
            P.dma('sp', pt_[:], self.ptd.ap[:, :, t_ * 1024:(t_ + 1) * 1024].rearrange("k p n -> p k n"),
                  self.ptd_r, [pt_])
        if not hasattr(self, "_pt_tiles") or self._pt_scope is not sc:
            self._pt_tiles = [P.sb(sc, "ptt%d" % i, [128, 16, 1024], BF16) for i in range(2)]
            self._pt_scope = sc
            ld(0)
        if ti + 1 < 3:
            ld(ti + 1)
        return self._pt_tiles[ti % 2]

    def run_layer(self, li, layer, sc, first, last):
        P = self.P
        kind = layer % 3
        jj = layer // 3
        hsc = ExitStack()
        hT = P.sb(hsc, "hT", [128, 8, NTOK], BF16, side="right")
        self.norm_phase(layer, first, hT, sc)
        if kind == 1:
            self.pool_mixer(jj, hT, sc)
            P.barrier()
            hsc.close()
            self.out_phase(self.pool_w_out, self.pt_from_dram, first, last, sc)
        elif kind == 2:
            self.mla_mixer(jj, hT, sc, hsc)
            P.barrier()
            self.out_phase(DTsub(self.mla_w_out, jj), self.pt_from_dram, first, last, sc)
        else:
            self.s5_mixer(jj, hT, sc, hsc, first, last)


def DTsub(dt, idx):
    d = DT(dt.ap[idx], "sub")
    d.r = dt.r
    return d


def _pool_mixer(self, jj, hT, sc):
    P = self.P
    with ExitStack() as s2:
        self.set_wbufs(s2, 4)
        def bufpair(name):
            return (P.sb(s2, name + "s", [128, 1, 2048 + 32], F32), P.sb(s2, name + "p", [128, 4, 256 + 32], F32))
        U = bufpair("pU")
        A = bufpair("pA")
        B = bufpair("pB")
        Ls = (2048, 256)
        for b in U:
            P.memset('pool', b[:], 0.0, [b])
        invcs = [P.sb(s2, "invc%d" % i, [128, NTOK], F32) for i in range(2)]
        pooled = P.sb(s2, "pooled", [128, 4, NTOK], BF16)
        pscale = P.sb(s2, "pscale", [128, 16], F32)
        P.dma('sp', pscale[:], self.pool_scaleT.ap[:, :], [self.pool_scaleT], [pscale])
        szs = [P.sb(s2, "sz%d" % i, [128, 512], BF16) for i in range(2)]
        ptb = [P.sb(s2, "ptb%d" % i, [128, NTOK], BF16) for i in range(2)]
        win_src = self.pool_w_in
        def ld_invc(g_):
            P.dma('sp', invcs[g_ % 2][:], self.pool_invc.ap[g_].partition_broadcast(128), [self.pool_invc],
                  [invcs[g_ % 2]])
        ld_invc(0)
        for g in range(4):
            invc = invcs[g % 2]
            if g + 1 < 4:
                ld_invc(g + 1)
            wu, wuv = self.load_w(win_src, 0, 8, g * 512, 512)
            eng = 'dve'
            for jb in range(4):
                for pc in range(6):
                    ps = P.next_ps()
                    for k in range(8):
                        P.mm(ps, ps[:, :], wuv[:, k, jb * 128:(jb + 1) * 128], hT[:, k, pc * 512:(pc + 1) * 512],
                             k == 0, k == 7, [wu, hT])
                    if pc < 4:
                        P.cp('act', U[0][:, 0, 16 + pc * 512:16 + (pc + 1) * 512], ps[:, :], [ps], [U[0]])
                    else:
                        q = pc - 4
                        P.cp('act', U[1][:, 2 * q:2 * q + 2, 16:272], ps[:, :].rearrange("p (s l) -> p s l", s=2),
                             [ps], [U[1]])
                eng = 'dve'
                cur = U
                nxt = [A, B]
                steps = [(1, 1, 0), (2, 3, 1), (4, 2, 6), (8, 4, 12)][:g + 1]
                for si, (olo, alo, blo) in enumerate(steps):
                    dst = nxt[si % 2]
                    for q in range(2):
                        L = Ls[q]
                        n = {1: L + 31, 2: L + 29, 4: L + 25, 8: L + 17}[olo]
                        P.tt(eng, dst[q][:, :, olo:olo + n], cur[q][:, :, alo:alo + n], cur[q][:, :, blo:blo + n],
                             ALU.add, [cur[q]], [dst[q]])
                    cur = dst
                other = B if cur is A else A
                for q in range(2):
                    L = Ls[q]
                    if q == 0:
                        iv = invc[:, 0:2048].rearrange("p (s l) -> p s l", s=1)
                        pv = pooled[:, jb, 0:2048].rearrange("p (s l) -> p s l", s=1)
                    else:
                        iv = invc[:, 2048:NTOK].rearrange("p (s l) -> p s l", s=4)
                        pv = pooled[:, jb, 2048:NTOK].rearrange("p (s l) -> p s l", s=4)
                    P.tt(eng, other[q][:, :, 16:16 + L], cur[q][:, :, 16:16 + L], iv, ALU.mult,
                         [cur[q], invc], [other[q]])
                    P.tt(eng, pv, other[q][:, :, 16:16 + L], U[q][:, :, 16:16 + L], ALU.subtract,
                         [other[q], U[q]], [pooled])
            pw, pwv = self.load_w(DTsub(self.pool_w, g), 0, 4, 0, 512)
            wz, wzv = self.load_w(win_src, 0, 8, W + g * 512, 512)
            for db in range(4):
                d = g * 4 + db
                pt = ptb[d % 2]
                for pc in range(6):
                    psm = P.next_ps()
                    for c in range(4):
                        P.mm(psm, psm[:, :], pwv[:, c, db * 128:(db + 1) * 128], pooled[:, c, pc * 512:(pc + 1) * 512],
                             c == 0, c == 3, [pw, pooled])
                    psz = P.next_ps()
                    for k in range(8):
                        P.mm(psz, psz[:, :], wzv[:, k, db * 128:(db + 1) * 128], hT[:, k, pc * 512:(pc + 1) * 512],
                             k == 0, k == 7, [wz, hT])
                    sz = szs[pc % 2]
                    P.act(sz[:], psz[:, :], AF.Silu, [psz], [sz])
                    P.stt(pt[:, pc * 512:(pc + 1) * 512], psm[:, :], pscale[:, d:d + 1], sz[:], ALU.mult, ALU.mult,
                          [psm, pscale, sz], [pt])
                P.dma('sp', self.ptd.ap[d], pt[:], [pt], [self.ptd_r[d]])


Builder.pool_mixer = _pool_mixer


_PROG_CACHE = {}


def _pool_invc():
    out = np.zeros((4, NTOK), np.float32)
    for g, win in enumerate((2, 4, 8, 16)):
        lo = win // 2
        for (base, L, n) in ((0, 2048, 1), (2048, 256, 4)):
            t = np.arange(L)
            cnt = (np.clip(t - lo + win, 0, L) - np.clip(t - lo, 0, L)).astype(np.float32)
            for s in range(n):
                out[g, base + s * L:base + (s + 1) * L] = 1.0 / cnt
    return out


def make_in_maps(inp, layers):
    f = lambda a: np.ascontiguousarray(np.asarray(a, dtype=np.float32))
    kinds = set(l % 3 for l in layers)
    shared = {
        "ident": np.eye(128, dtype=np.float32),
        "norm_g": f(inp["norm_g"]), "ada_w": f(inp["ada_w"]), "ada_b": f(inp["ada_b"]),
        "final_norm_g": f(inp["final_norm_g"]).reshape(1, D),
    }
    if 1 in kinds:
        shared.update({
            "pool_w_in": f(inp["pool_w_in"][0]), "pool_w": f(inp["pool_w"][0]),
            "pool_scaleT": f(np.asarray(inp["pool_scale"][0]).reshape(16, 128).T),
            "pool_w_out": f(inp["pool_w_out"][0]), "pool_invc": _pool_invc(),
        })
    if 2 in kinds:
        shared.update(mla_shared_inputs(inp))
    if 0 in kinds:
        shared.update(s5_shared_inputs(inp))
    maps = []
    xp = np.asarray(inp["x_prompt"], np.float32)
    xs = np.asarray(inp["x_sample"], np.float32)
    cc = np.asarray(inp["c"], np.float32)
    cctx = np.asarray(inp["c_ctx"], np.float32)
    for c in range(8):
        b = c % 2
        m = dict(shared)
        m["xs"] = f(xs[b])
        m["xp"] = f(xp[4 * c:4 * c + 4].reshape(1024, D))
        cond = np.stack([cctx, cc[b]], axis=-1)
        m["condT"] = f(cond.reshape(8, 128, 2).transpose(1, 0, 2).reshape(128, 16))
        if 2 in kinds:
            m.update(mla_core_inputs(inp, c))
        if 0 in kinds:
            m.update(s5_core_inputs(inp, c))
        maps.append(m)
    return maps


def run_layers(inp, layers):
    key = tuple(layers)
    if key not in _PROG_CACHE:
        b = Builder(layers)
        b.build()
        _PROG_CACHE[key] = b
    b = _PROG_CACHE[key]
    maps = make_in_maps(inp, layers)
    maps = [{k: v for k, v in m.items() if k in b.din_names} for m in maps]
    res = run_bass_kernel_spmd(b.nc, maps, core_ids=list(range(8)))
    return res.results


def assemble(results, layers):
    kinds = [l % 3 for l in layers]
    yp = np.concatenate([results[c]["yp"].reshape(4, 256, D) for c in range(8)], axis=0)
    ys = np.stack([results[b]["ys"] for b in range(2)], axis=0)
    outs = [yp.astype(np.float32), ys.astype(np.float32)]
    n_s5 = sum(1 for k in kinds if k == 0)
    n_mla = sum(1 for k in kinds if k == 2)
    if n_s5:
        re = np.concatenate([results[c]["ns_re"].reshape(4, n_s5, 2, 128, 64) for c in range(8)], axis=0)
        im = np.concatenate([results[c]["ns_im"].reshape(4, n_s5, 2, 128, 64) for c in range(8)], axis=0)
        outs += [re.astype(np.float32), im.astype(np.float32)]
    if n_mla:
        ck = np.concatenate([results[c]["nckv"].reshape(4, n_mla, 256, 128) for c in range(8)], axis=0)
        kp = np.concatenate([results[c]["nkpe"].reshape(4, n_mla, 256, 64) for c in range(8)], axis=0)
        outs += [ck.astype(np.float32), kp.astype(np.float32)]
    return tuple(outs)


LAYERS = (0, 1, 2, 3)


def kernel(**inputs):
    results = run_layers(inputs, LAYERS)
    return assemble(results, LAYERS)


KSTOP = os.environ.get("KSTOP", "")
KSKIP = os.environ.get("KSKIP", "")
NB_KEYS = 28


def _declare_mla(self):
    self.mla_w_in = self.din("mla_w_in", [D, 2496])
    self.mla_q_norm = self.din("mla_q_norm", [1, 256])
    self.mla_kv_norm = self.din("mla_kv_norm", [1, 128])
    self.mla_wq_b = self.din("mla_wq_b", [256, 3072])
    self.mla_wq_pesw = self.din("mla_wq_pesw", [256, 1024])
    self.mla_wukT = self.din("mla_wukT", [16, 128, 128])
    self.mla_wkv_b = self.din("mla_wkv_b", [128, 4096])
    self.mla_w_out = self.din("mla_w_out", [1, W, D])
    self.cckv = self.din("cckv", [512, 128])
    self.ckpe = self.din("ckpe", [512, 64])
    self.ropeT_cos = self.din("ropeT_cos", [64, 2048])
    self.ropeT_sin = self.din("ropeT_sin", [64, 2048])
    self.ropeK_cos = self.din("ropeK_cos", [2, 128, 8, 64])
    self.ropeK_sin = self.din("ropeK_sin", [2, 128, 8, 64])
    self.pmask = self.din("pmask", [128, 1024])
    self.szd = self.dscr("szd", [16, 128, NTOK], BF16)
    self.szd_r = [RW("szd%d" % a) for a in range(16)]
    self.nckv = self.dout("nckv", [1024, 128])
    self.nkpe = self.dout("nkpe", [1024, 64])


def _mla_mixer(self, jj, hT, sc, hsc):
    P = self.P
    with ExitStack() as s2:
        qnT = P.sb(s2, "qnT", [128, 2, NTOK], BF16)
        KT = P.sb(s2, "KT", [128, NB_KEYS, 128], BF16)
        PET = P.sb(s2, "PET", [128, NB_KEYS, 128], BF16)
        V = P.sb(s2, "V", [128, NB_KEYS, 130], BF16)
        P.memset('pool', V[:], 1.0, [V])
        win = self.mla_w_in
        with ExitStack() as s3:
            self.set_wbufs(s3, 1)
            qg = P.sb(s3, "qg", [128, 256], F32)
            kg = P.sb(s3, "kg", [128, 128], F32)
            P.dma('sp', qg[:], self.mla_q_norm.ap[0].partition_broadcast(128), [self.mla_q_norm], [qg])
            P.dma('sp', kg[:], self.mla_kv_norm.ap[0].partition_broadcast(128), [self.mla_kv_norm], [kg])
            wqa, wqav = self.load_w(win, 0, 8, 0, 448)
            raw = P.sb(s3, "raw", [128, 8, 448], F32)
            ss = P.sb(s3, "mss", [128, 2, 8], F32)
            rstd = P.sb(s3, "mrstd", [128, 2, 8], F32)
            qn_cm = P.sb(s3, "qn_cm", [128, 8, 256], BF16)
            ckvn = P.sb(s3, "ckvn", [128, 8, 128], F32)
            ckvb = P.sb(s3, "ckvb", [128, 8, 128], BF16)
            kr = P.sb(s3, "kr", [128, 8, 64], F32)
            krt = P.sb(s3, "krt", [128, 8, 64], F32)
            krb = P.sb(s3, "krb", [128, 8, 64], BF16)
            rc = P.sb(s3, "rc", [128, 8, 64], F32)
            rs = P.sb(s3, "rs", [128, 8, 64], F32)
            cx = P.sb(s3, "cx", [128, 4, 128], F32)
            cxb = P.sb(s3, "cxb", [128, 4, 128], BF16)
            cp_ = P.sb(s3, "cp_", [128, 4, 64], F32)
            cpb = P.sb(s3, "cpb", [128, 4, 64], BF16)
            P.dma('sp', cx[:], self.cckv.ap.rearrange("(b p) r -> p b r", p=128), [self.cckv], [cx])
            P.dma('sp', cp_[:], self.ckpe.ap.rearrange("(b p) r -> p b r", p=128), [self.ckpe], [cp_])
            P.cp('dve', cxb[:], cx[:], [cx], [cxb])
            P.cp('dve', cpb[:], cp_[:], [cp_], [cpb])
            P.cp('pool', V[:, 0:4, 0:128], cx[:], [cx], [V])
            ps = P.next_ps()
            pv = ps[:, :].bitcast(BF16).rearrange("p (k n) -> p k n", k=8)
            for b in range(4):
                P.tr(ps, pv[:, b, :], cxb[:, b, :], [cxb])
            P.cp('act', KT[:, 0:4, :], pv[:, 0:4, :], [ps], [KT])
            ps = P.next_ps()
            pv = ps[:, :].bitcast(BF16).rearrange("p (k n) -> p k n", k=8)
            for b in range(4):
                P.tr(ps, pv[0:64, b, :], cpb[:, b, :], [cpb])
            P.cp('act', PET[0:64, 0:4, :], pv[0:64, 0:4, :], [ps], [PET])
            if KSTOP == 'm1a':
                return
            for ti in range(3):
                if ti < 2 and 'D' not in KSKIP:
                    P.dma('sp', rc[:], self.ropeK_cos.ap[ti], [self.ropeK_cos], [rc])
                    P.dma('sp', rs[:], self.ropeK_sin.ap[ti], [self.ropeK_sin], [rs])
                for t in range(8):
                    ps = P.next_ps()
                    for k in range(8):
                        P.mm(ps, ps[:, 0:448], hT[:, k, ti * 1024 + t:(ti + 1) * 1024:8], wqav[:, k, :],
                             k == 0, k == 7, [hT, wqa])
                    P.cp('dve', raw[:, t, :], ps[:, 0:448], [ps], [raw])
                    if 'C' not in KSKIP:
                        P.act(self.junk[:, 0:256], raw[:, t, 0:256], AF.Square, [raw], [self.junk, ss],
                              accum=ss[:, 0, t:t + 1])
                        P.act(self.junk[:, 0:128], raw[:, t, 256:384], AF.Square, [raw], [self.junk, ss],
                              accum=ss[:, 1, t:t + 1])
                if KSTOP == 'm1c':
                    return
                self.rstd_from_ss(T2(ss, ss[:, 0, :]), T2(rstd, rstd[:, 0, :]), 8, 256)
                self.rstd_from_ss(T2(ss, ss[:, 1, :]), T2(rstd, rstd[:, 1, :]), 8, 128)
                if KSTOP == 'm1d':
                    return
                for t in range(8):
                    P.stt(qn_cm[:, t, :], raw[:, t, 0:256], rstd[:, 0, t:t + 1], qg[:], ALU.mult, ALU.mult,
                          [raw, rstd, qg], [qn_cm])
                    P.stt(ckvn[:, t, :], raw[:, t, 256:384], rstd[:, 1, t:t + 1], kg[:], ALU.mult, ALU.mult,
                          [raw, rstd, kg], [ckvn])
                if KSTOP == 'm1e':
                    return
                P.cp('pool', ckvb[:], ckvn[:], [ckvn], [ckvb])
                kb0 = 4 + ti * 8 if ti < 2 else 20
                P.cp('pool', V[:, kb0:kb0 + 8, 0:128], ckvn[:], [ckvn], [V])
                if ti == 2:
                    if 'A' not in KSKIP:
                        P.dma('sp', self.nckv.ap.rearrange("(p t) r -> p t r", t=8), ckvn[:], [ckvn], [self.nckv])
                    P.cp('pool', kr[:], raw[:, :, 384:448], [raw], [kr])
                    if 'A' not in KSKIP:
                        P.dma('sp', self.nkpe.ap.rearrange("(p t) r -> p t r", t=8), kr[:], [kr], [self.nkpe])
                    P.cp('pool', krb[:], kr[:], [kr], [krb])
                elif 'B' in KSKIP:
                    P.cp('pool', krb[:], raw[:, :, 384:448], [raw], [krb])
                else:
                    xv = raw[:, :, 384:448].rearrange("p t (s h i) -> p t s h i", s=2, h=2)
                    for s in range(2):
                        for h in range(2):
                            sl = slice(s * 32 + h * 16, s * 32 + h * 16 + 16)
                            so = slice(s * 32 + (1 - h) * 16, s * 32 + (1 - h) * 16 + 16)
                            P.tt('dve', krt[:, :, sl], raw[:, :, 384 + so.start:384 + so.stop], rs[:, :, sl], ALU.mult,
                                 [raw, rs], [krt])
                    P.tt('dve', kr[:], raw[:, :, 384:448], rc[:], ALU.mult, [raw, rc], [kr])
                    P.tt('dve', krb[:], kr[:], krt[:], ALU.add, [kr, krt], [krb])
                if KSTOP == 'm1f':
                    return
                for t in range(8):
                    ps = P.next_ps()
                    pv = ps[:, :].bitcast(BF16).rearrange("p (k n) -> p k n", k=8)
                    P.tr(ps, pv[:, 0, :], qn_cm[:, t, 0:128], [qn_cm])
                    P.tr(ps, pv[:, 1, :], qn_cm[:, t, 128:256], [qn_cm])
                    P.tr(ps, pv[:, 2, :], ckvb[:, t, :], [ckvb])
                    P.tr(ps, pv[0:64, 3, :], krb[:, t, :], [krb])
                    P.cp('act', qnT[:, :, ti * 1024 + t:(ti + 1) * 1024:8], pv[:, 0:2, :], [ps], [qnT])
                    P.cp('dve', KT[:, kb0 + t, :], pv[:, 2, :], [ps], [KT])
                    P.cp('act', PET[0:64, kb0 + t, :], pv[0:64, 3, :], [ps], [PET])
            if KSTOP == 'm1b':
                return
            refk = P.sb(s3, "refk", [128, 5], F32)
            refp = P.sb(s3, "refp", [128, 5], F32)
            P.cp('dve', refk[:, 0:1], KT[:, 0, 0:1], [KT], [refk])
            P.cp('dve', refp[0:64, 0:1], PET[0:64, 0, 0:1], [PET], [refp])
            for s in range(4):
                P.cp('dve', refk[:, 1 + s:2 + s], KT[:, 20, 32 * s:32 * s + 1], [KT], [refk])
                P.cp('dve', refp[0:64, 1 + s:2 + s], PET[0:64, 20, 32 * s:32 * s + 1], [PET], [refp])
            P.ts('dve', KT[:, 0:20, :], KT[:, 0:20, :], refk[:, 0:1], None, ALU.subtract, None, [KT, refk], [KT])
            P.ts('dve', PET[0:64, 0:20, :], PET[0:64, 0:20, :], refp[0:64, 0:1], None, ALU.subtract, None,
                 [PET, refp], [PET])
            for s in range(4):
                P.ts('dve', KT[:, 20:28, 32 * s:32 * s + 32], KT[:, 20:28, 32 * s:32 * s + 32], refk[:, 1 + s:2 + s],
                     None, ALU.subtract, None, [KT, refk], [KT])
                P.ts('dve', PET[0:64, 20:28, 32 * s:32 * s + 32], PET[0:64, 20:28, 32 * s:32 * s + 32],
                     refp[0:64, 1 + s:2 + s], None, ALU.subtract, None, [PET, refp], [PET])
        if KSTOP == 'm1':
            return
        P.barrier()
        with ExitStack() as s3:
            self.set_wbufs(s3, 3, 1024)
            szb = [P.sb(s3, "szb%d" % i, [128, NTOK], BF16) for i in range(2)]
            for h in range(16):
                wz, wzv = self.load_w(win, 0, 8, 448 + h * 128, 128)
                sb_ = szb[h % 2]
                for pc in range(6):
                    sl = slice(pc * 512, (pc + 1) * 512)
                    psz = P.next_ps()
                    for k in range(8):
                        P.mm(psz, psz[:, :], wzv[:, k, :], hT[:, k, sl], k == 0, k == 7, [wz, hT])
                    P.act(sb_[:, sl], psz[:, :], AF.Silu, [psz], [sb_])
                P.dma('sp', self.szd.ap[h], sb_[:], [sb_], [self.szd_r[h]])
        P.barrier()
        hsc.close()
        if KSTOP == 'm0':
            return
        wqb = P.sb(s2, "wqb", [128, 2, 3072], BF16)
        wsw = P.sb(s2, "wsw", [128, 2, 1024], BF16)
        wuk = P.sb(s2, "wuk", [128, 16, 128], BF16)
        wuv = P.sb(s2, "wuv", [128, 16, 128], BF16)
        for kc in range(2):
            for hf in range(2):
                P.dma('pool', wqb[:, kc, hf * 1536:(hf + 1) * 1536],
                      self.mla_wq_b.ap[kc * 128:(kc + 1) * 128, hf * 1536:(hf + 1) * 1536], [self.mla_wq_b], [wqb])
        P.dma('pool', wsw[:], self.mla_wq_pesw.ap.rearrange("(k p) n -> p k n", p=128), [self.mla_wq_pesw], [wsw])
        P.dma('pool', wuk[:], self.mla_wukT.ap.rearrange("h d r -> d h r"), [self.mla_wukT], [wuk])
        P.dma('pool', wuv[:], self.mla_wkv_b.ap.rearrange("r (h two v) -> r h two v", two=2, v=128)[:, :, 1, :],
              [self.mla_wkv_b], [wuv])
        cosT = P.sb(s2, "cosT", [64, 2048], F32)
        sinT = P.sb(s2, "sinT", [64, 2048], F32)
        P.dma('sp', cosT[:], self.ropeT_cos.ap[:, :], [self.ropeT_cos], [cosT])
        P.dma('sp', sinT[:], self.ropeT_sin.ap[:, :], [self.ropeT_sin], [sinT])
        mask = P.sb(s2, "pmaskb", [128, 1024], BF16)
        P.dma('pool', mask[:], self.pmask.ap[:, :], [self.pmask], [mask])
        qnope = [P.sb(s2, "qnope%d" % i, [128, 512], BF16) for i in range(2)]
        qabs = P.sb(s2, "qabs", [128, NTOK], BF16)
        qpe = P.sb(s2, "qpe", [64, NTOK], BF16)
        rt1 = P.sb(s2, "rt1", [64, 512], F32)
        rt2 = P.sb(s2, "rt2", [64, 512], F32)
        oT = P.sb(s2, "oT", [128, NTOK], BF16)
        PTall = P.sb(s2, "PTall", [128, 20, 512], BF16)
        ptb = [P.sb(s2, "mptb%d" % i, [128, NTOK], BF16) for i in range(2)]
        szs = [P.sb(s2, "msz%d" % i, [128, NTOK], BF16) for i in range(2)]
        rinv = [P.sb(s2, "rinv%d" % i, [128, 1], F32) for i in range(2)]
        On = [P.sb(s2, "On%d" % i, [128, 128], BF16) for i in range(2)]
        if KSTOP == 'm2w':
            return
        for h in range(16 if KSTOP not in ('m2q', 'm2a', 'm2h') else 1):
            sz = szs[h % 2]
            P.dma('sp', sz[:], self.szd.ap[h], [self.szd_r[h]], [sz])
            for pc in range(6):
                sl = slice(pc * 512, (pc + 1) * 512)
                ps = P.next_ps()
                for kc in range(2):
                    P.mm(ps, ps[:, :], wqb[:, kc, h * 192:h * 192 + 128], qnT[:, kc, sl], kc == 0, kc == 1, [wqb, qnT])
                qn = qnope[pc % 2]
                P.cp('act', qn[:], ps[:, :], [ps], [qn])
                ps2 = P.next_ps()
                P.mm(ps2, ps2[:, :], wuk[:, h, :], qn[:], True, True, [wuk, qn])
                P.cp('dve', qabs[:, sl], ps2[:, :], [ps2], [qabs])
                ps3 = P.next_ps()
                for kc in range(2):
                    P.mm(ps3, ps3[0:64, :], wqb[:, kc, h * 192 + 128:h * 192 + 192], qnT[:, kc, sl],
                         kc == 0, kc == 1, [wqb, qnT])
                if pc < 4:
                    ps4 = P.next_ps()
                    for kc in range(2):
                        P.mm(ps4, ps4[0:64, :], wsw[:, kc, h * 64:(h + 1) * 64], qnT[:, kc, sl],
                             kc == 0, kc == 1, [wsw, qnT])
                    P.tt('dve', rt1[:], ps3[0:64, :], cosT[:, sl], ALU.mult, [ps3, cosT], [rt1])
                    P.tt('dve', rt2[:], ps4[0:64, :], sinT[:, sl], ALU.mult, [ps4, sinT], [rt2])
                    P.tt('dve', qpe[:, sl], rt1[:], rt2[:], ALU.add, [rt1, rt2], [qpe])
                else:
                    P.cp('act', qpe[:, sl], ps3[0:64, :], [ps3], [qpe])
            if KSTOP == 'm2q':
                return
            for qp in range(6):
                sl = slice(qp * 512, (qp + 1) * 512)
                kbs = list(range(0, 20)) if qp < 4 else list(range(20, 28))
                for i, kb in enumerate(kbs):
                    ps = P.next_ps()
                    P.mm(ps, ps[:, :], KT[:, kb, :], qabs[:, sl], True, False, [KT, qabs])
                    P.mm(ps, ps[:, :], PET[0:64, kb, :], qpe[0:64, sl], False, True, [PET, qpe])
                    P.act(PTall[:, i, :], ps[:, :], AF.Exp, [ps], [PTall], scale=ATT_SCALE)
                    if qp >= 4:
                        P.tt('dve', PTall[:, i, :], PTall[:, i, :], mask[:, (qp - 4) * 512:(qp - 3) * 512], ALU.mult,
                             [PTall, mask], [PTall])
                pst = P.next_ps()
                ptv = pst[:, :].bitcast(BF16).rearrange("p (k n) -> p k n", k=8)
                for qs in range(4):
                    pso = P.next_ps()
                    for i, kb in enumerate(kbs):
                        P.mm(pso, pso[:, 0:129], PTall[:, i, qs * 128:(qs + 1) * 128], V[:, kb, 0:129],
                             i == 0, i == len(kbs) - 1, [PTall, V])
                    ri = rinv[qs % 2]
                    on = On[qs % 2]
                    P.recip(ri[:], pso[:, 128:129], [pso], [ri])
                    P.ts('dve', on[:], pso[:, 0:128], ri[:, 0:1], None, ALU.mult, None, [pso, ri], [on])
                    P.tr(pst, ptv[:, qs, :], on[:], [on])
                P.cp('act', oT[:, sl], ptv[:, 0:4, :], [pst], [oT])
            if KSTOP == 'm2a':
                return
            pt = ptb[h % 2]
            for pc in range(6):
                sl = slice(pc * 512, (pc + 1) * 512)
                psu = P.next_ps()
                P.mm(psu, psu[:, :], wuv[:, h, :], oT[:, sl], True, True, [wuv, oT])
                P.tt('dve', pt[:, sl], psu[:, :], sz[:, sl], ALU.mult, [psu, sz], [pt])
            P.dma('sp', self.ptd.ap[h], pt[:], [pt], [self.ptd_r[h]])


class T2:
    def __init__(self, base, view):
        self.r = base.r
        self.v = view

    def __getitem__(self, k):
        return self.v[k]


Builder.declare_mla = _declare_mla
Builder.mla_mixer = _mla_mixer


def _rope_tables():
    half = 16
    inv = (10000.0 ** (-np.arange(half, dtype=np.float32) / half)).astype(np.float32)
    tok = np.arange(2048)
    pos = [(tok // 64).astype(np.float32), (tok % 64).astype(np.float32)]
    cos = np.zeros((2048, 64), np.float32)
    sin = np.zeros((2048, 64), np.float32)
    for s in range(2):
        ang = (pos[s][:, None] * inv[None, :]).astype(np.float32)
        c, sn = np.cos(ang).astype(np.float32), np.sin(ang).astype(np.float32)
        cos[:, s * 32:s * 32 + 16] = c
        cos[:, s * 32 + 16:s * 32 + 32] = c
        sin[:, s * 32:s * 32 + 16] = -sn
        sin[:, s * 32 + 16:s * 32 + 32] = sn
    return cos, sin


def mla_shared_inputs(inp):
    f = lambda a: np.ascontiguousarray(np.asarray(a, dtype=np.float32))
    wq = np.asarray(inp["mla_wq_b"][0], np.float32)
    wq3 = wq.reshape(256, 16, 192)
    pe = wq3[:, :, 128:192]
    perm = np.array([s * 32 + (1 - hh) * 16 + i for s in range(2) for hh in range(2) for i in range(16)])
    pesw = pe[:, :, perm].reshape(256, 1024)
    wkv = np.asarray(inp["mla_wkv_b"][0], np.float32)
    wukT = wkv.reshape(128, 16, 2, 128)[:, :, 0, :].transpose(1, 2, 0)
    cos, sin = _rope_tables()
    pm = (np.arange(128)[:, None] // 32 == np.arange(1024)[None, :] // 256).astype(np.float32)
    return {
        "mla_w_in": f(inp["mla_w_in"][0]), "mla_q_norm": f(inp["mla_q_norm"][0]).reshape(1, 256),
        "mla_kv_norm": f(inp["mla_kv_norm"][0]).reshape(1, 128), "mla_wq_b": f(wq), "mla_wq_pesw": f(pesw),
        "mla_wukT": f(wukT), "mla_wkv_b": f(wkv), "mla_w_out": f(inp["mla_w_out"]),
        "ropeT_cos": f(cos.T), "ropeT_sin": f(sin.T),
        "ropeK_cos": f(cos.reshape(2, 128, 8, 64)), "ropeK_sin": f(sin.reshape(2, 128, 8, 64)),
        "pmask": f(pm),
    }


def mla_core_inputs(inp, c):
    f = lambda a: np.ascontiguousarray(np.asarray(a, dtype=np.float32))
    b = c % 2
    return {"cckv": f(inp["cache_ckv"][b, 0]), "ckpe": f(inp["cache_kpe"][b, 0])}


TWO_PI = 6.283185307179586
PI = 3.141592653589793
NCOL = 408
NSEG = 12


def _declare_s5(self):
    n5 = 2
    self.s5_w_in = self.din("s5_w_in", [n5, D, 2 * W])
    self.s5_lamT_re = self.din("s5_lamT_re", [n5, 128, 128])
    self.s5_lamT_im = self.din("s5_lamT_im", [n5, 128, 128])
    self.s5_lstepT = self.din("s5_lstepT", [n5, 128, 128])
    self.s5_bT_re = self.din("s5_bT_re", [n5, 128, 128, 16])
    self.s5_bT_im = self.din("s5_bT_im", [n5, 128, 128, 16])
    self.s5_cT_re = self.din("s5_cT_re", [n5, 128, 128, 16])
    self.s5_cT_im = self.din("s5_cT_im", [n5, 128, 128, 16])
    self.s5_dT = self.din("s5_dT", [n5, 128, 128])
    self.s5_glu_w = self.din("s5_glu_w", [n5, W, W])
    self.s5_glu_bT = self.din("s5_glu_bT", [n5, 128, 16])
    self.s5_w_out = self.din("s5_w_out", [n5, W, D])
    self.s5_st_re = self.din("s5_st_re", [n5, 128, 128])
    self.s5_st_im = self.din("s5_st_im", [n5, 128, 128])
    self.s5_maskf = self.din("s5_maskf", [128, 128])
    self.s5_maskb = self.din("s5_maskb", [128, 128])
    n_s5 = sum(1 for l in self.layers if l % 3 == 0)
    self.n_s5 = n_s5
    self.ns_re = self.dout("ns_re", [4, n_s5, 2, 128, 64])
    self.ns_im = self.dout("ns_im", [4, n_s5, 2, 128, 64])
    self.ud = self.dscr("ud", [16, 128, 3, 8, 128], BF16)
    self.ud_r = [RW("ud%d" % a) for a in range(16)]
    self.yd = self.dscr("yd", [3, 128, 8, W], BF16)
    self.yd_r = [[RW("yd%d_%d" % (a, b)) for b in range(16)] for a in range(3)]
    if not hasattr(self, "szd"):
        self.szd = self.dscr("szd", [16, 128, NTOK], BF16)
        self.szd_r = [RW("szd%d" % a) for a in range(16)]
    self.s5_seen = 0


def _s5_mixer(self, jj, hT, sc, hsc, first, last):
    P = self.P
    slot = self.s5_seen
    self.s5_seen += 1
    win = DTsub(self.s5_w_in, jj)
    with ExitStack() as s3:
        self.set_wbufs(s3, 3, 1024)
        ucm = [P.sb(s3, "ucm%d" % i, [128, 3, 8, 128], BF16) for i in range(2)]
        szb = [P.sb(s3, "szb%d" % i, [128, NTOK], BF16) for i in range(2)]
        for bi in range(16):
            wu, wuv = self.load_w(win, 0, 8, bi * 128, 128)
            uc = ucm[bi % 2]
            for ti in range(3):
                for t4 in range(2):
                    ps = P.next_ps()
                    for tq in range(4):
                        t = t4 * 4 + tq
                        for k in range(8):
                            P.mm(ps, ps[:, tq * 128:(tq + 1) * 128], hT[:, k, ti * 1024 + t:(ti + 1) * 1024:8],
                                 wuv[:, k, :], k == 0, k == 7, [hT, wu])
                    outv = uc[:, ti, :, :].rearrange("p g (t c) -> p t g c", c=16)[:, t4 * 4:t4 * 4 + 4, :, :]
                    inv = ps[:, :].rearrange("p (t g c) -> p t g c", t=4, c=16)
                    P.cp('act' if t4 == 0 else 'dve', outv, inv, [ps], [uc])
            P.dma('sp', self.ud.ap[bi], uc[:], [uc], [self.ud_r[bi]])
            wz, wzv = self.load_w(win, 0, 8, W + bi * 128, 128)
            sb_ = szb[bi % 2]
            for pc in range(6):
                sl = slice(pc * 512, (pc + 1) * 512)
                psz = P.next_ps()
                for k in range(8):
                    P.mm(psz, psz[:, :], wzv[:, k, :], hT[:, k, sl], k == 0, k == 7, [wz, hT])
                P.act(sb_[:, sl], psz[:, :], AF.Silu, [psz], [sb_])
            P.dma('sp', self.szd.ap[bi], sb_[:], [sb_], [self.szd_r[bi]])
    P.barrier()
    hsc.close()
    if KSTOP == 's5a':
        return
    with ExitStack() as s3:
        tabs = self.s5_tables(jj, s3)
        self.s5_core(jj, slot, tabs, s3)
    P.barrier()
    if KSTOP == 's5s':
        return
    self._s5_jj = jj
    self.out_phase(DTsub(self.s5_w_out, jj), self.s5_glu_tile, first, last, sc)


def _cmul(self, eng, o_re, o_im, a_re, a_im, b_re, b_im, t1, t2, reads, writes, neg_im=False):
    P = self.P
    P.tt(eng, t1[0], a_re, b_re, ALU.mult, reads, [t1[1]])
    P.tt(eng, t2[0], a_im, b_im, ALU.mult, reads, [t2[1]])
    P.tt(eng, o_re, t1[0], t2[0], ALU.subtract, [t1[1], t2[1]], writes)
    P.tt(eng, t1[0], a_re, b_im, ALU.mult, reads, [t1[1]])
    P.tt(eng, t2[0], a_im, b_re, ALU.mult, reads, [t2[1]])
    if neg_im:
        P.ts(eng, t1[0], t1[0], -1.0, None, ALU.mult, None, [t1[1]], [t1[1]])
        P.tt(eng, o_im, t1[0], t2[0], ALU.subtract, [t1[1], t2[1]], writes)
    else:
        P.tt(eng, o_im, t1[0], t2[0], ALU.add, [t1[1], t2[1]], writes)


def _s5_tables(self, jj, sc):
    P = self.P
    tb = {}
    mk = lambda n, shape: P.sb(sc, n, shape, F32)
    PWB = [mk("PWBre", [128, 8, 128]), mk("PWBim", [128, 8, 128])]
    PWC = [mk("PWCre", [128, 8, 128]), mk("PWCim", [128, 8, 128])]
    PWT = [mk("PWTre", [128, 8, 128]), mk("PWTim", [128, 8, 128])]
    L8 = [mk("L8re", [128, 128]), mk("L8im", [128, 128])]
    LP = [mk("LPre", [128, 6, 128]), mk("LPim", [128, 6, 128])]
    coef = [mk("coefre", [128, 128]), mk("coefim", [128, 128])]
    with ExitStack() as s2:
        lr = mk2(P, s2, "lr")
        li_ = mk2(P, s2, "li")
        ls = mk2(P, s2, "ls")
        P.dma('sp', lr[:], self.s5_lamT_re.ap[jj], [self.s5_lamT_re], [lr])
        P.dma('sp', li_[:], self.s5_lamT_im.ap[jj], [self.s5_lamT_im], [li_])
        P.dma('sp', ls[:], self.s5_lstepT.ap[jj], [self.s5_lstepT], [ls])
        stp = mk2(P, s2, "stp")
        P.act(stp[:], ls[:], AF.Exp, [ls], [stp])
        a = mk2(P, s2, "a")
        b = mk2(P, s2, "b")
        P.tt('dve', a[:], lr[:], stp[:], ALU.mult, [lr, stp], [a])
        P.tt('dve', b[:], li_[:], stp[:], ALU.mult, [li_, stp], [b])
        mag = mk2(P, s2, "mag")
        P.act(mag[:], a[:], AF.Exp, [a], [mag])
        t1 = mk2(P, s2, "t1")
        t2 = mk2(P, s2, "t2")
        ki = P.sb(s2, "ki", [128, 128], I32)

        def sin_of(dst, src, shift):
            P.ts('dve', t1[:], src[:], shift, 1.0 / TWO_PI, ALU.add, ALU.mult, [src], [t1])
            P.cp('dve', ki[:], t1[:], [t1], [ki])
            P.cp('dve', t2[:], ki[:], [ki], [t2])
            P.stt(t1[:], t2[:], -TWO_PI, src[:], ALU.mult, ALU.add, [t2, src], [t1])
            if shift != 0.0:
                P.ts('dve', t1[:], t1[:], shift, None, ALU.add, None, [t1], [t1])
            P.ts('dve', t2[:], t1[:], PI, None, ALU.is_gt, None, [t1], [t2])
            P.stt(t1[:], t2[:], -TWO_PI, t1[:], ALU.mult, ALU.add, [t2, t1], [t1])
            P.ts('dve', t2[:], t1[:], -PI, None, ALU.is_lt, None, [t1], [t2])
            P.stt(t1[:], t2[:], TWO_PI, t1[:], ALU.mult, ALU.add, [t2, t1], [t1])
            P.ts('dve', t1[:], t1[:], PI, -PI, ALU.min, ALU.max, [t1], [t1])
            P.act(dst[:], t1[:], AF.Sin, [t1], [dst])

        sn = mk2(P, s2, "sn")
        cs = mk2(P, s2, "cs")
        sin_of(sn, b, 0.0)
        sin_of(cs, b, PI / 2)
        PW = [P.sb(s2, "PWre", [128, 9, 128], F32), P.sb(s2, "PWim", [128, 9, 128], F32)]
        NP = [P.sb(s2, "NPre", [128, 8, 128], F32), P.sb(s2, "NPim", [128, 8, 128], F32)]
        P.memset('dve', PW[0][:, 0, :], 1.0, [PW[0]])
        P.memset('dve', PW[1][:, 0, :], 0.0, [PW[1]])
        P.memset('dve', NP[0][:, 0, :], 1.0, [NP[0]])
        P.memset('dve', NP[1][:, 0, :], 0.0, [NP[1]])
        P.tt('dve', PW[0][:, 1, :], mag[:], cs[:], ALU.mult, [mag, cs], [PW[0]])
        P.tt('dve', PW[1][:, 1, :], mag[:], sn[:], ALU.mult, [mag, sn], [PW[1]])
        for e in range(2, 9):
            self.cmul('dve', PW[0][:, e, :], PW[1][:, e, :], PW[0][:, e - 1, :], PW[1][:, e - 1, :],
                      PW[0][:, 1, :], PW[1][:, 1, :], (t1[:], t1), (t2[:], t2), [PW[0], PW[1]], [PW[0], PW[1]])
        den = mk2(P, s2, "den")
        P.tt('dve', t1[:], PW[0][:, 1, :], PW[0][:, 1, :], ALU.mult, [PW[0]], [t1])
        P.tt('dve', t2[:], PW[1][:, 1, :], PW[1][:, 1, :], ALU.mult, [PW[1]], [t2])
        P.tt('dve', den[:], t1[:], t2[:], ALU.add, [t1, t2], [den])
        P.recip(den[:], den[:], [den], [den])
        P.tt('dve', NP[0][:, 1, :], PW[0][:, 1, :], den[:], ALU.mult, [PW[0], den], [NP[0]])
        P.stt(NP[1][:, 1, :], PW[1][:, 1, :], -1.0, den[:], ALU.mult, ALU.mult, [PW[1], den], [NP[1]])
        for e in range(2, 8):
            self.cmul('dve', NP[0][:, e, :], NP[1][:, e, :], NP[0][:, e - 1, :], NP[1][:, e - 1, :],
                      NP[0][:, 1, :], NP[1][:, 1, :], (t1[:], t1), (t2[:], t2), [NP[0], NP[1]], [NP[0], NP[1]])
        nr = mk2(P, s2, "nr")
        P.ts('dve', nr[:], PW[0][:, 1, :], -1.0, None, ALU.add, None, [PW[0]], [nr])
        nre = mk2(P, s2, "nre")
        nim = mk2(P, s2, "nim")
        P.tt('dve', t1[:], nr[:], lr[:], ALU.mult, [nr, lr], [t1])
        P.tt('dve', t2[:], PW[1][:, 1, :], li_[:], ALU.mult, [PW[1], li_], [t2])
        P.tt('dve', nre[:], t1[:], t2[:], ALU.add, [t1, t2], [nre])
        P.tt('dve', t1[:], PW[1][:, 1, :], lr[:], ALU.mult, [PW[1], lr], [t1])
        P.tt('dve', t2[:], nr[:], li_[:], ALU.mult, [nr, li_], [t2])
        P.tt('dve', nim[:], t1[:], t2[:], ALU.subtract, [t1, t2], [nim])
        P.tt('dve', t1[:], lr[:], lr[:], ALU.mult, [lr], [t1])
        P.tt('dve', t2[:], li_[:], li_[:], ALU.mult, [li_], [t2])
        P.tt('dve', den[:], t1[:], t2[:], ALU.add, [t1, t2], [den])
        P.recip(den[:], den[:], [den], [den])
        P.tt('dve', coef[0][:], nre[:], den[:], ALU.mult, [nre, den], [coef[0]])
        P.tt('dve', coef[1][:], nim[:], den[:], ALU.mult, [nim, den], [coef[1]])
        for c in range(2):
            for j in range(8):
                P.cp('pool', PWB[c][0:64, j, :], PW[c][0:64, 7 - j, :], [PW[c]], [PWB[c]])
                P.cp('pool', PWB[c][64:128, j, :], PW[c][64:128, j, :], [PW[c]], [PWB[c]])
                P.cp('pool', PWC[c][0:64, j, :], PW[c][0:64, j + 1, :], [PW[c]], [PWC[c]])
                P.cp('pool', PWC[c][64:128, j, :], PW[c][64:128, 8 - j, :], [PW[c]], [PWC[c]])
                P.cp('pool', PWT[c][0:64, j, :], NP[c][0:64, 7 - j, :], [NP[c]], [PWT[c]])
                P.cp('pool', PWT[c][64:128, j, :], NP[c][64:128, j, :], [NP[c]], [PWT[c]])
            P.cp('pool', L8[c][:], PW[c][:, 8, :], [PW[c]], [L8[c]])
            P.cp('dve', LP[c][:, 0, :], PW[c][:, 8, :], [PW[c]], [LP[c]])
        for k in range(1, 6):
            self.cmul('dve', LP[0][:, k, :], LP[1][:, k, :], LP[0][:, k - 1, :], LP[1][:, k - 1, :],
                      LP[0][:, k - 1, :], LP[1][:, k - 1, :], (t1[:], t1), (t2[:], t2), [LP[0], LP[1]], [LP[0], LP[1]])
    P.barrier()
    for nm, tt_ in (("PWBre", PWB[0]), ("PWBim", PWB[1]), ("PWCre", PWC[0]), ("PWCim", PWC[1]), ("PWTre", PWT[0]),
                    ("PWTim", PWT[1])):
        self.dbg("%s_%d" % (nm, jj), tt_[:], [128, 8, 128])
    for nm, tt_ in (("L8re", L8[0]), ("L8im", L8[1]), ("coefre", coef[0]), ("coefim", coef[1])):
        self.dbg("%s_%d" % (nm, jj), tt_[:], [128, 128])
    tb.update(PWB=PWB, PWC=PWC, PWT=PWT, L8=L8, coef=coef, LP=LP)
    return tb


def mk2(P, sc, name):
    return P.sb(sc, name, [128, 128], F32)


Builder.declare_s5 = _declare_s5
Builder.s5_mixer = _s5_mixer
Builder.cmul = _cmul
Builder.s5_tables = _s5_tables


def _s5_core(self, jj, slot, tb, sc):
    P = self.P
    PWB, PWC, PWT, L8, coef = tb["PWB"], tb["PWC"], tb["PWT"], tb["L8"], tb["coef"]
    G = 32
    mk = lambda n, shape, dt=F32: P.sb(sc, n, shape, dt)
    maskf = mk("maskf", [128, 128])
    maskb = mk("maskb", [128, 128])
    dT = mk("dT", [128, 128])
    P.dma('sp', maskf[:], self.s5_maskf.ap[:, :], [self.s5_maskf], [maskf])
    P.dma('sp', maskb[:], self.s5_maskb.ap[:, :], [self.s5_maskb], [maskb])
    P.dma('sp', dT[:], self.s5_dT.ap[jj], [self.s5_dT], [dT])
    stin = [mk("stin_re", [128, 128]), mk("stin_im", [128, 128])]
    P.dma('sp', stin[0][:], self.s5_st_re.ap[jj], [self.s5_st_re], [stin[0]])
    P.dma('sp', stin[1][:], self.s5_st_im.ap[jj], [self.s5_st_im], [stin[1]])
    fin = mk("fin", [128, 2, 4, 128])
    Bt = [mk("Bre", [128, G, 16]), mk("Bim", [128, G, 16])]
    Ct = [mk("Cre", [128, G, 16]), mk("Cim", [128, G, 16])]
    BB = [mk("BBre", [128, G, 16]), mk("BBim", [128, G, 16])]
    U8 = mk("U8", [128, G, NCOL], BF16)
    P.memset('pool', U8[:], 0.0, [U8])
    EX = mk("EX", [128, 2, G, NCOL], BF16)
    STREAMS = [('dve', slice(0, 16)), ('dve', slice(16, 32))]
    NS = len(STREAMS)
    g2s = [next(i for i, (_, sl_) in enumerate(STREAMS) if sl_.start <= g < sl_.stop) for g in range(G)]
    EXg = [TV(EX, "EXg%d" % i) for i in range(NS)]
    P.memset('pool', EX[:], 0.0, EXg)
    W2 = [mk("W2re", [128, G, 128], BF16), mk("W2imn", [128, G, 128], BF16)]
    Toep = mk("Toep", [128, G, 128], BF16)
    ucm = mk("s_ucm", [128, 3, 8, 128], BF16)
    BP = [mk("BPre", [128, 8, 128], BF16), mk("BPim", [128, 8, 128], BF16)]
    CPT = [mk("CPTre", [128, 8, 128], BF16), mk("CPTimn", [128, 8, 128], BF16)]
    t1 = mk("s_t1", [128, 8, 128])
    t2 = mk("s_t2", [128, 8, 128])
    W1 = mk("W1", [128, 8, 2, 128], BF16)
    tz1 = mk("tz1", [128, 4, 128])
    tz2 = mk("tz2", [128, 4, 128])
    Ybf = [mk("Ybf%d" % i, [128, 384], BF16) for i in range(2)]
    ycm = ucm
    stM = mk("stM", [128, 2, G, NSEG])
    cA = mk("cA", [128, 2, G, NSEG])
    cB = mk("cB", [128, 2, G, NSEG])
    CR = mk("CR", [128, 2, G, 8])
    TS = mk("TS", [128, 2, 32, G])
    L1 = mk("L1", [128, 2, G])
    L2 = mk("L2", [128, 2, G])
    L1c = mk("L1c", [128, 2, G])
    L2c = mk("L2c", [128, 2, G])
    ch1, ch2 = t1, t2
    if os.environ.get("KDBG"):
        print("S5 core sbuf remaining", self.nc.sbuf_bytes_remaining)
    stMg = [TV(stM, "stMg%d" % i) for i in range(NS)]
    cAg = [TV(cA, "cAg%d" % i) for i in range(NS)]
    cBg = [TV(cB, "cBg%d" % i) for i in range(NS)]
    t1k = [TV(t1, "t1k0"), TV(t1, "t1k1")]
    t2k = [TV(t2, "t2k0"), TV(t2, "t2k1")]
    bsrc = [DTsub(self.s5_bT_re, jj), DTsub(self.s5_bT_im, jj)]
    csrc = [DTsub(self.s5_cT_re, jj), DTsub(self.s5_cT_im, jj)]
    halves = [('dve', slice(0, 64)), (os.environ.get('KHALF', 'pool'), slice(64, 128))]
    for bt in range(4):
        g0 = bt * G
        for c in range(2):
            P.dma('sp', Bt[c][:], bsrc[c].ap[:, g0:g0 + G, :], [bsrc[c]], [Bt[c]])
            P.dma('sp', Ct[c][:], csrc[c].ap[:, g0:g0 + G, :], [csrc[c]], [Ct[c]])
        cf = [coef[c][:, g0:g0 + G].unsqueeze(2).to_broadcast([128, G, 16]) for c in range(2)]
        tA = (t1[:, 0:4, :].rearrange("p a (b c) -> p (a b) c", c=16), t1)
        tB = (t2[:, 0:4, :].rearrange("p a (b c) -> p (a b) c", c=16), t2)
        self.cmul('pool', BB[0][:], BB[1][:], cf[0], cf[1], Bt[0][:], Bt[1][:], tA, tB, [coef[0], coef[1], Bt[0], Bt[1]],
                  [BB[0], BB[1]])
        P.cp('dve', L1[:, 0, :], L8[0][:, g0:g0 + G], [L8[0]], [L1])
        P.cp('dve', L1[:, 1, :], L8[0][:, g0:g0 + G], [L8[0]], [L1])
        P.ts('dve', L2[:, 0, :], L8[1][:, g0:g0 + G], -1.0, None, ALU.mult, None, [L8[1]], [L2])
        P.cp('dve', L2[:, 1, :], L8[1][:, g0:g0 + G], [L8[1]], [L2])
        for bl in range(4):
            bi = bt * 4 + bl
            gb = g0 + bl * 8
            gl = bl * 8
            eng = 'dve' if bl % 2 == 0 else 'pool'
            P.dma('sp', ucm[:], self.ud.ap[bi], [self.ud_r[bi]], [ucm])

            def bc_gc(x):
                return x.unsqueeze(2).to_broadcast([128, 8, 8, 16])

            def bc_gj(x):
                return x.rearrange("p j g -> p g j").unsqueeze(3).to_broadcast([128, 8, 8, 16])

            v4 = lambda x: x.rearrange("p g (j c) -> p g j c", c=16)
            tt1 = (v4(t1[:]), t1)
            tt2 = (v4(t2[:]), t2)
            pwb = [bc_gj(PWB[c][:, :, gb:gb + 8]) for c in range(2)]
            pwc = [bc_gj(PWC[c][:, :, gb:gb + 8]) for c in range(2)]
            pwt = [bc_gj(PWT[c][:, :, gb:gb + 8]) for c in range(2)]
            bbv = [bc_gc(BB[c][:, gl:gl + 8, :]) for c in range(2)]
            ccv = [bc_gc(Ct[c][:, gl:gl + 8, :]) for c in range(2)]
            rd = [BB[0], BB[1], Ct[0], Ct[1], PWB[0], PWB[1], PWC[0], PWC[1], PWT[0], PWT[1]]
            v4b = lambda x: x.rearrange("p a b -> p (a b)").rearrange("p (g j c) -> p g j c", g=8, j=8)
            tt3 = (v4b(TS[:, 0]), TS)
            tt4 = (v4b(TS[:, 1]), TS)
            self.cmul('dve', v4(BP[0][:]), v4(BP[1][:]), bbv[0], bbv[1], pwb[0], pwb[1], tt1, tt2, rd, [BP[0], BP[1]])
            self.cmul('pool', v4(W2[0][:, gl:gl + 8, :]), v4(W2[1][:, gl:gl + 8, :]), ccv[0], ccv[1], pwc[0], pwc[1],
                      tt3, tt4, rd, [W2[0], W2[1]], neg_im=True)
            self.cmul('dve', v4(CPT[0][:]), v4(CPT[1][:]), ccv[0], ccv[1], pwt[0], pwt[1], tt1, tt2, rd,
                      [CPT[0], CPT[1]], neg_im=True)
            for g in range(8):
                if g % 4 == 0:
                    ps = P.next_ps()
                    pv = ps[:, :].bitcast(BF16).rearrange("p (k n) -> p k n", k=8)
                P.tr(ps, pv[:, (g % 4) * 2, :], BP[0][:, g, :], [BP[0]])
                P.tr(ps, pv[:, (g % 4) * 2 + 1, :], BP[1][:, g, :], [BP[1]])
                if g % 4 == 3:
                    P.cp('act', W1[:, g - 3:g + 1, :, :].rearrange("p g r n -> p (g r) n"), pv, [ps], [W1])
            for q in range(2):
                psf = P.next_ps()
                psb = P.next_ps()
                for gi in range(4):
                    g = q * 4 + gi
                    for (pp, sl) in ((psf, slice(0, 64)), (psb, slice(64, 128))):
                        P.mm(pp, pp[:, gi * 128:(gi + 1) * 128], BP[0][sl, g, :], CPT[0][sl, g, :], True, False,
                             [BP[0], CPT[0]])
                        P.mm(pp, pp[:, gi * 128:(gi + 1) * 128], BP[1][sl, g, :], CPT[1][sl, g, :], False, True,
                             [BP[1], CPT[1]])
                mfb = maskf[:].unsqueeze(1).to_broadcast([128, 4, 128])
                mbb = maskb[:].unsqueeze(1).to_broadcast([128, 4, 128])
                idb = self.identf[:].unsqueeze(1).to_broadcast([128, 4, 128])
                dd = dT[:, gb + q * 4:gb + q * 4 + 4].unsqueeze(2).to_broadcast([128, 4, 128])
                p4 = lambda x: x[:, :].rearrange("p (a b) -> p a b", a=4)
                P.tt('dve', tz1[:], p4(psf), mfb, ALU.mult, [psf, maskf], [tz1])
                P.tt('dve', tz2[:], p4(psb), mbb, ALU.mult, [psb, maskb], [tz2])
                P.tt('pool', tz1[:], tz1[:], tz2[:], ALU.add, [tz1, tz2], [tz1])
                P.tt('pool', tz2[:], idb, dd, ALU.mult, [self.identf, dT], [tz2])
                P.tt('pool', Toep[:, gl + q * 4:gl + q * 4 + 4, :], tz1[:], tz2[:], ALU.add, [tz1, tz2], [Toep])
            for g in range(8):
                ps = P.next_ps()
                pv = ps[:, :].bitcast(BF16).rearrange("p (k n) -> p k n", k=8)
                for ti in range(3):
                    P.tr(ps, pv[:, ti, :], ucm[:, ti, g, :], [ucm])
                P.cp('act', U8[:, gl + g, :].rearrange("p (q l) -> p q l", l=34)[:, :, 0:32],
                     pv[:, 0:3, :].rearrange("p a (q l) -> p (a q) l", l=32), [ps], [U8])
                for r in range(2):
                    px = P.next_ps()
                    P.mm(px, px[:, 0:406], W1[:, g, r, :], U8[:, gl + g, 0:406], True, True, [W1, U8])
                    P.cp('act', EX[0:64, r, gl + g, 2:408], px[0:64, 0:406], [px], [EXg[g2s[gl + g]]])
                    P.cp('act', EX[64:128, r, gl + g, 0:406], px[64:128, 0:406], [px], [EXg[g2s[gl + g]]])
        lp = tb["LP"]
        hv = [(slice(0, 64), True), (slice(64, 128), False)]
        GH = [slice(0, 16), slice(16, 32)]
        for sl, fwd in hv:
            pos1 = 0 if fwd else 31
            for r in range(2):
                P.cp('dve', TS[sl, r, pos1, :], lp[r][sl, 0, g0:g0 + G], [lp[r]], [TS])
            m = 1
            for k in range(5):
                src = slice(0, m) if fwd else slice(32 - m, 32)
                dst = slice(m, 2 * m) if fwd else slice(32 - 2 * m, 32 - m)
                lk = [lp[r][sl, k, g0:g0 + G].unsqueeze(1).to_broadcast([64, m, G]) for r in range(2)]
                tv = lambda t_: t_[sl, 0:4, :].rearrange("p a (b c) -> p (a b) c", c=32)[:, 0:m, :]
                self.cmul('dve', TS[sl, 0, dst, :], TS[sl, 1, dst, :], TS[sl, 0, src, :], TS[sl, 1, src, :],
                          lk[0], lk[1], (tv(t1), t1), (tv(t2), t2), [TS, lp[0], lp[1]], [TS])
                m *= 2
        for r in range(2):
            P.cp('dve', L1c[:, r, :], lp[0][:, 5, g0:g0 + G], [lp[0]], [L1c])
        P.ts('dve', L2c[:, 0, :], lp[1][:, 5, g0:g0 + G], -1.0, None, ALU.mult, None, [lp[1]], [L2c])
        P.cp('dve', L2c[:, 1, :], lp[1][:, 5, g0:g0 + G], [lp[1]], [L2c])
        P.memset('dve', stM[:], 0.0, stMg)
        for sl, fwd in hv:
            q0 = 0 if fwd else 7
            for r in range(2):
                P.cp('dve', stM[sl, r, :, q0], stin[r][sl, g0:g0 + G], [stin[r]], stMg)
                P.cp('dve', EX[sl, r, :, 1 if fwd else 34 * 7 + 32], stin[r][sl, g0:g0 + G], [stin[r]], EXg)
        for i in range(32 if 'L' not in KSKIP else 0):
            for gh, (eng, gsl) in enumerate(STREAMS):
                ng = gsl.stop - gsl.start
                l1 = L1[:, :, gsl].unsqueeze(3).to_broadcast([128, 2, ng, NSEG])
                P.tt(eng, cA[:, :, gsl, :], stM[:, :, gsl, :], l1, ALU.mult, [stMg[gh], L1], [cAg[gh]])
            for gh, (eng, gsl) in enumerate(STREAMS):
                ng = gsl.stop - gsl.start
                l20 = L2[:, 0, gsl].unsqueeze(2).to_broadcast([128, ng, NSEG])
                P.tt(eng, cB[:, 0, gsl, :], stM[:, 1, gsl, :], l20, ALU.mult, [stMg[gh], L2], [cBg[gh]])
            for gh, (eng, gsl) in enumerate(STREAMS):
                ng = gsl.stop - gsl.start
                l21 = L2[:, 1, gsl].unsqueeze(2).to_broadcast([128, ng, NSEG])
                P.tt(eng, cB[:, 1, gsl, :], stM[:, 0, gsl, :], l21, ALU.mult, [stMg[gh], L2], [cBg[gh]])
            for gh, (eng, gsl) in enumerate(STREAMS):
                P.tt(eng, cA[:, :, gsl, :], cA[:, :, gsl, :], cB[:, :, gsl, :], ALU.add, [cAg[gh], cBg[gh]],
                     [cAg[gh]])
            for sl, fwd in hv:
                c0 = 2 + i if fwd else 31 - i
                for gh, (eng, gsl) in enumerate(STREAMS):
                    exv = EX[sl, :, gsl, c0:NCOL:34]
                    P.tt(eng, stM[sl, :, gsl, :], cA[sl, :, gsl, :], exv, ALU.add, [cAg[gh], EXg[gh]], [stMg[gh]])
                    P.cp('act', exv, stM[sl, :, gsl, :], [stMg[gh]], [EXg[gh]])
        P.cp('dve', fin[:, :, :, g0:g0 + G], stM[:, :, :, 8:12].rearrange("p r g s -> p r s g"), stMg, [fin])
        P.memset('dve', CR[:], 0.0, [CR])
        for sl, fwd in hv:
            order = list(range(1, 8)) if fwd else list(range(6, -1, -1))
            for n_, k in enumerate(order):
                prv = k - 1 if fwd else k + 1
                if n_ == 0:
                    P.cp('dve', CR[sl, :, :, k], stM[sl, :, :, prv], stMg, [CR])
                    continue
                a_ = cA[sl, :, :, 0]
                P.tt('dve', a_, CR[sl, :, :, prv], L1c[sl], ALU.mult, [CR, L1c], cAg)
                P.tt('dve', cB[sl, 0, :, 0], CR[sl, 1, :, prv], L2c[sl, 0, :], ALU.mult, [CR, L2c], cBg)
                P.tt('dve', cB[sl, 1, :, 0], CR[sl, 0, :, prv], L2c[sl, 1, :], ALU.mult, [CR, L2c], cBg)
                P.tt('dve', a_, a_, cB[sl, :, :, 0], ALU.add, cAg + cBg, cAg)
                P.tt('dve', CR[sl, :, :, k], a_, stM[sl, :, :, prv], ALU.add, cAg + stMg, [CR])
            if fwd:
                P.cp('act', EX[sl, :, :, 35:35 + 34 * 7:34], CR[sl, :, :, 1:8], [CR], EXg)
            else:
                P.cp('act', EX[sl, :, :, 32:32 + 34 * 7:34], CR[sl, :, :, 0:7], [CR], EXg)
        for n_, gs in enumerate(range(0, G if 'F' not in KSKIP else 0, 2)):
            ks = n_ % 2
            tvv = lambda t_: t_[:, 4 * ks:4 * ks + 4, :].rearrange("p a b -> p (a b)").rearrange(
                "p (g q i) -> p g q i", g=2, q=8)
            ta, tbb = tvv(t1), tvv(t2)
            tar, tbr = t1k[ks], t2k[ks]
            exg = EXg[g2s[gs]]
            cr = [CR[:, r, gs:gs + 2, :].unsqueeze(3).to_broadcast([128, 2, 8, 32]) for r in range(2)]
            ts_ = [TS[:, r, :, gs:gs + 2].rearrange("p i g -> p g i").unsqueeze(2).to_broadcast([128, 2, 8, 32])
                   for r in range(2)]
            for r_out in range(2):
                if r_out == 0:
                    P.tt('dve', ta, cr[0], ts_[0], ALU.mult, [CR, TS], [tar])
                    P.tt('dve', tbb, cr[1], ts_[1], ALU.mult, [CR, TS], [tbr])
                    P.tt('dve', ta, ta, tbb, ALU.subtract, [tar, tbr], [tar])
                else:
                    P.tt('dve', ta, cr[0], ts_[1], ALU.mult, [CR, TS], [tar])
                    P.tt('dve', tbb, cr[1], ts_[0], ALU.mult, [CR, TS], [tbr])
                    P.tt('dve', ta, ta, tbb, ALU.add, [tar, tbr], [tar])
                for sl, fwd in hv:
                    off = 2 if fwd else 0
                    exs = EX[sl, r_out, gs:gs + 2, 0:272].rearrange("p g (q l) -> p g q l", l=34)[:, :, :, off:off + 32]
                    lastw = [exg] if gs + 2 < G else [exg, t1, t2]
                    P.tt('dve' if (n_ % 4 == 3 and gs + 2 < G) else 'pool', exs, exs, ta[sl], ALU.add, [tar, exg], lastw)
        for bl in range(4):
            bi = bt * 4 + bl
            gl = bl * 8
            for g in range(8):
                py = P.next_ps()
                P.mm(py, py[:, 0:406], Toep[:, gl + g, :], U8[:, gl + g, 0:406], True, False, [Toep, U8])
                P.mm(py, py[:, 0:406], W2[0][:, gl + g, :], EX[:, 0, gl + g, 1:407], False, False,
                     [W2[0]] + EXg)
                P.mm(py, py[:, 0:406], W2[1][:, gl + g, :], EX[:, 1, gl + g, 1:407], False, True,
                     [W2[1]] + EXg)
                yb = Ybf[g % 2]
                P.cp('act', yb[:, :].rearrange("p (q l) -> p q l", l=32),
                     py[:, 0:408].rearrange("p (q l) -> p q l", l=34)[:, :, 0:32], [py], [yb])
                ps = P.next_ps()
                pv = ps[:, :].bitcast(BF16).rearrange("p (k n) -> p k n", k=8)
                for ti in range(3):
                    P.tr(ps, pv[:, ti, :], yb[:, ti * 128:(ti + 1) * 128], [yb])
                P.act(ycm[:, :, :, g * 16:(g + 1) * 16], pv[:, 0:3, :].rearrange("p a (t c) -> p a t c", c=16),
                      AF.Gelu_apprx_tanh, [ps], [ycm])
            for ti in range(3):
                P.dma('sp', self.yd.ap[ti][:, :, bi * 128:(bi + 1) * 128], ycm[:, ti, :, :], [ycm], [self.yd_r[ti][bi]])
    fo = mk("fo", [128, 128])
    for r, dst in ((0, self.ns_re), (1, self.ns_im)):
        for s in range(4):
            ps = P.next_ps()
            P.tr(ps, ps[:, 0:128], fin[:, r, s, :], [fin], ident=self.identf)
            P.cp('act', fo[:], ps[:, 0:128], [ps], [fo])
            P.dma('sp', dst.ap[s, slot].rearrange("d g p -> g d p"), fo[:].rearrange("g (d p) -> g d p", d=2),
                  [fo], [dst])


def _s5_glu_tile(self, ti, sc):
    P = self.P
    jj = self._s5_jj
    if getattr(self, "_g_scope", None) is not sc:
        self._g_scope = sc
        self._g_yT = P.sb(sc, "g_yT", [128, 16, 1024], BF16)
        self._g_PT = P.sb(sc, "g_PT", [128, 16, 1024], BF16)
        self._g_yc = [P.sb(sc, "g_yc%d" % i, [128, W], BF16) for i in range(2)]
        self._g_sz = [P.sb(sc, "g_sz%d" % i, [128, 1024], BF16) for i in range(2)]
        self._g_sig = [P.sb(sc, "g_sig%d" % i, [128, 512], BF16) for i in range(2)]
        self._g_tmp = [P.sb(sc, "g_tmp%d" % i, [128, 512], BF16) for i in range(2)]
        self._g_gb = P.sb(sc, "g_gb", [128, 16], F32)
        self._g_w = [P.sb(sc, "g_w%d" % i, [128, 16, 128], BF16) for i in range(3)]
        P.dma('sp', self._g_gb[:], self.s5_glu_bT.ap[jj], [self.s5_glu_bT], [self._g_gb])
    yT, PT = self._g_yT, self._g_PT
    for t in range(8):
        yc = self._g_yc[t % 2]
        P.dma('sp', yc[:], self.yd.ap[ti][:, t, :], self.yd_r[ti], [yc])
        for hf in range(2):
            ps = P.next_ps()
            pv = ps[:, :].bitcast(BF16).rearrange("p (k n) -> p k n", k=8)
            for b in range(8):
                P.tr(ps, pv[:, b, :], yc[:, (hf * 8 + b) * 128:(hf * 8 + b + 1) * 128], [yc])
            P.cp('act' if hf == 0 else 'dve', yT[:, hf * 8:hf * 8 + 8, t:1024:8], pv, [ps], [yT])
    gsrc = DTsub(self.s5_glu_w, jj)
    for d in range(16):
        gw = self._g_w[d % 3]
        for q in range(2):
            P.dma('pool', gw[:, q * 8:(q + 1) * 8, :],
                  gsrc.ap[q * 1024:(q + 1) * 1024, d * 128:(d + 1) * 128].rearrange("(k p) n -> p k n", p=128),
                  [gsrc], [gw])
        sz = self._g_sz[d % 2]
        P.dma('sp', sz[:], self.szd.ap[d][:, ti * 1024:(ti + 1) * 1024], [self.szd_r[d]], [sz])
        for pc in range(2):
            sl = slice(pc * 512, (pc + 1) * 512)
            ps = P.next_ps()
            for k in range(16):
                P.mm(ps, ps[:, :], gw[:, k, :], yT[:, k, sl], k == 0, k == 15, [gw, yT])
            sg = self._g_sig[pc]
            P.act(sg[:], ps[:, :], AF.Sigmoid, [ps, self._g_gb], [sg], bias=self._g_gb[:, d:d + 1])
            tm = self._g_tmp[pc]
            P.tt('dve', tm[:], yT[:, d, sl], sg[:], ALU.mult, [yT, sg], [tm])
            P.tt('dve', PT[:, d, sl], tm[:], sz[:, sl], ALU.mult, [tm, sz], [PT])
    return PT


Builder.s5_core = _s5_core
Builder.s5_glu_tile = _s5_glu_tile


def s5_shared_inputs(inp):
    f = lambda a: np.ascontiguousarray(np.asarray(a, dtype=np.float32))
    lam_re = np.asarray(inp["s5_lam_re"], np.float32)
    n = lam_re.shape[0]
    tr = lambda a: np.asarray(a, np.float32).transpose(0, 1, 3, 2).reshape(n, 128, 128)
    ls = np.asarray(inp["s5_log_step"], np.float32)
    lstep = np.broadcast_to(ls[:, :, None, :], (n, 2, 64, 128)).reshape(n, 128, 128)
    b_re = np.asarray(inp["s5_b_re"], np.float32)
    bt = lambda a: np.asarray(a, np.float32).transpose(0, 1, 3, 2, 4).reshape(n, 128, 128, 16)
    ct = lambda a: np.asarray(a, np.float32).transpose(0, 1, 4, 2, 3).reshape(n, 128, 128, 16)
    d = np.asarray(inp["s5_d"], np.float32).reshape(n, 128, 16)
    dT = np.broadcast_to(d.transpose(0, 2, 1)[:, None, :, :], (n, 8, 16, 128)).reshape(n, 128, 128)
    jc = np.arange(128) // 16
    maskf = (jc[None, :] >= jc[:, None]).astype(np.float32)
    maskb = (jc[None, :] <= jc[:, None]).astype(np.float32)
    return {
        "s5_w_in": f(inp["s5_w_in"]), "s5_lamT_re": f(tr(inp["s5_lam_re"])), "s5_lamT_im": f(tr(inp["s5_lam_im"])),
        "s5_lstepT": f(lstep), "s5_bT_re": f(bt(inp["s5_b_re"])), "s5_bT_im": f(bt(inp["s5_b_im"])),
        "s5_cT_re": f(ct(inp["s5_c_re"])), "s5_cT_im": f(ct(inp["s5_c_im"])), "s5_dT": f(dT),
        "s5_glu_w": f(inp["s5_glu_w"]),
        "s5_glu_bT": f(np.asarray(inp["s5_glu_b"], np.float32).reshape(n, 16, 128).transpose(0, 2, 1)),
        "s5_w_out": f(inp["s5_w_out"]), "s5_maskf": f(maskf), "s5_maskb": f(maskb),
    }


def s5_core_inputs(inp, c):
    f = lambda a: np.ascontiguousarray(np.asarray(a, dtype=np.float32))
    b = c % 2
    sr = np.asarray(inp["state_s5_re"], np.float32)[b]
    si = np.asarray(inp["state_s5_im"], np.float32)[b]
    n = sr.shape[0]
    tr = lambda a: a.transpose(0, 1, 3, 2).reshape(n, 128, 128)
    return {"s5_st_re": f(tr(sr)), "s5_st_im": f(tr(si))}
```

```python
import os
import numpy as np
from contextlib import ExitStack
import concourse.bass as bass
import concourse.mybir as mybir
from concourse.bass_utils import run_bass_kernel_spmd

F32 = mybir.dt.float32
BF16 = mybir.dt.bfloat16
I32 = mybir.dt.int32
AF = mybir.ActivationFunctionType
ALU = mybir.AluOpType
AX = mybir.AxisListType

ENGS = ['pe', 'act', 'dve', 'pool', 'sp']


class Res:
    __slots__ = ('name', 'lw', 'rs')

    def __init__(self, name):
        self.name = name
        self.lw = None
        self.rs = {}


class Sched:
    def __init__(self, nc, stack, n_dma=24):
        self.nc = nc
        self.sem = {e: stack.enter_context(nc.semaphore('s_' + e)) for e in ENGS}
        self.cnt = {e: 0 for e in ENGS}
        self.ops = {e: [] for e in ENGS}
        self.waited = {e: {} for e in ENGS}
        self.dsem = [stack.enter_context(nc.semaphore('d%d' % i)) for i in range(n_dma)]
        self.dval = [0] * n_dma
        half = n_dma // 2
        self.dpool = {'pool': list(range(0, half)), 'sp': list(range(half, n_dma)), 'act': list(range(half, n_dma))}
        self.dnext = {'pool': 0, 'sp': 0, 'act': 0}
        self.n_ops = 0

    def _wait(self, e, stamp):
        key, val = stamp
        if self.waited[e].get(key, 0) >= val:
            return
        self.waited[e][key] = val
        sem = self.sem[key[1]] if key[0] == 'e' else self.dsem[key[1]]
        self.ops[e].append(lambda eng, sem=sem, val=val: eng.wait_ge(sem, val))

    def _deps(self, e, reads, writes):
        deps = []
        for r in reads:
            if r.lw is not None:
                deps.append(r.lw)
        for w in writes:
            if w.lw is not None:
                deps.append(w.lw)
            deps.extend(w.rs.values())
        for d in deps:
            if e == 'pe' and d[0] == ('e', 'pe'):
                continue
            self._wait(e, d)

    def _update(self, stamp, reads, writes):
        for r in reads:
            r.rs[stamp[0]] = stamp
        for w in writes:
            w.lw = stamp
            w.rs = {}

    def op(self, e, fn, reads=(), writes=()):
        self._deps(e, reads, writes)
        self.cnt[e] += 1
        sem = self.sem[e]
        self.ops[e].append(lambda eng, fn=fn, sem=sem: fn(eng).then_inc(sem, 1))
        self._update((('e', e), self.cnt[e]), reads, writes)
        self.n_ops += 1

    def dma(self, e, fn, reads=(), writes=()):
        self._deps(e, reads, writes)
        lst = self.dpool[e]
        i = lst[self.dnext[e] % len(lst)]
        self.dnext[e] += 1
        if self.dval[i] > 0:
            self._wait(e, (('d', i), self.dval[i]))
        self.dval[i] += 16
        sem = self.dsem[i]
        self.ops[e].append(lambda eng, fn=fn, sem=sem: fn(eng).then_inc(sem, 16))
        self._update((('d', i), self.dval[i]), reads, writes)
        self.n_ops += 1

    def finish(self):
        for i, v in enumerate(self.dval):
            if v > 0:
                self._wait('sp', (('d', i), v))
        for e in ENGS:
            if e != 'sp' and self.cnt[e] > 0:
                self._wait('sp', (('e', e), self.cnt[e]))

    def replay(self):
        nc = self.nc
        with nc.Block() as block:
            @block.tensor
            def _(eng):
                for f in self.ops['pe']:
                    f(eng)

            @block.scalar
            def _(eng):
                for f in self.ops['act']:
                    f(eng)

            @block.vector
            def _(eng):
                for f in self.ops['dve']:
                    f(eng)

            @block.gpsimd
            def _(eng):
                for f in self.ops['pool']:
                    f(eng)

            @block.sync
            def _(eng):
                for f in self.ops['sp']:
                    f(eng)


D = 1024
W = 2048
NTOK = 3072
EPS = 1e-6
ATT_SCALE = 192.0 ** -0.5


class T:
    def __init__(self, t, name):
        self.t = t
        self.r = Res(name)

    def __getitem__(self, k):
        return self.t[k]


class Prog:
    def __init__(self, nc, st):
        self.nc = nc
        self.st = st
        self.S = Sched(nc, st)
        self.ps = []
        for i in range(8):
            t = st.enter_context(nc.psum_tensor("ps%d" % i, [128, 512], F32))
            self.ps.append(T(t, "ps%d" % i))
        self.psn = 0
        self.cnt = 0

    def sb(self, scope, name, shape, dt, side=None):
        self.cnt += 1
        nm = "%s_%d" % (name, self.cnt)
        kw = {} if side is None else {"side": side}
        return T(scope.enter_context(self.nc.sbuf_tensor(nm, list(shape), dt, **kw)), nm)

    def next_ps(self):
        p = self.ps[self.psn]
        self.psn = (self.psn + 1) % 8
        return p

    def barrier(self):
        S = self.S
        for e in ENGS:
            for e2 in ENGS:
                if S.cnt[e2] > 0:
                    S._wait(e, (('e', e2), S.cnt[e2]))
            for i, v in enumerate(S.dval):
                if v > 0:
                    S._wait(e, (('d', i), v))

    def mm(self, ps, out_ap, lhsT, rhs, start, stop, reads):
        self.S.op('pe', lambda e: e.matmul(out_ap, lhsT=lhsT, rhs=rhs, start=start, stop=stop),
                  reads=[x.r for x in reads], writes=[ps.r])

    def tr(self, ps, out_ap, in_ap, reads, ident=None):
        idb = self.identb if ident is None else ident
        self.S.op('pe', lambda e: e.transpose(out=out_ap, in_=in_ap, identity=idb[:]),
                  reads=[x.r for x in reads] + [idb.r], writes=[ps.r])

    def act(self, out_ap, in_ap, func, reads, writes, scale=None, bias=None, accum=None):
        kw = {}
        if scale is not None:
            kw['scale'] = scale
        if bias is not None:
            kw['bias'] = bias
        if accum is not None:
            kw['accum_out'] = accum
        self.S.op('act', lambda e: e.activation(out=out_ap, in_=in_ap, func=func, **kw),
                  reads=[x.r for x in reads], writes=[x.r for x in writes])

    def tt(self, eng, out_ap, a, b, op, reads, writes):
        self.S.op(eng, lambda e: e.tensor_tensor(out=out_ap, in0=a, in1=b, op=op),
                  reads=[x.r for x in reads], writes=[x.r for x in writes])

    def ts(self, eng, out_ap, a, s1, s2, op0, op1, reads, writes):
        if op1 is None:
            self.S.op(eng, lambda e: e.tensor_scalar(out=out_ap, in0=a, scalar1=s1, scalar2=None, op0=op0),
                      reads=[x.r for x in reads], writes=[x.r for x in writes])
        else:
            self.S.op(eng, lambda e: e.tensor_scalar(out=out_ap, in0=a, scalar1=s1, scalar2=s2, op0=op0, op1=op1),
                      reads=[x.r for x in reads], writes=[x.r for x in writes])

    def stt(self, out_ap, a, sc, b, op0, op1, reads, writes):
        self.S.op('dve', lambda e: e.scalar_tensor_tensor(out=out_ap, in0=a, scalar=sc, in1=b, op0=op0, op1=op1),
                  reads=[x.r for x in reads], writes=[x.r for x in writes])

    def cp(self, eng, out_ap, in_ap, reads, writes):
        if eng == 'act':
            self.S.op('act', lambda e: e.activation(out=out_ap, in_=in_ap, func=AF.Copy),
                      reads=[x.r for x in reads], writes=[x.r for x in writes])
        else:
            self.S.op(eng, lambda e: e.tensor_copy(out=out_ap, in_=in_ap),
                      reads=[x.r for x in reads], writes=[x.r for x in writes])

    def recip(self, out_ap, in_ap, reads, writes):
        self.S.op('dve', lambda e: e.reciprocal(out=out_ap, in_=in_ap),
                  reads=[x.r for x in reads], writes=[x.r for x in writes])

    def memset(self, eng, ap, val, writes):
        self.S.op(eng, lambda e: e.memset(ap, val), writes=[x.r for x in writes])

    def dma(self, eng, out_ap, in_ap, reads, writes):
        self.S.dma(eng, lambda e: e.dma_start(out=out_ap, in_=in_ap),
                   reads=[x.r for x in reads], writes=[x.r for x in writes])


class RW:
    def __init__(self, name):
        self.r = Res(name)


class TV:
    def __init__(self, base, name):
        self.t = base.t
        self.r = Res(name)

    def __getitem__(self, k):
        return self.t[k]


class DT:
    def __init__(self, ap, name):
        self.ap = ap
        self.r = Res(name)


def tile_kind(ti):
    return 1 if ti < 2 else 0


class Builder:
    def __init__(self, layers):
        self.layers = list(layers)
        nc = bass.Bass("TRN2", target_bir_lowering=False)
        self.nc = nc
        self.din_names = {}
        self.dout_names = {}

    def din(self, name, shape, dt=F32):
        ap = self.nc.dram_tensor(name, list(shape), dt, kind="ExternalInput").ap()
        self.din_names[name] = tuple(shape)
        return DT(ap, name)

    def dout(self, name, shape):
        ap = self.nc.dram_tensor(name, list(shape), F32, kind="ExternalOutput").ap()
        self.dout_names[name] = tuple(shape)
        return DT(ap, name)

    def dbg(self, name, t, shape):
        if not os.environ.get("KDBG"):
            return
        d = self.dout("dbg_" + name, shape)
        self.P.dma('sp', d.ap, t, [], [d])

    def dscr(self, name, shape, dt):
        return DT(self.nc.dram_tensor(name, list(shape), dt).ap(), name)

    def build(self):
        nc = self.nc
        with ExitStack() as st:
            P = Prog(nc, st)
            self.P = P
            self.declare_io()
            self.setup_consts(st)
            nl = len(self.layers)
            for li, layer in enumerate(self.layers):
                with ExitStack() as lsc:
                    self.run_layer(li, layer, lsc, first=(li == 0), last=(li == nl - 1))
                P.barrier()
            P.S.finish()
            P.S.replay()
        return nc

    def declare_io(self):
        self.xs = self.din("xs", [2048, D])
        self.xp = self.din("xp", [1024, D])
        self.condT = self.din("condT", [128, 16])
        self.ident = self.din("ident", [128, 128])
        self.norm_g = self.din("norm_g", [4, D])
        self.ada_w = self.din("ada_w", [4, D, 3 * D])
        self.ada_b = self.din("ada_b", [4, 3 * D])
        self.fng = self.din("final_norm_g", [1, D])
        self.ys = self.dout("ys", [2048, D])
        self.yp = self.dout("yp", [1024, D])
        self.xres = self.dscr("xres", [3, 128, 8, D], F32)
        self.ptd = self.dscr("ptd", [16, 128, NTOK], BF16)
        self.xres_r = [[RW("xres%d_%d" % (a, b)) for b in range(8)] for a in range(3)]
        self.xin_r = [[RW("xin%d_%d" % (a, b)) for b in range(8)] for a in range(3)]
        self.yout_r = [[RW("yout%d_%d" % (a, b)) for b in range(8)] for a in range(3)]
        self.ptd_r = [RW("ptd%d" % a) for a in range(16)]
        kinds = set(l % 3 for l in self.layers)
        if 1 in kinds:
            self.pool_w_in = self.din("pool_w_in", [D, 2 * W])
            self.pool_w = self.din("pool_w", [4, 512, 512])
            self.pool_scaleT = self.din("pool_scaleT", [128, 16])
            self.pool_w_out = self.din("pool_w_out", [W, D])
            self.pool_invc = self.din("pool_invc", [4, NTOK])
        if 2 in kinds:
            self.declare_mla()
        if 0 in kinds:
            self.declare_s5()

    def x_view(self, src_first, ti):
        if src_first:
            if ti < 2:
                return self.xs.ap.rearrange("(a p t) d -> a p t d", p=128, t=8)[ti], self.xin_r[ti]
            return self.xp.ap.rearrange("(p t) d -> p t d", t=8), self.xin_r[ti]
        return self.xres.ap[ti], self.xres_r[ti]

    def y_view(self, ti):
        if ti < 2:
            return self.ys.ap.rearrange("(a p t) d -> a p t d", p=128, t=8)[ti], self.yout_r[ti]
        return self.yp.ap.rearrange("(p t) d -> p t d", t=8), self.yout_r[ti]

    def setup_consts(self, st):
        P = self.P
        P.identb = P.sb(st, "identb", [128, 128], BF16)
        P.dma('pool', P.identb[:], self.ident.ap[:, :], [self.ident], [P.identb])
        self.identf = P.sb(st, "identf", [128, 128], F32)
        P.dma('sp', self.identf[:], self.ident.ap[:, :], [self.ident], [self.identf])
        cT = P.sb(st, "cT", [128, 16], F32)
        P.dma('sp', cT[:], self.condT.ap[:, :], [self.condT], [cT])
        cS = P.sb(st, "cS", [128, 16], F32)
        P.act(cS[:], cT[:], AF.Silu, [cT], [cS])
        self.cS = cS
        self.wbufs = []
        self.wbn = 0
        self.gbcd = self.dscr("gbcd", [128, 2, D], F32)
        self.junk = P.sb(st, "junk", [128, D], BF16)

    def set_wbufs(self, scope, n, size=4096):
        self.wbufs = [self.P.sb(scope, "wbuf%d" % i, [128, size], BF16) for i in range(n)]
        self.wbn = 0

    def wbuf(self):
        w = self.wbufs[self.wbn]
        self.wbn = (self.wbn + 1) % len(self.wbufs)
        return w

    def load_w(self, src, rows0, nk, col0, ncol):
        P = self.P
        w = self.wbuf()
        view = w[:, 0:nk * ncol].rearrange("p (k n) -> p k n", k=nk)
        sap = src.ap[rows0:rows0 + nk * 128, col0:col0 + ncol].rearrange("(k p) n -> p k n", p=128)
        P.dma('pool', view, sap, [src], [w])
        return w, view

    def ada_phase(self, layer, sc):
        P = self.P
        cS = self.cS
        self.condB = P.sb(sc, "condB", [128, 2, 8, 128], BF16)
        for j in range(2):
            for k in range(8):
                P.cp('dve', self.condB[:, j, k, :], cS[:, k * 2 + j:k * 2 + j + 1].to_broadcast([128, 128]),
                     [cS], [self.condB])
        mod = P.sb(sc, "mod", [128, 2, 3 * D], F32)
        bb = P.sb(sc, "adab", [128, 3 * D], F32)
        P.dma('sp', bb[:], self.ada_b.ap[layer].partition_broadcast(128), [self.ada_b], [bb])
        ng = P.sb(sc, "ngbc", [128, D], F32)
        P.dma('sp', ng[:], self.norm_g.ap[layer].partition_broadcast(128), [self.norm_g], [ng])
        aw = DT(self.ada_w.ap[layer], "x")
        aw.r = self.ada_w.r
        for cb in range(6):
            w, wv = self.load_w(aw, 0, 8, cb * 512, 512)
            for j in range(2):
                ps = P.next_ps()
                for k in range(8):
                    P.mm(ps, ps[:, :], self.condB[:, j, k, :], wv[:, k, :], k == 0, k == 7, [self.condB, w])
                P.tt('dve', mod[:, j, cb * 512:(cb + 1) * 512], ps[:, :], bb[:, cb * 512:(cb + 1) * 512], ALU.add,
                     [ps, bb], [mod])
        for j in range(2):
            P.stt(mod[:, j, D:2 * D], mod[:, j, D:2 * D], 1.0, ng[:], ALU.add, ALU.mult, [mod, ng], [mod])
            P.dma('sp', self.gbcd.ap[:, j, :], mod[:, j, 2 * D:3 * D], [mod], [self.gbcd])
        return mod

    def rstd_from_ss(self, ss, rstd, n, dim):
        P = self.P
        P.ts('dve', rstd[:, 0:n], ss[:, 0:n], 1.0 / dim, EPS, ALU.mult, ALU.add, [ss], [rstd])
        P.act(rstd[:, 0:n], rstd[:, 0:n], AF.Sqrt, [rstd], [rstd])
        P.recip(rstd[:, 0:n], rstd[:, 0:n], [rstd], [rstd])

    def norm_phase(self, layer, first, hT, sc):
        P = self.P
        with ExitStack() as s2:
            self.set_wbufs(s2, 3)
            mod = self.ada_phase(layer, s2)
            xtb = [P.sb(s2, "xt%d" % i, [128, 8, D], F32) for i in range(2)]
            hcm = P.sb(s2, "hcm", [128, 8, D], BF16)
            tmp = [P.sb(s2, "ntmp%d" % i, [128, D], F32) for i in range(2)]
            ss = P.sb(s2, "nss", [128, 8], F32)
            rstd = P.sb(s2, "nrstd", [128, 8], F32)
            for ti in range(3):
                j = tile_kind(ti)
                xt = xtb[ti % 2]
                xv, xsrc = self.x_view(first, ti)
                P.dma('sp', xt[:], xv, list(xsrc), [xt])
                for t in range(8):
                    P.act(self.junk[:], xt[:, t, :], AF.Square, [xt], [self.junk, ss], accum=ss[:, t:t + 1])
                self.rstd_from_ss(ss, rstd, 8, D)
                for t in range(8):
                    tm = tmp[t % 2]
                    P.stt(tm[:], xt[:, t, :], rstd[:, t:t + 1], mod[:, j, D:2 * D], ALU.mult, ALU.mult,
                          [xt, rstd, mod], [tm])
                    P.tt('pool' if t % 3 != 2 else 'dve', hcm[:, t, :], tm[:], mod[:, j, 0:D], ALU.add, [tm, mod], [hcm])
                for t in range(8):
                    ps = P.next_ps()
                    pv = ps[:, :].bitcast(BF16).rearrange("p (k n) -> p k n", k=8)
                    for k in range(8):
                        P.tr(ps, pv[:, k, :], hcm[:, t, k * 128:(k + 1) * 128], [hcm])
                    eng = 'act' if t % 2 == 0 else 'dve'
                    base = ti * 1024 + t
                    P.cp(eng, hT[:, :, base:(ti + 1) * 1024:8], pv, [ps], [hT])
        P.barrier()

    def out_phase(self, wout_src, pt_loader, first, last, sc):
        P = self.P
        with ExitStack() as s2:
            wo = P.sb(s2, "wo", [128, 16, D], BF16)
            for q in range(4):
                sap = wout_src.ap[q * 512:(q + 1) * 512, :].rearrange("(k p) n -> p k n", p=128)
                P.dma('pool', wo[:, q * 4:(q + 1) * 4, :], sap, [wout_src], [wo])
            self.gbc = P.sb(s2, "gbc", [128, 2, D], F32)
            P.dma('sp', self.gbc[:], self.gbcd.ap[:, :, :], [self.gbcd], [self.gbc])
            if last:
                self.fng_bc = P.sb(s2, "fng_bc", [128, D], F32)
                P.dma('sp', self.fng_bc[:], self.fng.ap[0].partition_broadcast(128), [self.fng], [self.fng_bc])
            xts = [P.sb(s2, "oxt%d" % i, [128, D], F32) for i in range(3)]
            tmps = [P.sb(s2, "otmp%d" % i, [128, 512], F32) for i in range(2)]
            ss = P.sb(s2, "oss", [128, 1], F32)
            rstd = P.sb(s2, "orstd", [128, 1], F32)
            for ti in range(3):
                j = tile_kind(ti)
                pt = pt_loader(ti, s2)
                xv, xsrc = self.x_view(first, ti)
                if last:
                    ov, odst = self.y_view(ti)
                else:
                    ov, odst = self.xres.ap[ti], self.xres_r[ti]
                def ld(t_):
                    P.dma('sp', xts[t_ % 3][:], xv[:, t_, :], [xsrc[t_]], [xts[t_ % 3]])
                ld(0)
                for t in range(8):
                    xt = xts[t % 3]
                    if t + 1 < 8:
                        ld(t + 1)
                    for h in range(2):
                        ps = P.next_ps()
                        for k in range(16):
                            P.mm(ps, ps[:, :], pt[:, k, t:1024:8], wo[:, k, h * 512:(h + 1) * 512],
                                 k == 0, k == 15, [pt, wo])
                        tm = tmps[h]
                        P.tt('dve', tm[:], ps[:, :], self.gbc[:, j, h * 512:(h + 1) * 512], ALU.mult,
                             [ps, self.gbc], [tm])
                        P.tt('pool' if h == 0 else 'dve', xt[:, h * 512:(h + 1) * 512], xt[:, h * 512:(h + 1) * 512], tm[:],
                             ALU.add, [xt, tm], [xt])
                    if last:
                        P.act(self.junk[:], xt[:], AF.Square, [xt], [self.junk, ss], accum=ss[:, 0:1])
                        self.rstd_from_ss(ss, rstd, 1, D)
                        P.stt(xt[:], xt[:], rstd[:, 0:1], self.fng_bc[:], ALU.mult, ALU.mult,
                              [xt, rstd, self.fng_bc], [xt])
                    P.dma('sp', ov[:, t, :], xt[:], [xt], [odst[t]])
        P.barrier()

    def pt_from_dram(self, ti, sc):
        P = self.P

        def ld(t_):
            pt_ = self._pt_tiles[t_ % 2]
            P.dma('sp', pt_[:], self.ptd.ap[:, :, t_ * 1024:(t_ + 1) * 1024].rearrange("k p n -> p k n"),
                  self.ptd_r, [pt_])
        if not hasattr(self, "_pt_tiles") or self._pt_scope is not sc:
            self._pt_tiles = [P.sb(sc, "ptt%d" % i, [128, 16, 1024], BF16) for i in range(2)]
            self._pt_scope = sc
            ld(0)
        if ti + 1 < 3:
            ld(ti + 1)
        return self._pt_tiles[ti % 2]

    def run_layer(self, li, layer, sc, first, last):
        P = self.P
        kind = layer % 3
        jj = layer // 3
        hsc = ExitStack()
        hT = P.sb(hsc, "hT", [128, 8, NTOK], BF16, side="right")
        self.norm_phase(layer, first, hT, sc)
        if kind == 1:
            self.pool_mixer(jj, hT, sc)
            P.barrier()
            hsc.close()
            self.out_phase(self.pool_w_out, self.pt_from_dram, first, last, sc)
        elif kind == 2:
            self.mla_mixer(jj, hT, sc, hsc)
            P.barrier()
            self.out_phase(DTsub(self.mla_w_out, jj), self.pt_from_dram, first, last, sc)
        else:
            self.s5_mixer(jj, hT, sc, hsc, first, last)


def DTsub(dt, idx):
    d = DT(dt.ap[idx], "sub")
    d.r = dt.r
    return d


def _pool_mixer(self, jj, hT, sc):
    P = self.P
    with ExitStack() as s2:
        self.set_wbufs(s2, 4)
        def bufpair(name):
            return (P.sb(s2, name + "s", [128, 1, 2048 + 32], F32), P.sb(s2, name + "p", [128, 4, 256 + 32], F32))
        U = bufpair("pU")
        A = bufpair("pA")
        B = bufpair("pB")
        Ls = (2048, 256)
        for b in U:
            P.memset('pool', b[:], 0.0, [b])
        invc = P.sb(s2, "invc", [128, NTOK], F32)
        pooled = P.sb(s2, "pooled", [128, 4, NTOK], BF16)
        pscale = P.sb(s2, "pscale", [128, 16], F32)
        P.dma('sp', pscale[:], self.pool_scaleT.ap[:, :], [self.pool_scaleT], [pscale])
        szs = [P.sb(s2, "sz%d" % i, [128, 512], BF16) for i in range(2)]
        ptb = [P.sb(s2, "ptb%d" % i, [128, NTOK], BF16) for i in range(2)]
        win_src = self.pool_w_in
        for g in range(4):
            P.dma('sp', invc[:], self.pool_invc.ap[g].partition_broadcast(128), [self.pool_invc], [invc])
            wu, wuv = self.load_w(win_src, 0, 8, g * 512, 512)
            eng = 'dve'
            for jb in range(4):
                for pc in range(6):
                    ps = P.next_ps()
                    for k in range(8):
                        P.mm(ps, ps[:, :], wuv[:, k, jb * 128:(jb + 1) * 128], hT[:, k, pc * 512:(pc + 1) * 512],
                             k == 0, k == 7, [wu, hT])
                    if pc < 4:
                        P.cp('act', U[0][:, 0, 16 + pc * 512:16 + (pc + 1) * 512], ps[:, :], [ps], [U[0]])
                    else:
                        q = pc - 4
                        P.cp('act', U[1][:, 2 * q:2 * q + 2, 16:272], ps[:, :].rearrange("p (s l) -> p s l", s=2),
                             [ps], [U[1]])
                eng = 'dve'
                cur = U
                nxt = [A, B]
                steps = [(1, 1, 0), (2, 3, 1), (4, 2, 6), (8, 4, 12)][:g + 1]
                for si, (olo, alo, blo) in enumerate(steps):
                    dst = nxt[si % 2]
                    for q in range(2):
                        L = Ls[q]
                        n = {1: L + 31, 2: L + 29, 4: L + 25, 8: L + 17}[olo]
                        P.tt(eng, dst[q][:, :, olo:olo + n], cur[q][:, :, alo:alo + n], cur[q][:, :, blo:blo + n],
                             ALU.add, [cur[q]], [dst[q]])
                    cur = dst
                other = B if cur is A else A
                for q in range(2):
                    L = Ls[q]
                    if q == 0:
                        iv = invc[:, 0:2048].rearrange("p (s l) -> p s l", s=1)
                        pv = pooled[:, jb, 0:2048].rearrange("p (s l) -> p s l", s=1)
                    else:
                        iv = invc[:, 2048:NTOK].rearrange("p (s l) -> p s l", s=4)
                        pv = pooled[:, jb, 2048:NTOK].rearrange("p (s l) -> p s l", s=4)
                    P.tt(eng, other[q][:, :, 16:16 + L], cur[q][:, :, 16:16 + L], iv, ALU.mult,
                         [cur[q], invc], [other[q]])
                    P.tt(eng, pv, other[q][:, :, 16:16 + L], U[q][:, :, 16:16 + L], ALU.subtract,
                         [other[q], U[q]], [pooled])
            pw, pwv = self.load_w(DTsub(self.pool_w, g), 0, 4, 0, 512)
            wz, wzv = self.load_w(win_src, 0, 8, W + g * 512, 512)
            for db in range(4):
                d = g * 4 + db
                pt = ptb[d % 2]
                for pc in range(6):
                    psm = P.next_ps()
                    for c in range(4):
                        P.mm(psm, psm[:, :], pwv[:, c, db * 128:(db + 1) * 128], pooled[:, c, pc * 512:(pc + 1) * 512],
                             c == 0, c == 3, [pw, pooled])
                    psz = P.next_ps()
                    for k in range(8):
                        P.mm(psz, psz[:, :], wzv[:, k, db * 128:(db + 1) * 128], hT[:, k, pc * 512:(pc + 1) * 512],
                             k == 0, k == 7, [wz, hT])
                    sz = szs[pc % 2]
                    P.act(sz[:], psz[:, :], AF.Silu, [psz], [sz])
                    P.stt(pt[:, pc * 512:(pc + 1) * 512], psm[:, :], pscale[:, d:d + 1], sz[:], ALU.mult, ALU.mult,
                          [psm, pscale, sz], [pt])
                P.dma('sp', self.ptd.ap[d], pt[:], [pt], [self.ptd_r[d]])


Builder.pool_mixer = _pool_mixer


_PROG_CACHE = {}


def _pool_invc():
    out = np.zeros((4, NTOK), np.float32)
    for g, win in enumerate((2, 4, 8, 16)):
        lo = win // 2
        for (base, L, n) in ((0, 2048, 1), (2048, 256, 4)):
            t = np.arange(L)
            cnt = (np.clip(t - lo + win, 0, L) - np.clip(t - lo, 0, L)).astype(np.float32)
            for s in range(n):
                out[g, base + s * L:base + (s + 1) * L] = 1.0 / cnt
    return out


def make_in_maps(inp, layers):
    f = lambda a: np.ascontiguousarray(np.asarray(a, dtype=np.float32))
    kinds = set(l % 3 for l in layers)
    shared = {
        "ident": np.eye(128, dtype=np.float32),
        "norm_g": f(inp["norm_g"]), "ada_w": f(inp["ada_w"]), "ada_b": f(inp["ada_b"]),
        "final_norm_g": f(inp["final_norm_g"]).reshape(1, D),
    }
    if 1 in kinds:
        shared.update({
            "pool_w_in": f(inp["pool_w_in"][0]), "pool_w": f(inp["pool_w"][0]),
            "pool_scaleT": f(np.asarray(inp["pool_scale"][0]).reshape(16, 128).T),
            "pool_w_out": f(inp["pool_w_out"][0]), "pool_invc": _pool_invc(),
        })
    if 2 in kinds:
        shared.update(mla_shared_inputs(inp))
    if 0 in kinds:
        shared.update(s5_shared_inputs(inp))
    maps = []
    xp = np.asarray(inp["x_prompt"], np.float32)
    xs = np.asarray(inp["x_sample"], np.float32)
    cc = np.asarray(inp["c"], np.float32)
    cctx = np.asarray(inp["c_ctx"], np.float32)
    for c in range(8):
        b = c % 2
        m = dict(shared)
        m["xs"] = f(xs[b])
        m["xp"] = f(xp[4 * c:4 * c + 4].reshape(1024, D))
        cond = np.stack([cctx, cc[b]], axis=-1)
        m["condT"] = f(cond.reshape(8, 128, 2).transpose(1, 0, 2).reshape(128, 16))
        if 2 in kinds:
            m.update(mla_core_inputs(inp, c))
        if 0 in kinds:
            m.update(s5_core_inputs(inp, c))
        maps.append(m)
    return maps


def run_layers(inp, layers):
    key = tuple(layers)
    if key not in _PROG_CACHE:
        b = Builder(layers)
        b.build()
        _PROG_CACHE[key] = b
    b = _PROG_CACHE[key]
    maps = make_in_maps(inp, layers)
    maps = [{k: v for k, v in m.items() if k in b.din_names} for m in maps]
    res = run_bass_kernel_spmd(b.nc, maps, core_ids=list(range(8)))
    return res.results


def assemble(results, layers):
    kinds = [l % 3 for l in layers]
    yp = np.concatenate([results[c]["yp"].reshape(4, 256, D) for c in range(8)], axis=0)
    ys = np.stack([results[b]["ys"] for b in range(2)], axis=0)
    outs = [yp.astype(np.float32), ys.astype(np.float32)]
    n_s5 = sum(1 for k in kinds if k == 0)
    n_mla = sum(1 for k in kinds if k == 2)
    if n_s5:
        re = np.concatenate([results[c]["ns_re"].reshape(4, n_s5, 2, 128, 64) for c in range(8)], axis=0)
        im = np.concatenate([results[c]["ns_im"].reshape(4, n_s5, 2, 128, 64) for c in range(8)], axis=0)
        outs += [re.astype(np.float32), im.astype(np.float32)]
    if n_mla:
        ck = np.concatenate([results[c]["nckv"].reshape(4, n_mla, 256, 128) for c in range(8)], axis=0)
        kp = np.concatenate([results[c]["nkpe"].reshape(4, n_mla, 256, 64) for c in range(8)], axis=0)
        outs += [ck.astype(np.float32), kp.astype(np.float32)]
    return tuple(outs)


LAYERS = (0, 1, 2, 3)


def kernel(**inputs):
    results = run_layers(inputs, LAYERS)
    return assemble(results, LAYERS)


KSTOP = os.environ.get("KSTOP", "")
KSKIP = os.environ.get("KSKIP", "")
NB_KEYS = 28


def _declare_mla(self):
    self.mla_w_in = self.din("mla_w_in", [D, 2496])
    self.mla_q_norm = self.din("mla_q_norm", [1, 256])
    self.mla_kv_norm = self.din("mla_kv_norm", [1, 128])
    self.mla_wq_b = self.din("mla_wq_b", [256, 3072])
    self.mla_wq_pesw = self.din("mla_wq_pesw", [256, 1024])
    self.mla_wukT = self.din("mla_wukT", [16, 128, 128])
    self.mla_wkv_b = self.din("mla_wkv_b", [128, 4096])
    self.mla_w_out = self.din("mla_w_out", [1, W, D])
    self.cckv = self.din("cckv", [512, 128])
    self.ckpe = self.din("ckpe", [512, 64])
    self.ropeT_cos = self.din("ropeT_cos", [64, 2048])
    self.ropeT_sin = self.din("ropeT_sin", [64, 2048])
    self.ropeK_cos = self.din("ropeK_cos", [2, 128, 8, 64])
    self.ropeK_sin = self.din("ropeK_sin", [2, 128, 8, 64])
    self.pmask = self.din("pmask", [128, 1024])
    self.szd = self.dscr("szd", [16, 128, NTOK], BF16)
    self.szd_r = [RW("szd%d" % a) for a in range(16)]
    self.nckv = self.dout("nckv", [1024, 128])
    self.nkpe = self.dout("nkpe", [1024, 64])


def _mla_mixer(self, jj, hT, sc, hsc):
    P = self.P
    with ExitStack() as s2:
        qnT = P.sb(s2, "qnT", [128, 2, NTOK], BF16)
        KT = P.sb(s2, "KT", [128, NB_KEYS, 128], BF16)
        PET = P.sb(s2, "PET", [128, NB_KEYS, 128], BF16)
        V = P.sb(s2, "V", [128, NB_KEYS, 130], BF16)
        P.memset('pool', V[:], 1.0, [V])
        win = self.mla_w_in
        with ExitStack() as s3:
            self.set_wbufs(s3, 1)
            qg = P.sb(s3, "qg", [128, 256], F32)
            kg = P.sb(s3, "kg", [128, 128], F32)
            P.dma('sp', qg[:], self.mla_q_norm.ap[0].partition_broadcast(128), [self.mla_q_norm], [qg])
            P.dma('sp', kg[:], self.mla_kv_norm.ap[0].partition_broadcast(128), [self.mla_kv_norm], [kg])
            wqa, wqav = self.load_w(win, 0, 8, 0, 448)
            raw = P.sb(s3, "raw", [128, 8, 448], F32)
            ss = P.sb(s3, "mss", [128, 2, 8], F32)
            rstd = P.sb(s3, "mrstd", [128, 2, 8], F32)
            qn_cm = P.sb(s3, "qn_cm", [128, 8, 256], BF16)
            ckvn = P.sb(s3, "ckvn", [128, 8, 128], F32)
            ckvb = P.sb(s3, "ckvb", [128, 8, 128], BF16)
            kr = P.sb(s3, "kr", [128, 8, 64], F32)
            krt = P.sb(s3, "krt", [128, 8, 64], F32)
            krb = P.sb(s3, "krb", [128, 8, 64], BF16)
            rc = P.sb(s3, "rc", [128, 8, 64], F32)
            rs = P.sb(s3, "rs", [128, 8, 64], F32)
            cx = P.sb(s3, "cx", [128, 4, 128], F32)
            cxb = P.sb(s3, "cxb", [128, 4, 128], BF16)
            cp_ = P.sb(s3, "cp_", [128, 4, 64], F32)
            cpb = P.sb(s3, "cpb", [128, 4, 64], BF16)
            P.dma('sp', cx[:], self.cckv.ap.rearrange("(b p) r -> p b r", p=128), [self.cckv], [cx])
            P.dma('sp', cp_[:], self.ckpe.ap.rearrange("(b p) r -> p b r", p=128), [self.ckpe], [cp_])
            P.cp('dve', cxb[:], cx[:], [cx], [cxb])
            P.cp('dve', cpb[:], cp_[:], [cp_], [cpb])
            P.cp('pool', V[:, 0:4, 0:128], cx[:], [cx], [V])
            ps = P.next_ps()
            pv = ps[:, :].bitcast(BF16).rearrange("p (k n) -> p k n", k=8)
            for b in range(4):
                P.tr(ps, pv[:, b, :], cxb[:, b, :], [cxb])
            P.cp('act', KT[:, 0:4, :], pv[:, 0:4, :], [ps], [KT])
            ps = P.next_ps()
            pv = ps[:, :].bitcast(BF16).rearrange("p (k n) -> p k n", k=8)
            for b in range(4):
                P.tr(ps, pv[0:64, b, :], cpb[:, b, :], [cpb])
            P.cp('act', PET[0:64, 0:4, :], pv[0:64, 0:4, :], [ps], [PET])
            if KSTOP == 'm1a':
                return
            for ti in range(3):
                if ti < 2 and 'D' not in KSKIP:
                    P.dma('sp', rc[:], self.ropeK_cos.ap[ti], [self.ropeK_cos], [rc])
                    P.dma('sp', rs[:], self.ropeK_sin.ap[ti], [self.ropeK_sin], [rs])
                for t in range(8):
                    ps = P.next_ps()
                    for k in range(8):
                        P.mm(ps, ps[:, 0:448], hT[:, k, ti * 1024 + t:(ti + 1) * 1024:8], wqav[:, k, :],
                             k == 0, k == 7, [hT, wqa])
                    P.cp('dve', raw[:, t, :], ps[:, 0:448], [ps], [raw])
                    if 'C' not in KSKIP:
                        P.act(self.junk[:, 0:256], raw[:, t, 0:256], AF.Square, [raw], [self.junk, ss],
                              accum=ss[:, 0, t:t + 1])
                        P.act(self.junk[:, 0:128], raw[:, t, 256:384], AF.Square, [raw], [self.junk, ss],
                              accum=ss[:, 1, t:t + 1])
                if KSTOP == 'm1c':
                    return
                self.rstd_from_ss(T2(ss, ss[:, 0, :]), T2(rstd, rstd[:, 0, :]), 8, 256)
                self.rstd_from_ss(T2(ss, ss[:, 1, :]), T2(rstd, rstd[:, 1, :]), 8, 128)
                if KSTOP == 'm1d':
                    return
                for t in range(8):
                    P.stt(qn_cm[:, t, :], raw[:, t, 0:256], rstd[:, 0, t:t + 1], qg[:], ALU.mult, ALU.mult,
                          [raw, rstd, qg], [qn_cm])
                    P.stt(ckvn[:, t, :], raw[:, t, 256:384], rstd[:, 1, t:t + 1], kg[:], ALU.mult, ALU.mult,
                          [raw, rstd, kg], [ckvn])
                if KSTOP == 'm1e':
                    return
                P.cp('pool', ckvb[:], ckvn[:], [ckvn], [ckvb])
                kb0 = 4 + ti * 8 if ti < 2 else 20
                P.cp('pool', V[:, kb0:kb0 + 8, 0:128], ckvn[:], [ckvn], [V])
                if ti == 2:
                    if 'A' not in KSKIP:
                        P.dma('sp', self.nckv.ap.rearrange("(p t) r -> p t r", t=8), ckvn[:], [ckvn], [self.nckv])
                    P.cp('pool', kr[:], raw[:, :, 384:448], [raw], [kr])
                    if 'A' not in KSKIP:
                        P.dma('sp', self.nkpe.ap.rearrange("(p t) r -> p t r", t=8), kr[:], [kr], [self.nkpe])
                    P.cp('pool', krb[:], kr[:], [kr], [krb])
                elif 'B' in KSKIP:
                    P.cp('pool', krb[:], raw[:, :, 384:448], [raw], [krb])
                else:
                    xv = raw[:, :, 384:448].rearrange("p t (s h i) -> p t s h i", s=2, h=2)
                    for s in range(2):
                        for h in range(2):
                            sl = slice(s * 32 + h * 16, s * 32 + h * 16 + 16)
                            so = slice(s * 32 + (1 - h) * 16, s * 32 + (1 - h) * 16 + 16)
                            P.tt('dve', krt[:, :, sl], raw[:, :, 384 + so.start:384 + so.stop], rs[:, :, sl], ALU.mult,
                                 [raw, rs], [krt])
                    P.tt('dve', kr[:], raw[:, :, 384:448], rc[:], ALU.mult, [raw, rc], [kr])
                    P.tt('dve', krb[:], kr[:], krt[:], ALU.add, [kr, krt], [krb])
                if KSTOP == 'm1f':
                    return
                for t in range(8):
                    ps = P.next_ps()
                    pv = ps[:, :].bitcast(BF16).rearrange("p (k n) -> p k n", k=8)
                    P.tr(ps, pv[:, 0, :], qn_cm[:, t, 0:128], [qn_cm])
                    P.tr(ps, pv[:, 1, :], qn_cm[:, t, 128:256], [qn_cm])
                    P.tr(ps, pv[:, 2, :], ckvb[:, t, :], [ckvb])
                    P.tr(ps, pv[0:64, 3, :], krb[:, t, :], [krb])
                    P.cp('act', qnT[:, :, ti * 1024 + t:(ti + 1) * 1024:8], pv[:, 0:2, :], [ps], [qnT])
                    P.cp('dve', KT[:, kb0 + t, :], pv[:, 2, :], [ps], [KT])
                    P.cp('act', PET[0:64, kb0 + t, :], pv[0:64, 3, :], [ps], [PET])
            if KSTOP == 'm1b':
                return
            refk = P.sb(s3, "refk", [128, 5], F32)
            refp = P.sb(s3, "refp", [128, 5], F32)
            P.cp('dve', refk[:, 0:1], KT[:, 0, 0:1], [KT], [refk])
            P.cp('dve', refp[0:64, 0:1], PET[0:64, 0, 0:1], [PET], [refp])
            for s in range(4):
                P.cp('dve', refk[:, 1 + s:2 + s], KT[:, 20, 32 * s:32 * s + 1], [KT], [refk])
                P.cp('dve', refp[0:64, 1 + s:2 + s], PET[0:64, 20, 32 * s:32 * s + 1], [PET], [refp])
            P.ts('dve', KT[:, 0:20, :], KT[:, 0:20, :], refk[:, 0:1], None, ALU.subtract, None, [KT, refk], [KT])
            P.ts('dve', PET[0:64, 0:20, :], PET[0:64, 0:20, :], refp[0:64, 0:1], None, ALU.subtract, None,
                 [PET, refp], [PET])
            for s in range(4):
                P.ts('dve', KT[:, 20:28, 32 * s:32 * s + 32], KT[:, 20:28, 32 * s:32 * s + 32], refk[:, 1 + s:2 + s],
                     None, ALU.subtract, None, [KT, refk], [KT])
                P.ts('dve', PET[0:64, 20:28, 32 * s:32 * s + 32], PET[0:64, 20:28, 32 * s:32 * s + 32],
                     refp[0:64, 1 + s:2 + s], None, ALU.subtract, None, [PET, refp], [PET])
        if KSTOP == 'm1':
            return
        P.barrier()
        with ExitStack() as s3:
            self.set_wbufs(s3, 6, 1024)
            szb = [P.sb(s3, "szb%d" % i, [128, NTOK], BF16) for i in range(2)]
            for h in range(16):
                wz, wzv = self.load_w(win, 0, 8, 448 + h * 128, 128)
                sb_ = szb[h % 2]
                for pc in range(6):
                    sl = slice(pc * 512, (pc + 1) * 512)
                    psz = P.next_ps()
                    for k in range(8):
                        P.mm(psz, psz[:, :], wzv[:, k, :], hT[:, k, sl], k == 0, k == 7, [wz, hT])
                    P.act(sb_[:, sl], psz[:, :], AF.Silu, [psz], [sb_])
                P.dma('sp', self.szd.ap[h], sb_[:], [sb_], [self.szd_r[h]])
        P.barrier()
        hsc.close()
        if KSTOP == 'm0':
            return
        wqb = P.sb(s2, "wqb", [128, 2, 3072], BF16)
        wsw = P.sb(s2, "wsw", [128, 2, 1024], BF16)
        wuk = P.sb(s2, "wuk", [128, 16, 128], BF16)
        wuv = P.sb(s2, "wuv", [128, 16, 128], BF16)
        for kc in range(2):
            for hf in range(2):
                P.dma('pool', wqb[:, kc, hf * 1536:(hf + 1) * 1536],
                      self.mla_wq_b.ap[kc * 128:(kc + 1) * 128, hf * 1536:(hf + 1) * 1536], [self.mla_wq_b], [wqb])
        P.dma('pool', wsw[:], self.mla_wq_pesw.ap.rearrange("(k p) n -> p k n", p=128), [self.mla_wq_pesw], [wsw])
        P.dma('pool', wuk[:], self.mla_wukT.ap.rearrange("h d r -> d h r"), [self.mla_wukT], [wuk])
        P.dma('pool', wuv[:], self.mla_wkv_b.ap.rearrange("r (h two v) -> r h two v", two=2, v=128)[:, :, 1, :],
              [self.mla_wkv_b], [wuv])
        cosT = P.sb(s2, "cosT", [64, 2048], F32)
        sinT = P.sb(s2, "sinT", [64, 2048], F32)
        P.dma('sp', cosT[:], self.ropeT_cos.ap[:, :], [self.ropeT_cos], [cosT])
        P.dma('sp', sinT[:], self.ropeT_sin.ap[:, :], [self.ropeT_sin], [sinT])
        mask = P.sb(s2, "pmaskb", [128, 1024], BF16)
        P.dma('pool', mask[:], self.pmask.ap[:, :], [self.pmask], [mask])
        qnope = [P.sb(s2, "qnope%d" % i, [128, 512], BF16) for i in range(2)]
        qabs = P.sb(s2, "qabs", [128, NTOK], BF16)
        qpe = P.sb(s2, "qpe", [64, NTOK], BF16)
        rt1 = P.sb(s2, "rt1", [64, 512], F32)
        rt2 = P.sb(s2, "rt2", [64, 512], F32)
        oT = P.sb(s2, "oT", [128, NTOK], BF16)
        PTall = P.sb(s2, "PTall", [128, 20, 512], BF16)
        ptb = [P.sb(s2, "mptb%d" % i, [128, NTOK], BF16) for i in range(2)]
        szs = [P.sb(s2, "msz%d" % i, [128, NTOK], BF16) for i in range(2)]
        rinv = [P.sb(s2, "rinv%d" % i, [128, 1], F32) for i in range(2)]
        On = [P.sb(s2, "On%d" % i, [128, 128], BF16) for i in range(2)]
        if KSTOP == 'm2w':
            return
        for h in range(16 if KSTOP not in ('m2q', 'm2a', 'm2h') else 1):
            sz = szs[h % 2]
            P.dma('sp', sz[:], self.szd.ap[h], [self.szd_r[h]], [sz])
            for pc in range(6):
                sl = slice(pc * 512, (pc + 1) * 512)
                ps = P.next_ps()
                for kc in range(2):
                    P.mm(ps, ps[:, :], wqb[:, kc, h * 192:h * 192 + 128], qnT[:, kc, sl], kc == 0, kc == 1, [wqb, qnT])
                qn = qnope[pc % 2]
                P.cp('act', qn[:], ps[:, :], [ps], [qn])
                ps2 = P.next_ps()
                P.mm(ps2, ps2[:, :], wuk[:, h, :], qn[:], True, True, [wuk, qn])
                P.cp('dve', qabs[:, sl], ps2[:, :], [ps2], [qabs])
                ps3 = P.next_ps()
                for kc in range(2):
                    P.mm(ps3, ps3[0:64, :], wqb[:, kc, h * 192 + 128:h * 192 + 192], qnT[:, kc, sl],
                         kc == 0, kc == 1, [wqb, qnT])
                if pc < 4:
                    ps4 = P.next_ps()
                    for kc in range(2):
                        P.mm(ps4, ps4[0:64, :], wsw[:, kc, h * 64:(h + 1) * 64], qnT[:, kc, sl],
                             kc == 0, kc == 1, [wsw, qnT])
                    P.tt('dve', rt1[:], ps3[0:64, :], cosT[:, sl], ALU.mult, [ps3, cosT], [rt1])
                    P.tt('dve', rt2[:], ps4[0:64, :], sinT[:, sl], ALU.mult, [ps4, sinT], [rt2])
                    P.tt('dve', qpe[:, sl], rt1[:], rt2[:], ALU.add, [rt1, rt2], [qpe])
                else:
                    P.cp('act', qpe[:, sl], ps3[0:64, :], [ps3], [qpe])
            if KSTOP == 'm2q':
                return
            for qp in range(6):
                sl = slice(qp * 512, (qp + 1) * 512)
                kbs = list(range(0, 20)) if qp < 4 else list(range(20, 28))
                for i, kb in enumerate(kbs):
                    ps = P.next_ps()
                    P.mm(ps, ps[:, :], KT[:, kb, :], qabs[:, sl], True, False, [KT, qabs])
                    P.mm(ps, ps[:, :], PET[0:64, kb, :], qpe[0:64, sl], False, True, [PET, qpe])
                    P.act(PTall[:, i, :], ps[:, :], AF.Exp, [ps], [PTall], scale=ATT_SCALE)
                    if qp >= 4:
                        P.tt('dve', PTall[:, i, :], PTall[:, i, :], mask[:, (qp - 4) * 512:(qp - 3) * 512], ALU.mult,
                             [PTall, mask], [PTall])
                pst = P.next_ps()
                ptv = pst[:, :].bitcast(BF16).rearrange("p (k n) -> p k n", k=8)
                for qs in range(4):
                    pso = P.next_ps()
                    for i, kb in enumerate(kbs):
                        P.mm(pso, pso[:, 0:129], PTall[:, i, qs * 128:(qs + 1) * 128], V[:, kb, 0:129],
                             i == 0, i == len(kbs) - 1, [PTall, V])
                    ri = rinv[qs % 2]
                    on = On[qs % 2]
                    P.recip(ri[:], pso[:, 128:129], [pso], [ri])
                    P.ts('dve', on[:], pso[:, 0:128], ri[:, 0:1], None, ALU.mult, None, [pso, ri], [on])
                    P.tr(pst, ptv[:, qs, :], on[:], [on])
                P.cp('act', oT[:, sl], ptv[:, 0:4, :], [pst], [oT])
            if KSTOP == 'm2a':
                return
            pt = ptb[h % 2]
            for pc in range(6):
                sl = slice(pc * 512, (pc + 1) * 512)
                psu = P.next_ps()
                P.mm(psu, psu[:, :], wuv[:, h, :], oT[:, sl], True, True, [wuv, oT])
                P.tt('dve', pt[:, sl], psu[:, :], sz[:, sl], ALU.mult, [psu, sz], [pt])
            P.dma('sp', self.ptd.ap[h], pt[:], [pt], [self.ptd_r[h]])


class T2:
    def __init__(self, base, view):
        self.r = base.r
        self.v = view

    def __getitem__(self, k):
        return self.v[k]


Builder.declare_mla = _declare_mla
Builder.mla_mixer = _mla_mixer


def _rope_tables():
    half = 16
    inv = (10000.0 ** (-np.arange(half, dtype=np.float32) / half)).astype(np.float32)
    tok = np.arange(2048)
    pos = [(tok // 64).astype(np.float32), (tok % 64).astype(np.float32)]
    cos = np.zeros((2048, 64), np.float32)
    sin = np.zeros((2048, 64), np.float32)
    for s in range(2):
        ang = (pos[s][:, None] * inv[None, :]).astype(np.float32)
        c, sn = np.cos(ang).astype(np.float32), np.sin(ang).astype(np.float32)
        cos[:, s * 32:s * 32 + 16] = c
        cos[:, s * 32 + 16:s * 32 + 32] = c
        sin[:, s * 32:s * 32 + 16] = -sn
        sin[:, s * 32 + 16:s * 32 + 32] = sn
    return cos, sin


def mla_shared_inputs(inp):
    f = lambda a: np.ascontiguousarray(np.asarray(a, dtype=np.float32))
    wq = np.asarray(inp["mla_wq_b"][0], np.float32)
    wq3 = wq.reshape(256, 16, 192)
    pe = wq3[:, :, 128:192]
    perm = np.array([s * 32 + (1 - hh) * 16 + i for s in range(2) for hh in range(2) for i in range(16)])
    pesw = pe[:, :, perm].reshape(256, 1024)
    wkv = np.asarray(inp["mla_wkv_b"][0], np.float32)
    wukT = wkv.reshape(128, 16, 2, 128)[:, :, 0, :].transpose(1, 2, 0)
    cos, sin = _rope_tables()
    pm = (np.arange(128)[:, None] // 32 == np.arange(1024)[None, :] // 256).astype(np.float32)
    return {
        "mla_w_in": f(inp["mla_w_in"][0]), "mla_q_norm": f(inp["mla_q_norm"][0]).reshape(1, 256),
        "mla_kv_norm": f(inp["mla_kv_norm"][0]).reshape(1, 128), "mla_wq_b": f(wq), "mla_wq_pesw": f(pesw),
        "mla_wukT": f(wukT), "mla_wkv_b": f(wkv), "mla_w_out": f(inp["mla_w_out"]),
        "ropeT_cos": f(cos.T), "ropeT_sin": f(sin.T),
        "ropeK_cos": f(cos.reshape(2, 128, 8, 64)), "ropeK_sin": f(sin.reshape(2, 128, 8, 64)),
        "pmask": f(pm),
    }


def mla_core_inputs(inp, c):
    f = lambda a: np.ascontiguousarray(np.asarray(a, dtype=np.float32))
    b = c % 2
    return {"cckv": f(inp["cache_ckv"][b, 0]), "ckpe": f(inp["cache_kpe"][b, 0])}


TWO_PI = 6.283185307179586
PI = 3.141592653589793
NCOL = 408
NSEG = 12


def _declare_s5(self):
    n5 = 2
    self.s5_w_in = self.din("s5_w_in", [n5, D, 2 * W])
    self.s5_lamT_re = self.din("s5_lamT_re", [n5, 128, 128])
    self.s5_lamT_im = self.din("s5_lamT_im", [n5, 128, 128])
    self.s5_lstepT = self.din("s5_lstepT", [n5, 128, 128])
    self.s5_bT_re = self.din("s5_bT_re", [n5, 128, 128, 16])
    self.s5_bT_im = self.din("s5_bT_im", [n5, 128, 128, 16])
    self.s5_cT_re = self.din("s5_cT_re", [n5, 128, 128, 16])
    self.s5_cT_im = self.din("s5_cT_im", [n5, 128, 128, 16])
    self.s5_dT = self.din("s5_dT", [n5, 128, 128])
    self.s5_glu_w = self.din("s5_glu_w", [n5, W, W])
    self.s5_glu_bT = self.din("s5_glu_bT", [n5, 128, 16])
    self.s5_w_out = self.din("s5_w_out", [n5, W, D])
    self.s5_st_re = self.din("s5_st_re", [n5, 128, 128])
    self.s5_st_im = self.din("s5_st_im", [n5, 128, 128])
    self.s5_maskf = self.din("s5_maskf", [128, 128])
    self.s5_maskb = self.din("s5_maskb", [128, 128])
    n_s5 = sum(1 for l in self.layers if l % 3 == 0)
    self.n_s5 = n_s5
    self.ns_re = self.dout("ns_re", [4, n_s5, 2, 128, 64])
    self.ns_im = self.dout("ns_im", [4, n_s5, 2, 128, 64])
    self.ud = self.dscr("ud", [16, 128, 3, 8, 128], BF16)
    self.ud_r = [RW("ud%d" % a) for a in range(16)]
    self.yd = self.dscr("yd", [3, 128, 8, W], BF16)
    self.yd_r = [[RW("yd%d_%d" % (a, b)) for b in range(16)] for a in range(3)]
    if not hasattr(self, "szd"):
        self.szd = self.dscr("szd", [16, 128, NTOK], BF16)
        self.szd_r = [RW("szd%d" % a) for a in range(16)]
    self.s5_seen = 0


def _s5_mixer(self, jj, hT, sc, hsc, first, last):
    P = self.P
    slot = self.s5_seen
    self.s5_seen += 1
    win = DTsub(self.s5_w_in, jj)
    with ExitStack() as s3:
        self.set_wbufs(s3, 6, 1024)
        ucm = [P.sb(s3, "ucm%d" % i, [128, 3, 8, 128], BF16) for i in range(2)]
        szb = [P.sb(s3, "szb%d" % i, [128, NTOK], BF16) for i in range(2)]
        for bi in range(16):
            wu, wuv = self.load_w(win, 0, 8, bi * 128, 128)
            uc = ucm[bi % 2]
            for ti in range(3):
                for t4 in range(2):
                    ps = P.next_ps()
                    for tq in range(4):
                        t = t4 * 4 + tq
                        for k in range(8):
                            P.mm(ps, ps[:, tq * 128:(tq + 1) * 128], hT[:, k, ti * 1024 + t:(ti + 1) * 1024:8],
                                 wuv[:, k, :], k == 0, k == 7, [hT, wu])
                    outv = uc[:, ti, :, :].rearrange("p g (t c) -> p t g c", c=16)[:, t4 * 4:t4 * 4 + 4, :, :]
                    inv = ps[:, :].rearrange("p (t g c) -> p t g c", t=4, c=16)
                    P.cp('act' if t4 == 0 else 'dve', outv, inv, [ps], [uc])
            P.dma('sp', self.ud.ap[bi], uc[:], [uc], [self.ud_r[bi]])
            wz, wzv = self.load_w(win, 0, 8, W + bi * 128, 128)
            sb_ = szb[bi % 2]
            for pc in range(6):
                sl = slice(pc * 512, (pc + 1) * 512)
                psz = P.next_ps()
                for k in range(8):
                    P.mm(psz, psz[:, :], wzv[:, k, :], hT[:, k, sl], k == 0, k == 7, [wz, hT])
                P.act(sb_[:, sl], psz[:, :], AF.Silu, [psz], [sb_])
            P.dma('sp', self.szd.ap[bi], sb_[:], [sb_], [self.szd_r[bi]])
    P.barrier()
    hsc.close()
    if KSTOP == 's5a':
        return
    with ExitStack() as s3:
        tabs = self.s5_tables(jj, s3)
        self.s5_core(jj, slot, tabs, s3)
    P.barrier()
    if KSTOP == 's5s':
        return
    self._s5_jj = jj
    self.out_phase(DTsub(self.s5_w_out, jj), self.s5_glu_tile, first, last, sc)


def _cmul(self, eng, o_re, o_im, a_re, a_im, b_re, b_im, t1, t2, reads, writes, neg_im=False):
    P = self.P
    P.tt(eng, t1[0], a_re, b_re, ALU.mult, reads, [t1[1]])
    P.tt(eng, t2[0], a_im, b_im, ALU.mult, reads, [t2[1]])
    P.tt(eng, o_re, t1[0], t2[0], ALU.subtract, [t1[1], t2[1]], writes)
    P.tt(eng, t1[0], a_re, b_im, ALU.mult, reads, [t1[1]])
    P.tt(eng, t2[0], a_im, b_re, ALU.mult, reads, [t2[1]])
    if neg_im:
        P.ts(eng, t1[0], t1[0], -1.0, None, ALU.mult, None, [t1[1]], [t1[1]])
        P.tt(eng, o_im, t1[0], t2[0], ALU.subtract, [t1[1], t2[1]], writes)
    else:
        P.tt(eng, o_im, t1[0], t2[0], ALU.add, [t1[1], t2[1]], writes)


def _s5_tables(self, jj, sc):
    P = self.P
    tb = {}
    mk = lambda n, shape: P.sb(sc, n, shape, F32)
    PWB = [mk("PWBre", [128, 8, 128]), mk("PWBim", [128, 8, 128])]
    PWC = [mk("PWCre", [128, 8, 128]), mk("PWCim", [128, 8, 128])]
    PWT = [mk("PWTre", [128, 8, 128]), mk("PWTim", [128, 8, 128])]
    L8 = [mk("L8re", [128, 128]), mk("L8im", [128, 128])]
    LP = [mk("LPre", [128, 6, 128]), mk("LPim", [128, 6, 128])]
    coef = [mk("coefre", [128, 128]), mk("coefim", [128, 128])]
    with ExitStack() as s2:
        lr = mk2(P, s2, "lr")
        li_ = mk2(P, s2, "li")
        ls = mk2(P, s2, "ls")
        P.dma('sp', lr[:], self.s5_lamT_re.ap[jj], [self.s5_lamT_re], [lr])
        P.dma('sp', li_[:], self.s5_lamT_im.ap[jj], [self.s5_lamT_im], [li_])
        P.dma('sp', ls[:], self.s5_lstepT.ap[jj], [self.s5_lstepT], [ls])
        stp = mk2(P, s2, "stp")
        P.act(stp[:], ls[:], AF.Exp, [ls], [stp])
        a = mk2(P, s2, "a")
        b = mk2(P, s2, "b")
        P.tt('dve', a[:], lr[:], stp[:], ALU.mult, [lr, stp], [a])
        P.tt('dve', b[:], li_[:], stp[:], ALU.mult, [li_, stp], [b])
        mag = mk2(P, s2, "mag")
        P.act(mag[:], a[:], AF.Exp, [a], [mag])
        t1 = mk2(P, s2, "t1")
        t2 = mk2(P, s2, "t2")
        ki = P.sb(s2, "ki", [128, 128], I32)

        def sin_of(dst, src, shift):
            P.ts('dve', t1[:], src[:], shift, 1.0 / TWO_PI, ALU.add, ALU.mult, [src], [t1])
            P.cp('dve', ki[:], t1[:], [t1], [ki])
            P.cp('dve', t2[:], ki[:], [ki], [t2])
            P.stt(t1[:], t2[:], -TWO_PI, src[:], ALU.mult, ALU.add, [t2, src], [t1])
            if shift != 0.0:
                P.ts('dve', t1[:], t1[:], shift, None, ALU.add, None, [t1], [t1])
            P.ts('dve', t2[:], t1[:], PI, None, ALU.is_gt, None, [t1], [t2])
            P.stt(t1[:], t2[:], -TWO_PI, t1[:], ALU.mult, ALU.add, [t2, t1], [t1])
            P.ts('dve', t2[:], t1[:], -PI, None, ALU.is_lt, None, [t1], [t2])
            P.stt(t1[:], t2[:], TWO_PI, t1[:], ALU.mult, ALU.add, [t2, t1], [t1])
            P.ts('dve', t1[:], t1[:], PI, -PI, ALU.min, ALU.max, [t1], [t1])
            P.act(dst[:], t1[:], AF.Sin, [t1], [dst])

        sn = mk2(P, s2, "sn")
        cs = mk2(P, s2, "cs")
        sin_of(sn, b, 0.0)
        sin_of(cs, b, PI / 2)
        PW = [P.sb(s2, "PWre", [128, 9, 128], F32), P.sb(s2, "PWim", [128, 9, 128], F32)]
        NP = [P.sb(s2, "NPre", [128, 8, 128], F32), P.sb(s2, "NPim", [128, 8, 128], F32)]
        P.memset('dve', PW[0][:, 0, :], 1.0, [PW[0]])
        P.memset('dve', PW[1][:, 0, :], 0.0, [PW[1]])
        P.memset('dve', NP[0][:, 0, :], 1.0, [NP[0]])
        P.memset('dve', NP[1][:, 0, :], 0.0, [NP[1]])
        P.tt('dve', PW[0][:, 1, :], mag[:], cs[:], ALU.mult, [mag, cs], [PW[0]])
        P.tt('dve', PW[1][:, 1, :], mag[:], sn[:], ALU.mult, [mag, sn], [PW[1]])
        for e in range(2, 9):
            self.cmul('dve', PW[0][:, e, :], PW[1][:, e, :], PW[0][:, e - 1, :], PW[1][:, e - 1, :],
                      PW[0][:, 1, :], PW[1][:, 1, :], (t1[:], t1), (t2[:], t2), [PW[0], PW[1]], [PW[0], PW[1]])
        den = mk2(P, s2, "den")
        P.tt('dve', t1[:], PW[0][:, 1, :], PW[0][:, 1, :], ALU.mult, [PW[0]], [t1])
        P.tt('dve', t2[:], PW[1][:, 1, :], PW[1][:, 1, :], ALU.mult, [PW[1]], [t2])
        P.tt('dve', den[:], t1[:], t2[:], ALU.add, [t1, t2], [den])
        P.recip(den[:], den[:], [den], [den])
        P.tt('dve', NP[0][:, 1, :], PW[0][:, 1, :], den[:], ALU.mult, [PW[0], den], [NP[0]])
        P.stt(NP[1][:, 1, :], PW[1][:, 1, :], -1.0, den[:], ALU.mult, ALU.mult, [PW[1], den], [NP[1]])
        for e in range(2, 8):
            self.cmul('dve', NP[0][:, e, :], NP[1][:, e, :], NP[0][:, e - 1, :], NP[1][:, e - 1, :],
                      NP[0][:, 1, :], NP[1][:, 1, :], (t1[:], t1), (t2[:], t2), [NP[0], NP[1]], [NP[0], NP[1]])
        nr = mk2(P, s2, "nr")
        P.ts('dve', nr[:], PW[0][:, 1, :], -1.0, None, ALU.add, None, [PW[0]], [nr])
        nre = mk2(P, s2, "nre")
        nim = mk2(P, s2, "nim")
        P.tt('dve', t1[:], nr[:], lr[:], ALU.mult, [nr, lr], [t1])
        P.tt('dve', t2[:], PW[1][:, 1, :], li_[:], ALU.mult, [PW[1], li_], [t2])
        P.tt('dve', nre[:], t1[:], t2[:], ALU.add, [t1, t2], [nre])
        P.tt('dve', t1[:], PW[1][:, 1, :], lr[:], ALU.mult, [PW[1], lr], [t1])
        P.tt('dve', t2[:], nr[:], li_[:], ALU.mult, [nr, li_], [t2])
        P.tt('dve', nim[:], t1[:], t2[:], ALU.subtract, [t1, t2], [nim])
        P.tt('dve', t1[:], lr[:], lr[:], ALU.mult, [lr], [t1])
        P.tt('dve', t2[:], li_[:], li_[:], ALU.mult, [li_], [t2])
        P.tt('dve', den[:], t1[:], t2[:], ALU.add, [t1, t2], [den])
        P.recip(den[:], den[:], [den], [den])
        P.tt('dve', coef[0][:], nre[:], den[:], ALU.mult, [nre, den], [coef[0]])
        P.tt('dve', coef[1][:], nim[:], den[:], ALU.mult, [nim, den], [coef[1]])
        for c in range(2):
            for j in range(8):
                P.cp('pool', PWB[c][0:64, j, :], PW[c][0:64, 7 - j, :], [PW[c]], [PWB[c]])
                P.cp('pool', PWB[c][64:128, j, :], PW[c][64:128, j, :], [PW[c]], [PWB[c]])
                P.cp('pool', PWC[c][0:64, j, :], PW[c][0:64, j + 1, :], [PW[c]], [PWC[c]])
                P.cp('pool', PWC[c][64:128, j, :], PW[c][64:128, 8 - j, :], [PW[c]], [PWC[c]])
                P.cp('pool', PWT[c][0:64, j, :], NP[c][0:64, 7 - j, :], [NP[c]], [PWT[c]])
                P.cp('pool', PWT[c][64:128, j, :], NP[c][64:128, j, :], [NP[c]], [PWT[c]])
            P.cp('pool', L8[c][:], PW[c][:, 8, :], [PW[c]], [L8[c]])
            P.cp('dve', LP[c][:, 0, :], PW[c][:, 8, :], [PW[c]], [LP[c]])
        for k in range(1, 6):
            self.cmul('dve', LP[0][:, k, :], LP[1][:, k, :], LP[0][:, k - 1, :], LP[1][:, k - 1, :],
                      LP[0][:, k - 1, :], LP[1][:, k - 1, :], (t1[:], t1), (t2[:], t2), [LP[0], LP[1]], [LP[0], LP[1]])
    P.barrier()
    for nm, tt_ in (("PWBre", PWB[0]), ("PWBim", PWB[1]), ("PWCre", PWC[0]), ("PWCim", PWC[1]), ("PWTre", PWT[0]),
                    ("PWTim", PWT[1])):
        self.dbg("%s_%d" % (nm, jj), tt_[:], [128, 8, 128])
    for nm, tt_ in (("L8re", L8[0]), ("L8im", L8[1]), ("coefre", coef[0]), ("coefim", coef[1])):
        self.dbg("%s_%d" % (nm, jj), tt_[:], [128, 128])
    tb.update(PWB=PWB, PWC=PWC, PWT=PWT, L8=L8, coef=coef, LP=LP)
    return tb


def mk2(P, sc, name):
    return P.sb(sc, name, [128, 128], F32)


Builder.declare_s5 = _declare_s5
Builder.s5_mixer = _s5_mixer
Builder.cmul = _cmul
Builder.s5_tables = _s5_tables


def _s5_core(self, jj, slot, tb, sc):
    P = self.P
    PWB, PWC, PWT, L8, coef = tb["PWB"], tb["PWC"], tb["PWT"], tb["L8"], tb["coef"]
    G = 32
    mk = lambda n, shape, dt=F32: P.sb(sc, n, shape, dt)
    maskf = mk("maskf", [128, 128])
    maskb = mk("maskb", [128, 128])
    dT = mk("dT", [128, 128])
    P.dma('sp', maskf[:], self.s5_maskf.ap[:, :], [self.s5_maskf], [maskf])
    P.dma('sp', maskb[:], self.s5_maskb.ap[:, :], [self.s5_maskb], [maskb])
    P.dma('sp', dT[:], self.s5_dT.ap[jj], [self.s5_dT], [dT])
    stin = [mk("stin_re", [128, 128]), mk("stin_im", [128, 128])]
    P.dma('sp', stin[0][:], self.s5_st_re.ap[jj], [self.s5_st_re], [stin[0]])
    P.dma('sp', stin[1][:], self.s5_st_im.ap[jj], [self.s5_st_im], [stin[1]])
    fin = mk("fin", [128, 2, 4, 128])
    Bt = [mk("Bre", [128, G, 16]), mk("Bim", [128, G, 16])]
    Ct = [mk("Cre", [128, G, 16]), mk("Cim", [128, G, 16])]
    BB = [mk("BBre", [128, G, 16]), mk("BBim", [128, G, 16])]
    U8 = mk("U8", [128, G, NCOL], BF16)
    P.memset('pool', U8[:], 0.0, [U8])
    EX = mk("EX", [128, 2, G, NCOL], BF16)
    STREAMS = [('dve', slice(0, 16)), ('dve', slice(16, 32))]
    NS = len(STREAMS)
    g2s = [next(i for i, (_, sl_) in enumerate(STREAMS) if sl_.start <= g < sl_.stop) for g in range(G)]
    EXg = [TV(EX, "EXg%d" % i) for i in range(NS)]
    P.memset('pool', EX[:], 0.0, EXg)
    W2 = [mk("W2re", [128, G, 128], BF16), mk("W2imn", [128, G, 128], BF16)]
    Toep = mk("Toep", [128, G, 128], BF16)
    ucm = mk("s_ucm", [128, 3, 8, 128], BF16)
    BP = [mk("BPre", [128, 8, 128], BF16), mk("BPim", [128, 8, 128], BF16)]
    CPT = [mk("CPTre", [128, 8, 128], BF16), mk("CPTimn", [128, 8, 128], BF16)]
    t1 = mk("s_t1", [128, 8, 128])
    t2 = mk("s_t2", [128, 8, 128])
    W1 = mk("W1", [128, 8, 2, 128], BF16)
    tz1 = mk("tz1", [128, 4, 128])
    tz2 = mk("tz2", [128, 4, 128])
    Ybf = [mk("Ybf%d" % i, [128, 384], BF16) for i in range(2)]
    ycm = ucm
    stM = mk("stM", [128, 2, G, NSEG])
    cA = mk("cA", [128, 2, G, NSEG])
    cB = mk("cB", [128, 2, G, NSEG])
    CR = mk("CR", [128, 2, G, 8])
    TS = mk("TS", [128, 2, 32, G])
    L1 = mk("L1", [128, 2, G])
    L2 = mk("L2", [128, 2, G])
    L1c = mk("L1c", [128, 2, G])
    L2c = mk("L2c", [128, 2, G])
    ch1, ch2 = t1, t2
    if os.environ.get("KDBG"):
        print("S5 core sbuf remaining", self.nc.sbuf_bytes_remaining)
    stMg = [TV(stM, "stMg%d" % i) for i in range(NS)]
    cAg = [TV(cA, "cAg%d" % i) for i in range(NS)]
    cBg = [TV(cB, "cBg%d" % i) for i in range(NS)]
    t1k = [TV(t1, "t1k0"), TV(t1, "t1k1")]
    t2k = [TV(t2, "t2k0"), TV(t2, "t2k1")]
    bsrc = [DTsub(self.s5_bT_re, jj), DTsub(self.s5_bT_im, jj)]
    csrc = [DTsub(self.s5_cT_re, jj), DTsub(self.s5_cT_im, jj)]
    halves = [('dve', slice(0, 64)), (os.environ.get('KHALF', 'pool'), slice(64, 128))]
    for bt in range(4):
        g0 = bt * G
        for c in range(2):
            P.dma('sp', Bt[c][:], bsrc[c].ap[:, g0:g0 + G, :], [bsrc[c]], [Bt[c]])
            P.dma('sp', Ct[c][:], csrc[c].ap[:, g0:g0 + G, :], [csrc[c]], [Ct[c]])
        cf = [coef[c][:, g0:g0 + G].unsqueeze(2).to_broadcast([128, G, 16]) for c in range(2)]
        tA = (t1[:, 0:4, :].rearrange("p a (b c) -> p (a b) c", c=16), t1)
        tB = (t2[:, 0:4, :].rearrange("p a (b c) -> p (a b) c", c=16), t2)
        self.cmul('pool', BB[0][:], BB[1][:], cf[0], cf[1], Bt[0][:], Bt[1][:], tA, tB, [coef[0], coef[1], Bt[0], Bt[1]],
                  [BB[0], BB[1]])
        P.cp('dve', L1[:, 0, :], L8[0][:, g0:g0 + G], [L8[0]], [L1])
        P.cp('dve', L1[:, 1, :], L8[0][:, g0:g0 + G], [L8[0]], [L1])
        P.ts('dve', L2[:, 0, :], L8[1][:, g0:g0 + G], -1.0, None, ALU.mult, None, [L8[1]], [L2])
        P.cp('dve', L2[:, 1, :], L8[1][:, g0:g0 + G], [L8[1]], [L2])
        for bl in range(4):
            bi = bt * 4 + bl
            gb = g0 + bl * 8
            gl = bl * 8
            eng = 'dve' if bl % 2 == 0 else 'pool'
            P.dma('sp', ucm[:], self.ud.ap[bi], [self.ud_r[bi]], [ucm])

            def bc_gc(x):
                return x.unsqueeze(2).to_broadcast([128, 8, 8, 16])

            def bc_gj(x):
                return x.rearrange("p j g -> p g j").unsqueeze(3).to_broadcast([128, 8, 8, 16])

            v4 = lambda x: x.rearrange("p g (j c) -> p g j c", c=16)
            tt1 = (v4(t1[:]), t1)
            tt2 = (v4(t2[:]), t2)
            pwb = [bc_gj(PWB[c][:, :, gb:gb + 8]) for c in range(2)]
            pwc = [bc_gj(PWC[c][:, :, gb:gb + 8]) for c in range(2)]
            pwt = [bc_gj(PWT[c][:, :, gb:gb + 8]) for c in range(2)]
            bbv = [bc_gc(BB[c][:, gl:gl + 8, :]) for c in range(2)]
            ccv = [bc_gc(Ct[c][:, gl:gl + 8, :]) for c in range(2)]
            rd = [BB[0], BB[1], Ct[0], Ct[1], PWB[0], PWB[1], PWC[0], PWC[1], PWT[0], PWT[1]]
            v4b = lambda x: x.rearrange("p a b -> p (a b)").rearrange("p (g j c) -> p g j c", g=8, j=8)
            tt3 = (v4b(TS[:, 0]), TS)
            tt4 = (v4b(TS[:, 1]), TS)
            self.cmul('dve', v4(BP[0][:]), v4(BP[1][:]), bbv[0], bbv[1], pwb[0], pwb[1], tt1, tt2, rd, [BP[0], BP[1]])
            self.cmul('pool', v4(W2[0][:, gl:gl + 8, :]), v4(W2[1][:, gl:gl + 8, :]), ccv[0], ccv[1], pwc[0], pwc[1],
                      tt3, tt4, rd, [W2[0], W2[1]], neg_im=True)
            self.cmul('dve', v4(CPT[0][:]), v4(CPT[1][:]), ccv[0], ccv[1], pwt[0], pwt[1], tt1, tt2, rd,
                      [CPT[0], CPT[1]], neg_im=True)
            for g in range(8):
                if g % 4 == 0:
                    ps = P.next_ps()
                    pv = ps[:, :].bitcast(BF16).rearrange("p (k n) -> p k n", k=8)
                P.tr(ps, pv[:, (g % 4) * 2, :], BP[0][:, g, :], [BP[0]])
                P.tr(ps, pv[:, (g % 4) * 2 + 1, :], BP[1][:, g, :], [BP[1]])
                if g % 4 == 3:
                    P.cp('act', W1[:, g - 3:g + 1, :, :].rearrange("p g r n -> p (g r) n"), pv, [ps], [W1])
            for q in range(2):
                psf = P.next_ps()
                psb = P.next_ps()
                for gi in range(4):
                    g = q * 4 + gi
                    for (pp, sl) in ((psf, slice(0, 64)), (psb, slice(64, 128))):
                        P.mm(pp, pp[:, gi * 128:(gi + 1) * 128], BP[0][sl, g, :], CPT[0][sl, g, :], True, False,
                             [BP[0], CPT[0]])
                        P.mm(pp, pp[:, gi * 128:(gi + 1) * 128], BP[1][sl, g, :], CPT[1][sl, g, :], False, True,
                             [BP[1], CPT[1]])
                mfb = maskf[:].unsqueeze(1).to_broadcast([128, 4, 128])
                mbb = maskb[:].unsqueeze(1).to_broadcast([128, 4, 128])
                idb = self.identf[:].unsqueeze(1).to_broadcast([128, 4, 128])
                dd = dT[:, gb + q * 4:gb + q * 4 + 4].unsqueeze(2).to_broadcast([128, 4, 128])
                p4 = lambda x: x[:, :].rearrange("p (a b) -> p a b", a=4)
                P.tt('dve', tz1[:], p4(psf), mfb, ALU.mult, [psf, maskf], [tz1])
                P.tt('dve', tz2[:], p4(psb), mbb, ALU.mult, [psb, maskb], [tz2])
                P.tt('pool', tz1[:], tz1[:], tz2[:], ALU.add, [tz1, tz2], [tz1])
                P.tt('pool', tz2[:], idb, dd, ALU.mult, [self.identf, dT], [tz2])
                P.tt('pool', Toep[:, gl + q * 4:gl + q * 4 + 4, :], tz1[:], tz2[:], ALU.add, [tz1, tz2], [Toep])
            for g in range(8):
                ps = P.next_ps()
                pv = ps[:, :].bitcast(BF16).rearrange("p (k n) -> p k n", k=8)
                for ti in range(3):
                    P.tr(ps, pv[:, ti, :], ucm[:, ti, g, :], [ucm])
                P.cp('act', U8[:, gl + g, :].rearrange("p (q l) -> p q l", l=34)[:, :, 0:32],
                     pv[:, 0:3, :].rearrange("p a (q l) -> p (a q) l", l=32), [ps], [U8])
                for r in range(2):
                    px = P.next_ps()
                    P.mm(px, px[:, 0:406], W1[:, g, r, :], U8[:, gl + g, 0:406], True, True, [W1, U8])
                    P.cp('act', EX[0:64, r, gl + g, 2:408], px[0:64, 0:406], [px], [EXg[g2s[gl + g]]])
                    P.cp('act', EX[64:128, r, gl + g, 0:406], px[64:128, 0:406], [px], [EXg[g2s[gl + g]]])
        lp = tb["LP"]
        hv = [(slice(0, 64), True), (slice(64, 128), False)]
        GH = [slice(0, 16), slice(16, 32)]
        for sl, fwd in hv:
            pos1 = 0 if fwd else 31
            for r in range(2):
                P.cp('dve', TS[sl, r, pos1, :], lp[r][sl, 0, g0:g0 + G], [lp[r]], [TS])
            m = 1
            for k in range(5):
                src = slice(0, m) if fwd else slice(32 - m, 32)
                dst = slice(m, 2 * m) if fwd else slice(32 - 2 * m, 32 - m)
                lk = [lp[r][sl, k, g0:g0 + G].unsqueeze(1).to_broadcast([64, m, G]) for r in range(2)]
                tv = lambda t_: t_[sl, 0:4, :].rearrange("p a (b c) -> p (a b) c", c=32)[:, 0:m, :]
                self.cmul('dve', TS[sl, 0, dst, :], TS[sl, 1, dst, :], TS[sl, 0, src, :], TS[sl, 1, src, :],
                          lk[0], lk[1], (tv(t1), t1), (tv(t2), t2), [TS, lp[0], lp[1]], [TS])
                m *= 2
        for r in range(2):
            P.cp('dve', L1c[:, r, :], lp[0][:, 5, g0:g0 + G], [lp[0]], [L1c])
        P.ts('dve', L2c[:, 0, :], lp[1][:, 5, g0:g0 + G], -1.0, None, ALU.mult, None, [lp[1]], [L2c])
        P.cp('dve', L2c[:, 1, :], lp[1][:, 5, g0:g0 + G], [lp[1]], [L2c])
        P.memset('dve', stM[:], 0.0, stMg)
        for sl, fwd in hv:
            q0 = 0 if fwd else 7
            for r in range(2):
                P.cp('dve', stM[sl, r, :, q0], stin[r][sl, g0:g0 + G], [stin[r]], stMg)
                P.cp('dve', EX[sl, r, :, 1 if fwd else 34 * 7 + 32], stin[r][sl, g0:g0 + G], [stin[r]], EXg)
        for i in range(32 if 'L' not in KSKIP else 0):
            for gh, (eng, gsl) in enumerate(STREAMS):
                ng = gsl.stop - gsl.start
                l1 = L1[:, :, gsl].unsqueeze(3).to_broadcast([128, 2, ng, NSEG])
                P.tt(eng, cA[:, :, gsl, :], stM[:, :, gsl, :], l1, ALU.mult, [stMg[gh], L1], [cAg[gh]])
            for gh, (eng, gsl) in enumerate(STREAMS):
                ng = gsl.stop - gsl.start
                l20 = L2[:, 0, gsl].unsqueeze(2).to_broadcast([128, ng, NSEG])
                P.tt(eng, cB[:, 0, gsl, :], stM[:, 1, gsl, :], l20, ALU.mult, [stMg[gh], L2], [cBg[gh]])
            for gh, (eng, gsl) in enumerate(STREAMS):
                ng = gsl.stop - gsl.start
                l21 = L2[:, 1, gsl].unsqueeze(2).to_broadcast([128, ng, NSEG])
                P.tt(eng, cB[:, 1, gsl, :], stM[:, 0, gsl, :], l21, ALU.mult, [stMg[gh], L2], [cBg[gh]])
            for gh, (eng, gsl) in enumerate(STREAMS):
                P.tt(eng, cA[:, :, gsl, :], cA[:, :, gsl, :], cB[:, :, gsl, :], ALU.add, [cAg[gh], cBg[gh]],
                     [cAg[gh]])
            for sl, fwd in hv:
                c0 = 2 + i if fwd else 31 - i
                for gh, (eng, gsl) in enumerate(STREAMS):
                    exv = EX[sl, :, gsl, c0:NCOL:34]
                    P.tt(eng, stM[sl, :, gsl, :], cA[sl, :, gsl, :], exv, ALU.add, [cAg[gh], EXg[gh]], [stMg[gh]])
                    P.cp('act', exv, stM[sl, :, gsl, :], [stMg[gh]], [EXg[gh]])
        P.cp('dve', fin[:, :, :, g0:g0 + G], stM[:, :, :, 8:12].rearrange("p r g s -> p r s g"), stMg, [fin])
        P.memset('dve', CR[:], 0.0, [CR])
        for sl, fwd in hv:
            order = list(range(1, 8)) if fwd else list(range(6, -1, -1))
            for n_, k in enumerate(order):
                prv = k - 1 if fwd else k + 1
                if n_ == 0:
                    P.cp('dve', CR[sl, :, :, k], stM[sl, :, :, prv], stMg, [CR])
                    continue
                a_ = cA[sl, :, :, 0]
                P.tt('dve', a_, CR[sl, :, :, prv], L1c[sl], ALU.mult, [CR, L1c], cAg)
                P.tt('dve', cB[sl, 0, :, 0], CR[sl, 1, :, prv], L2c[sl, 0, :], ALU.mult, [CR, L2c], cBg)
                P.tt('dve', cB[sl, 1, :, 0], CR[sl, 0, :, prv], L2c[sl, 1, :], ALU.mult, [CR, L2c], cBg)
                P.tt('dve', a_, a_, cB[sl, :, :, 0], ALU.add, cAg + cBg, cAg)
                P.tt('dve', CR[sl, :, :, k], a_, stM[sl, :, :, prv], ALU.add, cAg + stMg, [CR])
            if fwd:
                P.cp('act', EX[sl, :, :, 35:35 + 34 * 7:34], CR[sl, :, :, 1:8], [CR], EXg)
            else:
                P.cp('act', EX[sl, :, :, 32:32 + 34 * 7:34], CR[sl, :, :, 0:7], [CR], EXg)
        for n_, gs in enumerate(range(0, G if 'F' not in KSKIP else 0, 2)):
            ks = n_ % 2
            tvv = lambda t_: t_[:, 4 * ks:4 * ks + 4, :].rearrange("p a b -> p (a b)").rearrange(
                "p (g q i) -> p g q i", g=2, q=8)
            ta, tbb = tvv(t1), tvv(t2)
            tar, tbr = t1k[ks], t2k[ks]
            exg = EXg[g2s[gs]]
            cr = [CR[:, r, gs:gs + 2, :].unsqueeze(3).to_broadcast([128, 2, 8, 32]) for r in range(2)]
            ts_ = [TS[:, r, :, gs:gs + 2].rearrange("p i g -> p g i").unsqueeze(2).to_broadcast([128, 2, 8, 32])
                   for r in range(2)]
            for r_out in range(2):
                if r_out == 0:
                    P.tt('dve', ta, cr[0], ts_[0], ALU.mult, [CR, TS], [tar])
                    P.tt('dve', tbb, cr[1], ts_[1], ALU.mult, [CR, TS], [tbr])
                    P.tt('dve', ta, ta, tbb, ALU.subtract, [tar, tbr], [tar])
                else:
                    P.tt('dve', ta, cr[0], ts_[1], ALU.mult, [CR, TS], [tar])
                    P.tt('dve', tbb, cr[1], ts_[0], ALU.mult, [CR, TS], [tbr])
                    P.tt('dve', ta, ta, tbb, ALU.add, [tar, tbr], [tar])
                for sl, fwd in hv:
                    off = 2 if fwd else 0
                    exs = EX[sl, r_out, gs:gs + 2, 0:272].rearrange("p g (q l) -> p g q l", l=34)[:, :, :, off:off + 32]
                    lastw = [exg] if gs + 2 < G else [exg, t1, t2]
                    P.tt('dve' if (n_ % 4 == 3 and gs + 2 < G) else 'pool', exs, exs, ta[sl], ALU.add, [tar, exg], lastw)
        for bl in range(4):
            bi = bt * 4 + bl
            gl = bl * 8
            for g in range(8):
                py = P.next_ps()
                P.mm(py, py[:, 0:406], Toep[:, gl + g, :], U8[:, gl + g, 0:406], True, False, [Toep, U8])
                P.mm(py, py[:, 0:406], W2[0][:, gl + g, :], EX[:, 0, gl + g, 1:407], False, False,
                     [W2[0]] + EXg)
                P.mm(py, py[:, 0:406], W2[1][:, gl + g, :], EX[:, 1, gl + g, 1:407], False, True,
                     [W2[1]] + EXg)
                yb = Ybf[g % 2]
                P.cp('act', yb[:, :].rearrange("p (q l) -> p q l", l=32),
                     py[:, 0:408].rearrange("p (q l) -> p q l", l=34)[:, :, 0:32], [py], [yb])
                ps = P.next_ps()
                pv = ps[:, :].bitcast(BF16).rearrange("p (k n) -> p k n", k=8)
                for ti in range(3):
                    P.tr(ps, pv[:, ti, :], yb[:, ti * 128:(ti + 1) * 128], [yb])
                P.act(ycm[:, :, :, g * 16:(g + 1) * 16], pv[:, 0:3, :].rearrange("p a (t c) -> p a t c", c=16),
                      AF.Gelu_apprx_tanh, [ps], [ycm])
            for ti in range(3):
                P.dma('sp', self.yd.ap[ti][:, :, bi * 128:(bi + 1) * 128], ycm[:, ti, :, :], [ycm], [self.yd_r[ti][bi]])
    fo = mk("fo", [128, 128])
    for r, dst in ((0, self.ns_re), (1, self.ns_im)):
        for s in range(4):
            ps = P.next_ps()
            P.tr(ps, ps[:, 0:128], fin[:, r, s, :], [fin], ident=self.identf)
            P.cp('act', fo[:], ps[:, 0:128], [ps], [fo])
            P.dma('sp', dst.ap[s, slot].rearrange("d g p -> g d p"), fo[:].rearrange("g (d p) -> g d p", d=2),
                  [fo], [dst])


def _s5_glu_tile(self, ti, sc):
    P = self.P
    jj = self._s5_jj
    if getattr(self, "_g_scope", None) is not sc:
        self._g_scope = sc
        self._g_yT = P.sb(sc, "g_yT", [128, 16, 1024], BF16)
        self._g_PT = P.sb(sc, "g_PT", [128, 16, 1024], BF16)
        self._g_yc = [P.sb(sc, "g_yc%d" % i, [128, W], BF16) for i in range(2)]
        self._g_sz = [P.sb(sc, "g_sz%d" % i, [128, 1024], BF16) for i in range(2)]
        self._g_sig = [P.sb(sc, "g_sig%d" % i, [128, 512], BF16) for i in range(2)]
        self._g_tmp = [P.sb(sc, "g_tmp%d" % i, [128, 512], BF16) for i in range(2)]
        self._g_gb = P.sb(sc, "g_gb", [128, 16], F32)
        self._g_w = [P.sb(sc, "g_w%d" % i, [128, 16, 128], BF16) for i in range(3)]
        P.dma('sp', self._g_gb[:], self.s5_glu_bT.ap[jj], [self.s5_glu_bT], [self._g_gb])
    yT, PT = self._g_yT, self._g_PT
    for t in range(8):
        yc = self._g_yc[t % 2]
        P.dma('sp', yc[:], self.yd.ap[ti][:, t, :], self.yd_r[ti], [yc])
        for hf in range(2):
            ps = P.next_ps()
            pv = ps[:, :].bitcast(BF16).rearrange("p (k n) -> p k n", k=8)
            for b in range(8):
                P.tr(ps, pv[:, b, :], yc[:, (hf * 8 + b) * 128:(hf * 8 + b + 1) * 128], [yc])
            P.cp('act' if hf == 0 else 'dve', yT[:, hf * 8:hf * 8 + 8, t:1024:8], pv, [ps], [yT])
    gsrc = DTsub(self.s5_glu_w, jj)
    for d in range(16):
        gw = self._g_w[d % 3]
        for q in range(2):
            P.dma('pool', gw[:, q * 8:(q + 1) * 8, :],
                  gsrc.ap[q * 1024:(q + 1) * 1024, d * 128:(d + 1) * 128].rearrange("(k p) n -> p k n", p=128),
                  [gsrc], [gw])
        sz = self._g_sz[d % 2]
        P.dma('sp', sz[:], self.szd.ap[d][:, ti * 1024:(ti + 1) * 1024], [self.szd_r[d]], [sz])
        for pc in range(2):
            sl = slice(pc * 512, (pc + 1) * 512)
            ps = P.next_ps()
            for k in range(16):
                P.mm(ps, ps[:, :], gw[:, k, :], yT[:, k, sl], k == 0, k == 15, [gw, yT])
            sg = self._g_sig[pc]
            P.act(sg[:], ps[:, :], AF.Sigmoid, [ps, self._g_gb], [sg], bias=self._g_gb[:, d:d + 1])
            tm = self._g_tmp[pc]
            P.tt('dve', tm[:], yT[:, d, sl], sg[:], ALU.mult, [yT, sg], [tm])
            P.tt('dve', PT[:, d, sl], tm[:], sz[:, sl], ALU.mult, [tm, sz], [PT])
    return PT


Builder.s5_core = _s5_core
Builder.s5_glu_tile = _s5_glu_tile


def s5_shared_inputs(inp):
    f = lambda a: np.ascontiguousarray(np.asarray(a, dtype=np.float32))
    lam_re = np.asarray(inp["s5_lam_re"], np.float32)
    n = lam_re.shape[0]
    tr = lambda a: np.asarray(a, np.float32).transpose(0, 1, 3, 2).reshape(n, 128, 128)
    ls = np.asarray(inp["s5_log_step"], np.float32)
    lstep = np.broadcast_to(ls[:, :, None, :], (n, 2, 64, 128)).reshape(n, 128, 128)
    b_re = np.asarray(inp["s5_b_re"], np.float32)
    bt = lambda a: np.asarray(a, np.float32).transpose(0, 1, 3, 2, 4).reshape(n, 128, 128, 16)
    ct = lambda a: np.asarray(a, np.float32).transpose(0, 1, 4, 2, 3).reshape(n, 128, 128, 16)
    d = np.asarray(inp["s5_d"], np.float32).reshape(n, 128, 16)
    dT = np.broadcast_to(d.transpose(0, 2, 1)[:, None, :, :], (n, 8, 16, 128)).reshape(n, 128, 128)
    jc = np.arange(128) // 16
    maskf = (jc[None, :] >= jc[:, None]).astype(np.float32)
    maskb = (jc[None, :] <= jc[:, None]).astype(np.float32)
    return {
        "s5_w_in": f(inp["s5_w_in"]), "s5_lamT_re": f(tr(inp["s5_lam_re"])), "s5_lamT_im": f(tr(inp["s5_lam_im"])),
        "s5_lstepT": f(lstep), "s5_bT_re": f(bt(inp["s5_b_re"])), "s5_bT_im": f(bt(inp["s5_b_im"])),
        "s5_cT_re": f(ct(inp["s5_c_re"])), "s5_cT_im": f(ct(inp["s5_c_im"])), "s5_dT": f(dT),
        "s5_glu_w": f(inp["s5_glu_w"]),
        "s5_glu_bT": f(np.asarray(inp["s5_glu_b"], np.float32).reshape(n, 16, 128).transpose(0, 2, 1)),
        "s5_w_out": f(inp["s5_w_out"]), "s5_maskf": f(maskf), "s5_maskb": f(maskb),
    }


def s5_core_inputs(inp, c):
    f = lambda a: np.ascontiguousarray(np.asarray(a, dtype=np.float32))
    b = c % 2
    sr = np.asarray(inp["state_s5_re"], np.float32)[b]
    si = np.asarray(inp["state_s5_im"], np.float32)[b]
    n = sr.shape[0]
    tr = lambda a: a.transpose(0, 1, 3, 2).reshape(n, 128, 128)
    return {"s5_st_re": f(tr(sr)), "s5_st_im": f(tr(si))}
```
